# Optimizing a Trainium2 kernel written in Bass

```python
import functools
import jax
import jax.numpy as jnp
from jax import lax
import numpy as np


D_MODEL = 2048
BATCH = 4
SEQ = 4096
DEPTH = 2

N_MEM = 256
NORM_EPS = 1e-6

DN_HEADS = 8
DN_DK = 128
DN_DV = 128
DN_CONV = 4
DN_CHUNK = 64
GLA_HEADS = 4
GLA_DK = 128
GLA_DV = 256
GLA_RANK = 16
GLA_TAU = 16.0
GLA_CHUNK = 64
S5_GROUP = 16
S5_GROUPS = 64
S5_WIDTH = S5_GROUP * S5_GROUPS
S5_STATE = 64
XA_HEADS = 4
XA_DH = 128
FFN_HIDDEN = 5632
FFN_CONV = 3

IN_SIZES = (
    DN_HEADS * DN_DK,
    DN_HEADS * DN_DK,
    DN_HEADS * DN_DV,
    DN_HEADS * DN_DV,
    DN_HEADS,
    DN_HEADS,
    GLA_HEADS * GLA_DK,
    GLA_HEADS * GLA_DK,
    GLA_HEADS * GLA_DV,
    GLA_RANK,
    GLA_HEADS * GLA_DV,
    S5_WIDTH,
    3 * D_MODEL,
)
IN_WIDTH = sum(IN_SIZES)

kernel_name = 'hybrid_deltanet_gla_s5_xattn_convffn'


def rms_norm(x, g):
    xf = x.astype(jnp.float32)
    y = xf * lax.rsqrt(jnp.mean(xf * xf, axis=-1, keepdims=True) + NORM_EPS)
    return (y * g.astype(jnp.float32)).astype(x.dtype)


def l2_normalize(x):
    xf = x.astype(jnp.float32)
    return xf * lax.rsqrt(jnp.sum(xf * xf, axis=-1, keepdims=True) + NORM_EPS)


def causal_depthwise_conv(x, w):
    k_w = w.shape[-1]
    s = x.shape[1]
    xp = jnp.pad(x, ((0, 0), (k_w - 1, 0), (0, 0)))
    y = xp[:, 0:s] * w[:, 0]
    for j in range(1, k_w):
        y = y + xp[:, j:j + s] * w[:, j]
    return y


def split_columns(z, sizes):
    idx = []
    acc = 0
    for n in sizes[:-1]:
        acc += n
        idx.append(acc)
    return jnp.split(z, idx, axis=-1)


def to_chunks(t, c):
    b, s, h = t.shape[:3]
    t = t.reshape((b, s // c, c, h) + t.shape[3:])
    t = jnp.moveaxis(t, 3, 2)
    return jnp.moveaxis(t, 1, 0)


def from_chunks(o):
    n, b, h, c, d = o.shape
    return o.transpose(1, 0, 3, 2, 4).reshape(b, n * c, h, d)


def gated_delta_rule(q, k, v, g, beta):
    f32 = jnp.float32
    c = DN_CHUNK
    dk = q.shape[-1]
    q = to_chunks(l2_normalize(q) * (dk ** -0.5), c)
    k = to_chunks(l2_normalize(k), c)
    v = to_chunks(v.astype(f32), c)
    beta = to_chunks(beta.astype(f32), c)
    gc = jnp.cumsum(to_chunks(g.astype(f32), c), axis=-1)
    causal = jnp.tril(jnp.ones((c, c), dtype=bool))
    strict = jnp.tril(jnp.ones((c, c), dtype=bool), k=-1)
    decay = jnp.exp(jnp.where(causal, gc[..., :, None] - gc[..., None, :], -jnp.inf))
    kb = k * beta[..., None]
    lower = jnp.where(strict, jnp.einsum('nbhid,nbhjd->nbhij', kb, k) * decay, 0.0)
    t_mat = lower + jnp.eye(c, dtype=f32)
    solve = functools.partial(lax.linalg.triangular_solve, left_side=True, lower=True, unit_diagonal=True)
    u = solve(t_mat, v * beta[..., None])
    w = solve(t_mat, kb * jnp.exp(gc)[..., None])
    a_qk = jnp.where(causal, jnp.einsum('nbhid,nbhjd->nbhij', q, k) * decay, 0.0)
    q_dec = q * jnp.exp(gc)[..., None]
    k_dec = k * jnp.exp(gc[..., -1:] - gc)[..., None]
    g_last = jnp.exp(gc[..., -1])

    def step(state, inp):
        a_c, qd, kd, wc, uc, gl = inp
        v_new = uc - jnp.einsum('bhcd,bhde->bhce', wc, state)
        o = jnp.einsum('bhcd,bhde->bhce', qd, state) + jnp.einsum('bhij,bhje->bhie', a_c, v_new)
        state = state * gl[..., None, None] + jnp.einsum('bhcd,bhce->bhde', kd, v_new)
        return state, o

    s0 = jnp.zeros(q.shape[1:3] + (dk, v.shape[-1]), f32)
    _, o = lax.scan(step, s0, (a_qk, q_dec, k_dec, w, u, g_last))
    return from_chunks(o)


def gla_chunked(q, k, v, log_a):
    f32 = jnp.float32
    c = GLA_CHUNK
    dk = q.shape[-1]
    q = to_chunks(q.astype(f32) * (dk ** -0.5), c)
    k = to_chunks(k.astype(f32), c)
    v = to_chunks(v.astype(f32), c)
    b = jnp.cumsum(to_chunks(log_a.astype(f32), c), axis=-2)
    q_dec = q * jnp.exp(b)
    k_dec = k * jnp.exp(b[..., -1:, :] - b)
    a_last = jnp.exp(b[..., -1, :])
    causal = jnp.tril(jnp.ones((c, c), dtype=bool))[:, :, None]

    def step(state, inp):
        qc, kc, vc, bc, qd, kd, al = inp
        dec = jnp.exp(jnp.where(causal, bc[:, :, :, None, :] - bc[:, :, None, :, :], -jnp.inf))
        a_qk = jnp.einsum('bhijd,bhjd->bhij', qc[:, :, :, None, :] * dec, kc)
        o = jnp.einsum('bhid,bhde->bhie', qd, state) + jnp.einsum('bhij,bhje->bhie', a_qk, vc)
        state = state * al[..., :, None] + jnp.einsum('bhcd,bhce->bhde', kd, vc)
        return state, o

    s0 = jnp.zeros(q.shape[1:3] + (dk, v.shape[-1]), f32)
    _, o = lax.scan(step, s0, (q, k, v, b, q_dec, k_dec, a_last))
    return from_chunks(o)


def s5_ssm(u, lam_re, lam_im, log_step, b_re, b_im, c_re, c_im, d_skip):
    f32 = jnp.float32
    bsz, s, _ = u.shape
    uf = u.astype(f32).reshape(bsz, s, S5_GROUPS, S5_GROUP)
    lr = jnp.minimum(lam_re.astype(f32), -1e-4)
    li = lam_im.astype(f32)
    dt = jnp.exp(log_step.astype(f32))[:, None]
    mag = jnp.exp(lr * dt)
    ar = mag * jnp.cos(li * dt)
    ai = mag * jnp.sin(li * dt)
    nr = ar - 1.0
    den = lr * lr + li * li
    zr = (nr * lr + ai * li) / den
    zi = (ai * lr - nr * li) / den
    br = b_re.astype(f32)
    bi = b_im.astype(f32)
    bbr = zr[..., None] * br - zi[..., None] * bi
    bbi = zr[..., None] * bi + zi[..., None] * br
    bu_r = jnp.einsum('bsgh,gph->sbgp', uf, bbr)
    bu_i = jnp.einsum('bsgh,gph->sbgp', uf, bbi)
    a_r = jnp.broadcast_to(ar, (s, 1) + ar.shape)
    a_i = jnp.broadcast_to(ai, (s, 1) + ai.shape)

    def combine(e1, e2):
        a1r, a1i, b1r, b1i = e1
        a2r, a2i, b2r, b2i = e2
        return (a2r * a1r - a2i * a1i,
                a2r * a1i + a2i * a1r,
                a2r * b1r - a2i * b1i + b2r,
                a2r * b1i + a2i * b1r + b2i)

    _, _, xr, xi = lax.associative_scan(combine, (a_r, a_i, bu_r, bu_i), axis=0)
    y = (jnp.einsum('sbgp,ghp->bsgh', xr, c_re.astype(f32))
         - jnp.einsum('sbgp,ghp->bsgh', xi, c_im.astype(f32)))
    y = y + d_skip.astype(f32).reshape(S5_GROUPS, S5_GROUP) * uf
    return y.reshape(bsz, s, S5_WIDTH).astype(u.dtype)


def hybrid_mixer(h, w_in, dn_conv_w, dn_a_log, dn_dt_bias, dn_norm_w, w_dn_out,
                 gla_w_up, gla_b_up, gla_norm_w, w_gla_out,
                 s5_lam_re, s5_lam_im, s5_log_step, s5_b_re, s5_b_im, s5_c_re, s5_c_im, s5_d,
                 w_s5_glu, w_mix_out):
    bsz, s, _ = h.shape
    z = h @ w_in
    (dq, dkk, dv, dz, db, da, gq, gk, gv, glow, gr, su, gates) = split_columns(z, IN_SIZES)

    qkv = jax.nn.silu(causal_depthwise_conv(jnp.concatenate([dq, dkk, dv], axis=-1), dn_conv_w))
    dq, dkk, dv = jnp.split(qkv, [DN_HEADS * DN_DK, 2 * DN_HEADS * DN_DK], axis=-1)
    beta = jax.nn.sigmoid(db.astype(jnp.float32))
    g = -jnp.exp(dn_a_log.astype(jnp.float32)) * jax.nn.softplus(da.astype(jnp.float32) + dn_dt_bias.astype(jnp.float32))
    o_a = gated_delta_rule(dq.reshape(bsz, s, DN_HEADS, DN_DK), dkk.reshape(bsz, s, DN_HEADS, DN_DK),
                           dv.reshape(bsz, s, DN_HEADS, DN_DV), g, beta)
    o_a = rms_norm(o_a, dn_norm_w) * jax.nn.silu(dz.reshape(bsz, s, DN_HEADS, DN_DV).astype(jnp.float32))
    y_a = o_a.reshape(bsz, s, DN_HEADS * DN_DV).astype(h.dtype) @ w_dn_out

    gate_logit = (glow @ gla_w_up + gla_b_up).astype(jnp.float32)
    log_a = jax.nn.log_sigmoid(gate_logit) / GLA_TAU
    o_b = gla_chunked(gq.reshape(bsz, s, GLA_HEADS, GLA_DK), gk.reshape(bsz, s, GLA_HEADS, GLA_DK),
                      gv.reshape(bsz, s, GLA_HEADS, GLA_DV), log_a.reshape(bsz, s, GLA_HEADS, GLA_DK))
    o_b = rms_norm(o_b, gla_norm_w) * jax.nn.silu(gr.reshape(bsz, s, GLA_HEADS, GLA_DV).astype(jnp.float32))
    y_b = o_b.reshape(bsz, s, GLA_HEADS * GLA_DV).astype(h.dtype) @ w_gla_out

    y_s = jax.nn.gelu(s5_ssm(su, s5_lam_re, s5_lam_im, s5_log_step, s5_b_re, s5_b_im, s5_c_re, s5_c_im, s5_d))
    glu_a, glu_b = jnp.split(y_s @ w_s5_glu, 2, axis=-1)
    y_c = glu_a * jax.nn.sigmoid(glu_b)

    g_a, g_b, g_c = jnp.split(jax.nn.sigmoid(gates), 3, axis=-1)
    merged = g_a * y_a + g_b * y_b + g_c * y_c
    return merged @ w_mix_out


def cross_attention(h, mem_n, w_xa_q, w_xa_kv, w_xa_out):
    bsz, s, _ = h.shape
    m = mem_n.shape[1]
    q = (h @ w_xa_q).reshape(bsz, s, XA_HEADS, XA_DH)
    kv = (mem_n @ w_xa_kv).reshape(bsz, m, 2, XA_HEADS, XA_DH)
    k = kv[:, :, 0]
    v = kv[:, :, 1]
    scores = jnp.einsum('bshd,bmhd->bhsm', q, k).astype(jnp.float32) * (XA_DH ** -0.5)
    p = jax.nn.softmax(scores, axis=-1).astype(v.dtype)
    o = jnp.einsum('bhsm,bmhd->bshd', p, v).reshape(bsz, s, XA_HEADS * XA_DH)
    return o @ w_xa_out


def conv_ffn(h, w_ffn_up, ffn_conv_w, w_ffn_down):
    gate, up = jnp.split(h @ w_ffn_up, 2, axis=-1)
    gate = causal_depthwise_conv(gate, ffn_conv_w)
    return (jax.nn.gelu(gate) * up) @ w_ffn_down


def setup_inputs(seed: int = 0) -> dict:
    key = jax.random.key(seed)
    ks = iter(jax.random.split(key, 48))
    f32 = jnp.float32
    L = DEPTH

    def dense(shape, fan_in):
        return jax.random.normal(next(ks), shape, f32) * (fan_in ** -0.5)

    def gain(shape):
        return 1.0 + 0.02 * jax.random.normal(next(ks), shape, f32)

    def log_uniform(shape, lo, hi):
        return jax.random.uniform(next(ks), shape, f32, minval=float(np.log(lo)), maxval=float(np.log(hi)))

    x = jax.random.normal(next(ks), (BATCH, SEQ, D_MODEL), f32)
    mem = jax.random.normal(next(ks), (BATCH, N_MEM, D_MODEL), f32)
    dt_init = jnp.exp(log_uniform((L, DN_HEADS), 1e-3, 1e-1))
    lam_im = jnp.broadcast_to(jnp.pi * jnp.arange(S5_STATE, dtype=f32), (L, S5_GROUPS, S5_STATE))
    return {
        'x': x,
        'mem': mem,
        'norm_mix_pre': gain((L, D_MODEL)),
        'norm_mix_post': gain((L, D_MODEL)),
        'w_in': dense((L, D_MODEL, IN_WIDTH), D_MODEL),
        'dn_conv_w': dense((L, 3 * DN_HEADS * DN_DK, DN_CONV), DN_CONV),
        'dn_a_log': jnp.log(jax.random.uniform(next(ks), (L, DN_HEADS), f32, minval=1.0, maxval=16.0)),
        'dn_dt_bias': dt_init + jnp.log(-jnp.expm1(-dt_init)),
        'dn_norm_w': gain((L, DN_DV)),
        'w_dn_out': dense((L, DN_HEADS * DN_DV, D_MODEL), DN_HEADS * DN_DV),
        'gla_w_up': dense((L, GLA_RANK, GLA_HEADS * GLA_DK), GLA_RANK),
        'gla_b_up': 0.1 * jax.random.normal(next(ks), (L, GLA_HEADS * GLA_DK), f32),
        'gla_norm_w': gain((L, GLA_DV)),
        'w_gla_out': dense((L, GLA_HEADS * GLA_DV, D_MODEL), GLA_HEADS * GLA_DV),
        's5_lam_re': -0.5 + 0.01 * jax.random.normal(next(ks), (L, S5_GROUPS, S5_STATE), f32),
        's5_lam_im': lam_im,
        's5_log_step': log_uniform((L, S5_GROUPS), 1e-3, 1e-1),
        's5_b_re': dense((L, S5_GROUPS, S5_STATE, S5_GROUP), 2 * S5_GROUP),
        's5_b_im': dense((L, S5_GROUPS, S5_STATE, S5_GROUP), 2 * S5_GROUP),
        's5_c_re': dense((L, S5_GROUPS, S5_GROUP, S5_STATE), S5_STATE),
        's5_c_im': dense((L, S5_GROUPS, S5_GROUP, S5_STATE), S5_STATE),
        's5_d': jax.random.normal(next(ks), (L, S5_WIDTH), f32),
        'w_s5_glu': dense((L, S5_WIDTH, 2 * D_MODEL), S5_WIDTH),
        'w_mix_out': dense((L, D_MODEL, D_MODEL), D_MODEL),
        'norm_xa_pre': gain((L, D_MODEL)),
        'norm_xa_post': gain((L, D_MODEL)),
        'norm_mem': gain((L, D_MODEL)),
        'w_xa_q': dense((L, D_MODEL, XA_HEADS * XA_DH), D_MODEL),
        'w_xa_kv': dense((L, D_MODEL, 2 * XA_HEADS * XA_DH), D_MODEL),
        'w_xa_out': dense((L, XA_HEADS * XA_DH, D_MODEL), XA_HEADS * XA_DH),
        'norm_ffn_pre': gain((L, D_MODEL)),
        'norm_ffn_post': gain((L, D_MODEL)),
        'w_ffn_up': dense((L, D_MODEL, 2 * FFN_HIDDEN), D_MODEL),
        'ffn_conv_w': dense((L, FFN_HIDDEN, FFN_CONV), FFN_CONV),
        'w_ffn_down': dense((L, FFN_HIDDEN, D_MODEL), FFN_HIDDEN),
    }


def reference(x, mem, norm_mix_pre, norm_mix_post, w_in, dn_conv_w, dn_a_log, dn_dt_bias, dn_norm_w,
              w_dn_out, gla_w_up, gla_b_up, gla_norm_w, w_gla_out, s5_lam_re, s5_lam_im, s5_log_step,
              s5_b_re, s5_b_im, s5_c_re, s5_c_im, s5_d, w_s5_glu, w_mix_out, norm_xa_pre, norm_xa_post,
              norm_mem, w_xa_q, w_xa_kv, w_xa_out, norm_ffn_pre, norm_ffn_post, w_ffn_up, ffn_conv_w,
              w_ffn_down):
    for l in range(DEPTH):
        h = rms_norm(x, norm_mix_pre[l])
        y = hybrid_mixer(h, w_in[l], dn_conv_w[l], dn_a_log[l], dn_dt_bias[l], dn_norm_w[l], w_dn_out[l],
                         gla_w_up[l], gla_b_up[l], gla_norm_w[l], w_gla_out[l],
                         s5_lam_re[l], s5_lam_im[l], s5_log_step[l], s5_b_re[l], s5_b_im[l],
                         s5_c_re[l], s5_c_im[l], s5_d[l], w_s5_glu[l], w_mix_out[l])
        x = x + rms_norm(y, norm_mix_post[l])
        h = rms_norm(x, norm_xa_pre[l])
        y = cross_attention(h, rms_norm(mem, norm_mem[l]), w_xa_q[l], w_xa_kv[l], w_xa_out[l])
        x = x + rms_norm(y, norm_xa_post[l])
        h = rms_norm(x, norm_ffn_pre[l])
        y = conv_ffn(h, w_ffn_up[l], ffn_conv_w[l], w_ffn_down[l])
        x = x + rms_norm(y, norm_ffn_post[l])
    return x
```

```python
import numpy as np
from contextlib import ExitStack
import concourse.bass as bass
import concourse.mybir as mybir
from concourse.bass_utils import run_bass_kernel_spmd

F32 = mybir.dt.float32
BF16 = mybir.dt.bfloat16
I32 = mybir.dt.int32
ALU = mybir.AluOpType
AF = mybir.ActivationFunctionType
AX = mybir.AxisListType


class Prog:
    NDMA = 40

    def __init__(self, nc):
        self.nc = nc
        self.es = ExitStack()
        self.E = {"pe": nc.tensor, "dve": nc.vector, "act": nc.scalar,
                  "pool": nc.gpsimd, "sp": nc.sync}
        self.sem = {}
        self.cnt = {}
        for e in ("pe", "dve", "act", "pool"):
            self.sem[e] = self.es.enter_context(nc.semaphore("sem_" + e))
            self.cnt[e] = 0
        self.dsem = [self.es.enter_context(nc.semaphore("dsem%d" % i)) for i in range(self.NDMA)]
        self.dcnt = [0] * self.NDMA
        self.drr = 0
        self.seen = {e: {} for e in self.E}
        self.lastw = {}
        self.readers = {}
        self.nid = 0
        self.n_inst = 0

    def sb(self, shape, dtype=F32, name=None):
        self.nid += 1
        name = name or "t%d" % self.nid
        return self.es.enter_context(self.nc.sbuf_tensor(name, list(shape), dtype))

    def ps(self, shape, dtype=F32, name=None):
        self.nid += 1
        name = name or "p%d" % self.nid
        return self.es.enter_context(self.nc.psum_tensor(name, list(shape), dtype))

    def dram(self, name, shape, dtype=F32, kind="Internal"):
        return self.nc.dram_tensor(name, list(shape), dtype, kind=kind).ap()

    @staticmethod
    def _k(k):
        if isinstance(k, str):
            return k
        if isinstance(k, tuple):
            return (Prog._k(k[0]),) + tuple(k[1:])
        return k.name

    def _need(self, eng, reads, writes):
        need = {}
        reads = [self._k(k) for k in reads]
        writes = [self._k(k) for k in writes]

        def add(ev):
            if ev is None:
                return
            skey, val, src = ev
            if src == "pe" and eng == "pe":
                return
            if need.get(skey, 0) < val:
                need[skey] = val

        for k in reads:
            add(self.lastw.get(k))
        for k in writes:
            add(self.lastw.get(k))
            for ev in self.readers.get(k, ()):
                add(ev)
        return need

    def _semobj(self, skey):
        return self.sem[skey] if isinstance(skey, str) else self.dsem[skey]

    def _emit_waits(self, eng, need):
        seen = self.seen[eng]
        for skey, val in need.items():
            if seen.get(skey, 0) >= val:
                continue
            self.E[eng].wait_ge(self._semobj(skey), val)
            self.n_inst += 1
            seen[skey] = val

    def _record(self, ev, reads, writes):
        reads = [self._k(k) for k in reads]
        writes = [self._k(k) for k in writes]
        for k in writes:
            self.lastw[k] = ev
            self.readers[k] = []
        for k in reads:
            if k in writes:
                continue
            lst = self.readers.setdefault(k, [])
            lst[:] = [x for x in lst if x[0] != ev[0]] + [ev]

    def op(self, eng, fn, reads=(), writes=()):
        need = self._need(eng, reads, writes)
        self._emit_waits(eng, need)
        inst = fn()
        inst.then_inc(self.sem[eng], 1)
        self.cnt[eng] += 1
        self.n_inst += 1
        ev = (eng, self.cnt[eng], eng)
        self._record(ev, reads, writes)
        return inst

    def pe_group(self, fns, reads=(), writes=()):
        need = self._need("pe", reads, writes)
        self._emit_waits("pe", need)
        inst = None
        for fn in fns:
            inst = fn()
            self.n_inst += 1
        inst.then_inc(self.sem["pe"], 1)
        self.cnt["pe"] += 1
        ev = ("pe", self.cnt["pe"], "pe")
        self._record(ev, reads, writes)

    def dma(self, q, out, in_, reads=(), writes=(), **kw):
        i = self.drr
        self.drr = (self.drr + 1) % self.NDMA
        need = self._need(q, reads, writes)
        if self.dcnt[i] > 0:
            need[i] = max(need.get(i, 0), 16 * self.dcnt[i])
        self._emit_waits(q, need)
        inst = self.E[q].dma_start(out=out, in_=in_, **kw)
        self.dcnt[i] += 1
        inst.then_inc(self.dsem[i], 16)
        self.n_inst += 1
        ev = (i, 16 * self.dcnt[i], "dma")
        self._record(ev, reads, writes)
        return ev

    def finish(self, keys):
        need = {}
        for k in keys:
            ev = self.lastw.get(self._k(k))
            if ev is not None:
                need[ev[0]] = max(need.get(ev[0], 0), ev[1])
        self._emit_waits("sp", need)

    def close(self):
        self.es.close()


    def barrier(self):
        need = {e: self.cnt[e] for e in self.cnt if self.cnt[e] > 0}
        for i in range(self.NDMA):
            if self.dcnt[i] > 0:
                need[i] = 16 * self.dcnt[i]
        for e in ("sp", "pool", "act", "dve", "pe"):
            self._emit_waits(e, dict(need))

    class _Scope:
        def __init__(self, P):
            self.P = P

        def __enter__(self):
            self.saved = self.P.es
            self.P.es = ExitStack()
            return self

        def __exit__(self, *a):
            self.P.barrier()
            self.P.es.close()
            self.P.es = self.saved
            return False

    def scope(self):
        return Prog._Scope(self)


class Rot:
    def __init__(self, tiles):
        self.t = tiles
        self.i = 0

    def n(self):
        x = self.t[self.i]
        self.i = (self.i + 1) % len(self.t)
        return x


def rot_sb(P, n, shape, dtype=F32):
    return Rot([P.sb(shape, dtype) for _ in range(n)])


def rot_ps(P, n, shape, dtype=F32):
    return Rot([P.ps(shape, dtype) for _ in range(n)])


EPS = 1e-6


def load_consts(P, ident_d):
    C = {}
    C["ident"] = P.sb([128, 128], F32, name="ident_sb")
    P.dma("sp", C["ident"][:], ident_d, writes=[C["ident"]])
    return C


def rms_scale(P, nc, xt, D, sq, ss, rs, reads):
    P.op("act", lambda: nc.scalar.activation(out=sq[:, 0:D], in_=xt, func=AF.Square, accum_out=ss[:]),
         reads=reads, writes=[sq, ss])
    P.op("dve", lambda: nc.vector.tensor_scalar(out=rs[:], in0=ss[:], scalar1=1.0 / D, scalar2=EPS,
                                                op0=ALU.mult, op1=ALU.add), reads=[ss], writes=[rs])
    P.op("act", lambda: nc.scalar.activation(out=rs[:], in_=rs[:], func=AF.Sqrt), reads=[rs], writes=[rs])
    P.op("dve", lambda: nc.vector.reciprocal(out=rs[:], in_=rs[:]), reads=[rs], writes=[rs])


def build_actT(P, nc, C, src, skey, r0, NTOK, K, actT, norm_g=None):
    KT = K // 128
    with P.scope():
        xin = rot_sb(P, 2, [128, K])
        psT = rot_ps(P, 2, [128, 512])
        if norm_g is not None:
            gb = P.sb([128, K])
            P.dma("sp", gb[:], norm_g.partition_broadcast(128), writes=[gb])
            sq = P.sb([128, K])
            ssr = rot_sb(P, 2, [128, 1])
            rsr = rot_sb(P, 2, [128, 1])
            hnr = rot_sb(P, 2, [128, K])
        cnt = 0
        for t in range(NTOK // 128):
            xt = xin.n()
            P.dma("sp", xt[:], src[r0 + t * 128:r0 + (t + 1) * 128, 0:K], reads=[(skey, (r0 // 128) + t)], writes=[xt])
            if norm_g is not None:
                ss = ssr.n(); rs = rsr.n(); hn = hnr.n()
                rms_scale(P, nc, xt[:], K, sq, ss, rs, [xt])
                P.op("dve", lambda: nc.vector.scalar_tensor_tensor(out=hn[:], in0=xt[:], scalar=rs[:, 0:1], in1=gb[:],
                                                                    op0=ALU.mult, op1=ALU.mult),
                     reads=[xt, rs, gb], writes=[hn])
                xt = hn
            for k0 in range(0, KT, 4):
                pt = psT.n()
                P.pe_group([(lambda kk=kk: nc.tensor.transpose(out=pt[:, kk * 128:(kk + 1) * 128],
                                                               in_=xt[:, (k0 + kk) * 128:(k0 + kk + 1) * 128],
                                                               identity=C["ident"][:])) for kk in range(4)],
                           reads=[xt, C["ident"]], writes=[pt])
                e = "dve" if cnt % 2 == 0 else "act"
                cnt += 1
                dst = actT[:, k0:k0 + 4, t * 128:(t + 1) * 128]
                srcv = pt[:].rearrange("p (a b) -> p a b", a=4)
                if e == "dve":
                    P.op("dve", lambda: nc.vector.tensor_copy(out=dst, in_=srcv), reads=[pt], writes=[(actT, t)])
                else:
                    P.op("act", lambda: nc.scalar.copy(out=dst, in_=srcv), reads=[pt], writes=[(actT, t)])


def gemm_T(P, nc, actT, NTOK, K, w, wc0, N, dst, dkey, dr0, dc0, CW=512):
    KT = K // 128
    with P.scope():
        wbr = rot_sb(P, 2, [128, KT, CW], BF16)
        psO = rot_ps(P, 4, [128, CW])
        otr = rot_sb(P, 4, [128, CW])
        cnt = 0
        for c0 in range(0, N, CW):
            cw = min(CW, N - c0)
            wb = wbr.n()
            P.dma("pool", wb[:, :, 0:cw], w[:, wc0 + c0:wc0 + c0 + cw].rearrange("(kt p) n -> p kt n", p=128),
                  writes=[wb])
            for t in range(NTOK // 128):
                po = psO.n()
                P.pe_group([(lambda kt=kt: nc.tensor.matmul(po[:, 0:cw], lhsT=actT[:, kt, t * 128:(t + 1) * 128],
                                                            rhs=wb[:, kt, 0:cw], start=(kt == 0), stop=(kt == KT - 1)))
                            for kt in range(KT)], reads=[(actT, t), wb], writes=[po])
                ot = otr.n()
                if cnt % 2 == 0:
                    P.op("act", lambda: nc.scalar.copy(out=ot[:, 0:cw], in_=po[:, 0:cw]), reads=[po], writes=[ot])
                else:
                    P.op("dve", lambda: nc.vector.tensor_copy(out=ot[:, 0:cw], in_=po[:, 0:cw]), reads=[po], writes=[ot])
                cnt += 1
                P.dma("sp", dst[dr0 + t * 128:dr0 + (t + 1) * 128, dc0 + c0:dc0 + c0 + cw], ot[:, 0:cw],
                      reads=[ot], writes=[(dkey, dr0 // 128 + t)])


def linear_T(P, nc, C, src, skey, sr0, NTOK, K, w, wc0, N, dst, dkey, dr0=0, dc0=0, norm_g=None):
    with P.scope():
        actT = P.sb([128, K // 128, NTOK], BF16)
        build_actT(P, nc, C, src, skey, sr0, NTOK, K, actT, norm_g)
        gemm_T(P, nc, actT, NTOK, K, w, wc0, N, dst, dkey, dr0, dc0)


def residual_stage(P, nc, xsrc, xkey, ysrc, ykey, g, dst, dkey, NTOK, D=2048, xr0=0, yr0=0, dr0=0):
    with P.scope():
        gb = P.sb([128, D])
        P.dma("sp", gb[:], g.partition_broadcast(128), writes=[gb])
        xr = rot_sb(P, 2, [128, D]); yr = rot_sb(P, 2, [128, D]); sq = P.sb([128, D])
        ssr = rot_sb(P, 2, [128, 1]); rsr = rot_sb(P, 2, [128, 1])
        for t in range(NTOK // 128):
            xt = xr.n(); yt = yr.n(); ss = ssr.n(); rs = rsr.n()
            P.dma("sp", xt[:], xsrc[xr0 + t * 128:xr0 + (t + 1) * 128, :], reads=[(xkey, xr0 // 128 + t)], writes=[xt])
            P.dma("sp", yt[:], ysrc[yr0 + t * 128:yr0 + (t + 1) * 128, :], reads=[(ykey, yr0 // 128 + t)], writes=[yt])
            rms_scale(P, nc, yt[:], D, sq, ss, rs, [yt])
            P.op("dve", lambda: nc.vector.scalar_tensor_tensor(out=yt[:], in0=yt[:], scalar=rs[:, 0:1], in1=gb[:],
                                                                op0=ALU.mult, op1=ALU.mult), reads=[yt, rs, gb], writes=[yt])
            P.op("pool", lambda: nc.gpsimd.tensor_tensor(out=xt[:], in0=xt[:], in1=yt[:], op=ALU.add),
                 reads=[xt, yt], writes=[xt])
            P.dma("sp", dst[dr0 + t * 128:dr0 + (t + 1) * 128, :], xt[:], reads=[xt], writes=[(dkey, dr0 // 128 + t)])


D_MODEL = 2048
SEQ = 4096
BATCH = 4
DEPTH = 2
N_MEM = 256
IN_SIZES = (1024, 1024, 1024, 1024, 8, 8, 512, 512, 1024, 16, 1024, 1024, 6144)
IN_OFF = [0]
for _s in IN_SIZES:
    IN_OFF.append(IN_OFF[-1] + _s)
IN_WIDTH = IN_OFF[-1]
(O_DQ, O_DK, O_DV, O_DZ, O_DB, O_DA, O_GQ, O_GK, O_GV, O_GLOW, O_GR, O_SU, O_GATES) = IN_OFF[:13]
FFN_H = 5632
NTOK = 2048

_NC_CACHE = {}


def build_LA():
    nc = bass.Bass("TRN2", target_bir_lowering=False)
    P = Prog(nc)
    x = P.dram("x", [NTOK, D_MODEL], F32, "ExternalInput")
    w = P.dram("w_in", [D_MODEL, IN_WIDTH], F32, "ExternalInput")
    g = P.dram("g", [1, D_MODEL], F32, "ExternalInput")
    ident = P.dram("ident", [128, 128], F32, "ExternalInput")
    z = P.dram("z", [NTOK, IN_WIDTH], F32, "ExternalOutput")
    C = load_consts(P, ident)
    linear_T(P, nc, C, x, "x", 0, NTOK, D_MODEL, w, 0, IN_WIDTH, z, "z", norm_g=g)
    P.finish([("z", t) for t in range(NTOK // 128)])
    P.close()
    return nc


def run(nc, in_maps):
    res = run_bass_kernel_spmd(nc, in_maps, core_ids=list(range(8)))
    return res.results


def merge_stage(P, nc, ya, yb, yg, gates, gkey, gc0, merged, NTOK, r0=0):
    D = D_MODEL
    with P.scope():
        yar = rot_sb(P, 2, [128, D]); ybr = rot_sb(P, 2, [128, D]); ygr = rot_sb(P, 2, [128, 2 * D])
        gtr = rot_sb(P, 2, [128, 3 * D])
        for t in range(NTOK // 128):
            rows = slice(r0 + t * 128, r0 + (t + 1) * 128)
            tt = r0 // 128 + t
            a = yar.n(); b = ybr.n(); gl = ygr.n(); gt = gtr.n()
            P.dma("sp", a[:], ya[rows, :], reads=[("ya", tt)], writes=[a])
            P.dma("sp", b[:], yb[rows, :], reads=[("yb", tt)], writes=[b])
            P.dma("sp", gl[:], yg[rows, :], reads=[("yg", tt)], writes=[gl])
            P.dma("sp", gt[:], gates[rows, gc0:gc0 + 3 * D], reads=[(gkey, tt)], writes=[gt])
            P.op("act", lambda: nc.scalar.activation(out=gt[:], in_=gt[:], func=AF.Sigmoid), reads=[gt], writes=[gt])
            P.op("act", lambda: nc.scalar.activation(out=gl[:, D:2 * D], in_=gl[:, D:2 * D], func=AF.Sigmoid),
                 reads=[gl], writes=[gl])
            P.op("dve", lambda: nc.vector.tensor_tensor(out=a[:], in0=a[:], in1=gt[:, 0:D], op=ALU.mult),
                 reads=[a, gt], writes=[a])
            P.op("pool", lambda: nc.gpsimd.tensor_tensor(out=b[:], in0=b[:], in1=gt[:, D:2 * D], op=ALU.mult),
                 reads=[b, gt], writes=[b])
            P.op("dve", lambda: nc.vector.tensor_tensor(out=gl[:, 0:D], in0=gl[:, 0:D], in1=gl[:, D:2 * D], op=ALU.mult),
                 reads=[gl], writes=[gl])
            P.op("pool", lambda: nc.gpsimd.tensor_tensor(out=a[:], in0=a[:], in1=b[:], op=ALU.add),
                 reads=[a, b], writes=[a])
            P.op("dve", lambda: nc.vector.tensor_tensor(out=gl[:, 0:D], in0=gl[:, 0:D], in1=gt[:, 2 * D:3 * D], op=ALU.mult),
                 reads=[gl, gt], writes=[gl])
            P.op("dve", lambda: nc.vector.tensor_tensor(out=a[:], in0=a[:], in1=gl[:, 0:D], op=ALU.add),
                 reads=[a, gl], writes=[a])
            P.dma("sp", merged[rows, :], a[:], reads=[a], writes=[("merged", tt)])


def attn_stage(P, nc, C, q, kv, o, NTOK, r0=0):
    H = 4
    sc = 128.0 ** -0.5
    with P.scope():
        kvt = P.sb([128, 2, 1024])
        P.dma("sp", kvt[:], kv.rearrange("(a p) n -> p a n", p=128), reads=[("kv", 0), ("kv", 1)], writes=[kvt])
        kT = P.sb([128, H, 256])
        psT = rot_ps(P, 2, [128, 512])
        psS = [P.ps([128, 512]) for _ in range(H)]
        psO = rot_ps(P, 2, [128, 512])
        for h in range(H):
            pt = psT.n()
            P.pe_group([(lambda a=a: nc.tensor.transpose(out=pt[:, a * 128:(a + 1) * 128],
                                                         in_=kvt[:, a, h * 128:(h + 1) * 128], identity=C["ident"][:]))
                        for a in range(2)], reads=[kvt, C["ident"]], writes=[pt])
            P.op("dve", lambda: nc.vector.tensor_copy(out=kT[:, h, :], in_=pt[:, 0:256]), reads=[pt], writes=[kT])
        qr = rot_sb(P, 2, [128, 512]); qTr = rot_sb(P, 2, [128, 512]); otr = rot_sb(P, 2, [128, 512])
        hp = [dict(p=rot_sb(P, 2, [128, 256]), pT=rot_sb(P, 2, [128, 256]), mx=rot_sb(P, 2, [128, 1]), sm=rot_sb(P, 2, [128, 1])) for _ in range(H)]

        def head(h, qT, ot):
            s_ps = psS[h]
            mx = hp[h]["mx"].n(); sm = hp[h]["sm"].n(); p = hp[h]["p"].n(); pT = hp[h]["pT"].n()
            P.pe_group([lambda: nc.tensor.matmul(s_ps[:, 0:256], lhsT=qT[:, h * 128:(h + 1) * 128], rhs=kT[:, h, :],
                                                 start=True, stop=True)], reads=[qT, kT], writes=[s_ps])
            P.op("dve", lambda: nc.vector.tensor_reduce(out=mx[:], in_=s_ps[:, 0:256], axis=AX.X, op=ALU.max),
                 reads=[s_ps], writes=[mx])
            yield
            P.op("dve", lambda: nc.vector.tensor_scalar(out=mx[:], in0=mx[:], scalar1=-sc, scalar2=None, op0=ALU.mult),
                 reads=[mx], writes=[mx])
            yield
            P.op("act", lambda: nc.scalar.activation(out=p[:], in_=s_ps[:, 0:256], func=AF.Exp, bias=mx[:, 0:1], scale=sc,
                                                     accum_out=sm[:]), reads=[s_ps, mx], writes=[p, sm])
            yield
            P.op("dve", lambda: nc.vector.reciprocal(out=sm[:], in_=sm[:]), reads=[sm], writes=[sm])
            pt2 = psT.n()
            P.pe_group([(lambda a=a: nc.tensor.transpose(out=pt2[:, a * 128:(a + 1) * 128], in_=p[:, a * 128:(a + 1) * 128],
                                                         identity=C["ident"][:])) for a in range(2)],
                       reads=[p, C["ident"]], writes=[pt2])
            P.op("dve", lambda: nc.vector.tensor_copy(out=pT[:], in_=pt2[:, 0:256]), reads=[pt2], writes=[pT])
            yield
            o_ps = psO.n()
            P.pe_group([(lambda a=a: nc.tensor.matmul(o_ps[:, 0:128], lhsT=pT[:, a * 128:(a + 1) * 128],
                                                      rhs=kvt[:, a, 512 + h * 128:512 + (h + 1) * 128],
                                                      start=(a == 0), stop=(a == 1))) for a in range(2)],
                       reads=[pT, kvt], writes=[o_ps])
            P.op("dve", lambda: nc.vector.tensor_scalar(out=ot[:, h * 128:(h + 1) * 128], in0=o_ps[:, 0:128], scalar1=sm[:, 0:1], scalar2=None,
                                                        op0=ALU.mult), reads=[o_ps, sm], writes=[(ot, h)])
            yield

        for t in range(NTOK // 128):
            rows = slice(r0 + t * 128, r0 + (t + 1) * 128)
            tt = r0 // 128 + t
            qt = qr.n(); qT = qTr.n(); ot = otr.n()
            P.dma("sp", qt[:], q[rows, :], reads=[("q", tt)], writes=[qt])
            pt = psT.n()
            P.pe_group([(lambda h=h: nc.tensor.transpose(out=pt[:, h * 128:(h + 1) * 128], in_=qt[:, h * 128:(h + 1) * 128],
                                                         identity=C["ident"][:])) for h in range(H)],
                       reads=[qt, C["ident"]], writes=[pt])
            P.op("act", lambda: nc.scalar.copy(out=qT[:], in_=pt[:]), reads=[pt], writes=[qT])
            gens = [head(h, qT, ot) for h in range(H)]
            while gens:
                for g_ in list(gens):
                    try:
                        next(g_)
                    except StopIteration:
                        gens.remove(g_)
            P.dma("sp", o[rows, :], ot[:], reads=[(ot, h) for h in range(H)], writes=[("o", tt)])


def build_LC():
    nc = bass.Bass("TRN2", target_bir_lowering=False)
    P = Prog(nc)
    D = D_MODEL
    dr = lambda n, s, k="ExternalInput": P.dram(n, s, F32, k)
    x = dr("x", [NTOK, D]); oa = dr("oa", [NTOK, 1024]); ob = dr("ob", [NTOK, 1024]); ys = dr("ys", [NTOK, 1024])
    gates = dr("gates", [NTOK, 3 * D]); mem = dr("mem", [N_MEM, D])
    w_dn = dr("w_dn_out", [1024, D]); w_gla = dr("w_gla_out", [1024, D]); w_glu = dr("w_s5_glu", [1024, 2 * D])
    w_mix = dr("w_mix_out", [D, D]); w_q = dr("w_xa_q", [D, 512]); w_kv = dr("w_xa_kv", [D, 1024]); w_o = dr("w_xa_out", [512, D])
    g_post = dr("g_mix_post", [1, D]); g_xpre = dr("g_xa_pre", [1, D]); g_xpost = dr("g_xa_post", [1, D]); g_mem = dr("g_mem", [1, D])
    ident = dr("ident", [128, 128])
    x2 = dr("x2", [NTOK, D], "ExternalOutput")
    ya = dr("ya", [NTOK, D], "Internal"); yb = dr("yb", [NTOK, D], "Internal"); yg = dr("yg", [NTOK, 2 * D], "Internal")
    merged = dr("merged", [NTOK, D], "Internal"); y1 = dr("y1", [NTOK, D], "Internal"); x1 = dr("x1", [NTOK, D], "Internal")
    q = dr("q", [NTOK, 512], "Internal"); kv = dr("kv", [N_MEM, 1024], "Internal"); o = dr("o", [NTOK, 512], "Internal")
    y2 = dr("y2", [NTOK, D], "Internal")
    C = load_consts(P, ident)
    linear_T(P, nc, C, oa, "oa", 0, NTOK, 1024, w_dn, 0, D, ya, "ya")
    linear_T(P, nc, C, ob, "ob", 0, NTOK, 1024, w_gla, 0, D, yb, "yb")
    linear_T(P, nc, C, ys, "ys", 0, NTOK, 1024, w_glu, 0, 2 * D, yg, "yg")
    merge_stage(P, nc, ya, yb, yg, gates, merged, NTOK)
    linear_T(P, nc, C, merged, "merged", 0, NTOK, D, w_mix, 0, D, y1, "y1")
    residual_stage(P, nc, x, "x", y1, "y1", g_post, x1, "x1", NTOK)
    linear_T(P, nc, C, x1, "x1", 0, NTOK, D, w_q, 0, 512, q, "q", norm_g=g_xpre)
    linear_T(P, nc, C, mem, "mem", 0, N_MEM, D, w_kv, 0, 1024, kv, "kv", norm_g=g_mem)
    attn_stage(P, nc, C, q, kv, o, NTOK)
    linear_T(P, nc, C, o, "o", 0, NTOK, 512, w_o, 0, D, y2, "y2")
    residual_stage(P, nc, x1, "x1", y2, "y2", g_xpost, x2, "x2", NTOK)
    P.finish([("x2", t) for t in range(NTOK // 128)])
    P.close()
    return nc


def gelu_tanh_mul(P, nc, cbuf, up_ps, out_ap, tmp1, tmp2, wkey):
    P.op("pool", lambda: nc.gpsimd.tensor_tensor(out=tmp1[:], in0=cbuf[:], in1=cbuf[:], op=ALU.mult),
         reads=[cbuf], writes=[tmp1])
    P.op("dve", lambda: nc.vector.tensor_scalar(out=tmp1[:], in0=tmp1[:], scalar1=0.044715, scalar2=1.0,
                                                op0=ALU.mult, op1=ALU.add), reads=[tmp1], writes=[tmp1])
    P.op("pool", lambda: nc.gpsimd.tensor_tensor(out=tmp1[:], in0=tmp1[:], in1=cbuf[:], op=ALU.mult),
         reads=[tmp1, cbuf], writes=[tmp1])
    P.op("act", lambda: nc.scalar.activation(out=tmp1[:], in_=tmp1[:], func=AF.Sigmoid, scale=1.5957691216057308),
         reads=[tmp1], writes=[tmp1])
    P.op("dve", lambda: nc.vector.tensor_tensor(out=tmp2[:], in0=cbuf[:], in1=up_ps, op=ALU.mult),
         reads=[cbuf, wkey], writes=[tmp2])
    P.op("dve", lambda: nc.vector.tensor_tensor(out=out_ap, in0=tmp1[:], in1=tmp2[:], op=ALU.mult),
         reads=[tmp1, tmp2], writes=[wkey + "_o"])


def ffn_stage(P, nc, C, x2h, w_up, w_dn, cwd, g_pre, y3):
    D = D_MODEL
    FT = FFN_H // 128
    cw = P.sb([128, FT, 3])
    P.dma("sp", cw[:], cwd.rearrange("p (f j) -> p f j", j=3), writes=[cw])
    HALF = 1024
    for half in range(SEQ // HALF):
        with P.scope():
            AT = P.sb([128, FT, HALF], BF16)
            with P.scope():
                hT = P.sb([128, 16, 128 + HALF], BF16)
                build_actT(P, nc, C, x2h, "x2h", half * HALF, 128 + HALF, D, hT, norm_g=g_pre)
                wgr = rot_sb(P, 2, [128, 16, 256], BF16); wur = rot_sb(P, 2, [128, 16, 256], BF16)
                psH = rot_ps(P, 1, [128, 512]); psG = rot_ps(P, 2, [128, 512]); psU = rot_ps(P, 2, [128, 512])
                gbr = rot_sb(P, 2, [128, 514]); cbr = rot_sb(P, 2, [128, 512]); t1r = rot_sb(P, 2, [128, 512]); t2r = rot_sb(P, 2, [128, 512])
                for fc in range(FFN_H // 256):
                    wg = wgr.n(); wu = wur.n()
                    P.dma("pool", wg[:], w_up[:, fc * 256:(fc + 1) * 256].rearrange("(kt p) n -> p kt n", p=128), writes=[wg])
                    P.dma("pool", wu[:], w_up[:, FFN_H + fc * 256:FFN_H + (fc + 1) * 256].rearrange("(kt p) n -> p kt n", p=128), writes=[wu])
                    for sub in range(2):
                        f = fc * 2 + sub
                        ph = psH.n()
                        P.pe_group([(lambda kt=kt: nc.tensor.matmul(ph[:, 0:128], lhsT=wg[:, kt, sub * 128:(sub + 1) * 128], rhs=hT[:, kt, 0:128],
                                                                    start=(kt == 0), stop=(kt == 15))) for kt in range(16)],
                                   reads=[wg, (hT, 0)], writes=[ph])
                        gb = gbr.n()
                        P.op("act", lambda: nc.scalar.copy(out=gb[:, 0:2], in_=ph[:, 126:128]), reads=[ph], writes=[gb])
                        for tb in range(HALF // 512):
                            c0 = 128 + tb * 512
                            pg = psG.n(); pu = psU.n()
                            rk = [(hT, (c0 // 128) + i) for i in range(4)]
                            P.pe_group([(lambda kt=kt: nc.tensor.matmul(pg[:], lhsT=wg[:, kt, sub * 128:(sub + 1) * 128], rhs=hT[:, kt, c0:c0 + 512],
                                                                        start=(kt == 0), stop=(kt == 15))) for kt in range(16)],
                                       reads=[wg] + rk, writes=[pg])
                            P.pe_group([(lambda kt=kt: nc.tensor.matmul(pu[:], lhsT=wu[:, kt, sub * 128:(sub + 1) * 128], rhs=hT[:, kt, c0:c0 + 512],
                                                                        start=(kt == 0), stop=(kt == 15))) for kt in range(16)],
                                       reads=[wu] + rk, writes=[pu])
                            P.op("act", lambda: nc.scalar.copy(out=gb[:, 2:514], in_=pg[:]), reads=[pg], writes=[gb])
                            cb = cbr.n(); t1 = t1r.n(); t2 = t2r.n()
                            P.op("dve", lambda: nc.vector.tensor_scalar(out=cb[:], in0=gb[:, 0:512], scalar1=cw[:, f, 0:1], scalar2=None, op0=ALU.mult),
                                 reads=[gb, cw], writes=[cb])
                            P.op("dve", lambda: nc.vector.scalar_tensor_tensor(out=cb[:], in0=gb[:, 1:513], scalar=cw[:, f, 1:2], in1=cb[:],
                                                                                op0=ALU.mult, op1=ALU.add), reads=[gb, cw, cb], writes=[cb])
                            P.op("dve", lambda: nc.vector.scalar_tensor_tensor(out=cb[:], in0=gb[:, 2:514], scalar=cw[:, f, 2:3], in1=cb[:],
                                                                                op0=ALU.mult, op1=ALU.add), reads=[gb, cw, cb], writes=[cb])
                            if tb + 1 < HALF // 512:
                                gb2 = gbr.n()
                                P.op("act", lambda: nc.scalar.copy(out=gb2[:, 0:2], in_=gb[:, 512:514]), reads=[gb], writes=[gb2])
                            P.op("pool", lambda: nc.gpsimd.tensor_tensor(out=t1[:], in0=cb[:], in1=cb[:], op=ALU.mult), reads=[cb], writes=[t1])
                            P.op("dve", lambda: nc.vector.tensor_scalar(out=t1[:], in0=t1[:], scalar1=0.044715, scalar2=1.0, op0=ALU.mult, op1=ALU.add),
                                 reads=[t1], writes=[t1])
                            P.op("pool", lambda: nc.gpsimd.tensor_tensor(out=t1[:], in0=t1[:], in1=cb[:], op=ALU.mult), reads=[t1, cb], writes=[t1])
                            P.op("act", lambda: nc.scalar.activation(out=t1[:], in_=t1[:], func=AF.Sigmoid, scale=1.5957691216057308),
                                 reads=[t1], writes=[t1])
                            P.op("dve", lambda: nc.vector.tensor_tensor(out=t2[:], in0=cb[:], in1=pu[:], op=ALU.mult), reads=[cb, pu], writes=[t2])
                            P.op("dve", lambda: nc.vector.tensor_tensor(out=AT[:, f, tb * 512:(tb + 1) * 512], in0=t1[:], in1=t2[:], op=ALU.mult),
                                 reads=[t1, t2], writes=[(AT, f)])
                            if tb + 1 < HALF // 512:
                                gb = gb2
            with P.scope():
                wdr = rot_sb(P, 2, [128, FT, 256], BF16)
                psO = rot_ps(P, 3, [128, 512]); otr = rot_sb(P, 3, [128, 256])
                akeys = [(AT, f) for f in range(FT)]
                cnt = 0
                for cbk in range(D // 256):
                    wd = wdr.n()
                    P.dma("pool", wd[:], w_dn[:, cbk * 256:(cbk + 1) * 256].rearrange("(ft p) n -> p ft n", p=128), writes=[wd])
                    for t in range(HALF // 128):
                        po = psO.n()
                        P.pe_group([(lambda f=f: nc.tensor.matmul(po[:, 0:256], lhsT=AT[:, f, t * 128:(t + 1) * 128], rhs=wd[:, f, :],
                                                                  start=(f == 0), stop=(f == FT - 1))) for f in range(FT)],
                                   reads=[wd] + akeys, writes=[po])
                        ot = otr.n()
                        if cnt % 2 == 0:
                            P.op("act", lambda: nc.scalar.copy(out=ot[:], in_=po[:, 0:256]), reads=[po], writes=[ot])
                        else:
                            P.op("dve", lambda: nc.vector.tensor_copy(out=ot[:], in_=po[:, 0:256]), reads=[po], writes=[ot])
                        cnt += 1
                        r0 = half * HALF + t * 128
                        P.dma("sp", y3[r0:r0 + 128, cbk * 256:(cbk + 1) * 256], ot[:], reads=[ot], writes=[("y3", r0 // 128)])


TWO_PI = 6.283185307179586


def sin_reduced(P, nc, out, ang, shift, shape, tmpa, tmpk, tmpf):
    P.op("dve", lambda: nc.vector.tensor_scalar(out=tmpa, in0=ang, scalar1=float(shift), scalar2=None, op0=ALU.add),
         reads=["sr_ang"], writes=["sr_a"])
    P.op("dve", lambda: nc.vector.tensor_scalar(out=tmpk, in0=tmpa, scalar1=1.0 / TWO_PI, scalar2=None, op0=ALU.mult),
         reads=["sr_a"], writes=["sr_k"])
    P.op("dve", lambda: nc.vector.tensor_copy(out=tmpf, in_=tmpk), reads=["sr_k"], writes=["sr_f"])
    P.op("dve", lambda: nc.vector.scalar_tensor_tensor(out=tmpa, in0=tmpf, scalar=-TWO_PI, in1=tmpa, op0=ALU.mult, op1=ALU.add),
         reads=["sr_f", "sr_a"], writes=["sr_a"])
    P.op("dve", lambda: nc.vector.tensor_scalar(out=tmpf, in0=tmpa, scalar1=3.141592653589793, scalar2=None, op0=ALU.is_gt),
         reads=["sr_a"], writes=["sr_f"])
    P.op("dve", lambda: nc.vector.scalar_tensor_tensor(out=tmpa, in0=tmpf, scalar=-TWO_PI, in1=tmpa, op0=ALU.mult, op1=ALU.add),
         reads=["sr_f", "sr_a"], writes=["sr_a"])
    P.op("dve", lambda: nc.vector.tensor_scalar(out=tmpf, in0=tmpa, scalar1=-3.141592653589793, scalar2=None, op0=ALU.is_lt),
         reads=["sr_a"], writes=["sr_f"])
    P.op("dve", lambda: nc.vector.scalar_tensor_tensor(out=tmpa, in0=tmpf, scalar=TWO_PI, in1=tmpa, op0=ALU.mult, op1=ALU.add),
         reads=["sr_f", "sr_a"], writes=["sr_a"])
    P.op("act", lambda: nc.scalar.activation(out=out, in_=tmpa, func=AF.Sin), reads=["sr_a"], writes=["sr_out"])


def s5_stage(P, nc, C, z, ysT, par):
    CH = 256
    NCH = SEQ // CH
    for half in range(2):
        with P.scope():
            NS = 16
            s0 = half * NS
            lre = P.sb([128, NS]); lim = P.sb([128, NS]); dt = P.sb([128, NS]); mag = P.sb([128, NS]); th = P.sb([128, NS])
            zr = P.sb([128, NS]); zi = P.sb([128, NS]); ar = P.sb([128, NS]); ai = P.sb([128, NS]); den = P.sb([128, NS]); tq = P.sb([128, NS])
            dcol = P.sb([128, 4])
            Bre = P.sb([128, NS, 128]); Bim = P.sb([128, NS, 128]); Cre = P.sb([128, NS, 128]); Cim = P.sb([128, NS, 128])
            tau = P.sb([128, CH])
            P.dma("sp", lre[:], par["s5_lre"][:, s0:s0 + NS], writes=[lre])
            P.dma("sp", lim[:], par["s5_lim"][:, s0:s0 + NS], writes=[lim])
            P.dma("sp", dt[:], par["s5_lstep"][:, s0:s0 + NS], writes=[dt])
            P.dma("sp", dcol[:], par["s5_d"][:, half * 4:(half + 1) * 4], writes=[dcol])
            P.dma("sp", tau[:], par["tau"], writes=[tau])
            for nm, tl in (("s5_Bre", Bre), ("s5_Bim", Bim), ("s5_Cre", Cre), ("s5_Cim", Cim)):
                P.dma("sp", tl[:], par[nm][s0:s0 + NS].rearrange("s p m -> p s m"), writes=[tl])
            P.op("pool", lambda: nc.gpsimd.tensor_scalar(out=Cim[:], in0=Cim[:], scalar1=-1.0, scalar2=None, op0=ALU.mult), reads=[Cim], writes=[Cim])
            V = lambda fn, r, w: P.op("dve", fn, reads=r, writes=w)
            P.op("act", lambda: nc.scalar.activation(out=dt[:], in_=dt[:], func=AF.Exp), reads=[dt], writes=[dt])
            V(lambda: nc.vector.tensor_scalar(out=lre[:], in0=lre[:], scalar1=-1e-4, scalar2=None, op0=ALU.min), [lre], [lre])
            V(lambda: nc.vector.tensor_tensor(out=mag[:], in0=lre[:], in1=dt[:], op=ALU.mult), [lre, dt], [mag])
            P.op("act", lambda: nc.scalar.activation(out=mag[:], in_=mag[:], func=AF.Exp), reads=[mag], writes=[mag])
            V(lambda: nc.vector.tensor_tensor(out=th[:], in0=lim[:], in1=dt[:], op=ALU.mult), [lim, dt], [th])
            ta = P.sb([128, CH]); tk = P.sb([128, CH], I32); tf = P.sb([128, CH])
            sn = P.sb([128, NS]); cs = P.sb([128, NS])
            P._record(("dve", P.cnt["dve"], "dve"), [], ["sr_ang"])
            sin_reduced(P, nc, sn[:], th[:], 0.0, None, ta[:, 0:NS], tk[:, 0:NS], tf[:, 0:NS])
            sin_reduced(P, nc, cs[:], th[:], 1.5707963267948966, None, ta[:, 0:NS], tk[:, 0:NS], tf[:, 0:NS])
            V(lambda: nc.vector.tensor_tensor(out=ar[:], in0=mag[:], in1=cs[:], op=ALU.mult), [mag, "sr_out"], [ar])
            V(lambda: nc.vector.tensor_tensor(out=ai[:], in0=mag[:], in1=sn[:], op=ALU.mult), [mag, "sr_out"], [ai])
            V(lambda: nc.vector.tensor_scalar(out=ar[:], in0=ar[:], scalar1=-1.0, scalar2=None, op0=ALU.add), [ar], [ar])
            V(lambda: nc.vector.tensor_tensor(out=den[:], in0=lre[:], in1=lre[:], op=ALU.mult), [lre], [den])
            V(lambda: nc.vector.tensor_tensor(out=tq[:], in0=lim[:], in1=lim[:], op=ALU.mult), [lim], [tq])
            V(lambda: nc.vector.tensor_tensor(out=den[:], in0=den[:], in1=tq[:], op=ALU.add), [den, tq], [den])
            V(lambda: nc.vector.reciprocal(out=den[:], in_=den[:]), [den], [den])
            V(lambda: nc.vector.tensor_tensor(out=zr[:], in0=ar[:], in1=lre[:], op=ALU.mult), [ar, lre], [zr])
            V(lambda: nc.vector.tensor_tensor(out=tq[:], in0=ai[:], in1=lim[:], op=ALU.mult), [ai, lim], [tq])
            V(lambda: nc.vector.tensor_tensor(out=zr[:], in0=zr[:], in1=tq[:], op=ALU.add), [zr, tq], [zr])
            V(lambda: nc.vector.tensor_tensor(out=zr[:], in0=zr[:], in1=den[:], op=ALU.mult), [zr, den], [zr])
            V(lambda: nc.vector.tensor_tensor(out=zi[:], in0=ai[:], in1=lre[:], op=ALU.mult), [ai, lre], [zi])
            V(lambda: nc.vector.tensor_tensor(out=tq[:], in0=ar[:], in1=lim[:], op=ALU.mult), [ar, lim], [tq])
            V(lambda: nc.vector.tensor_tensor(out=zi[:], in0=zi[:], in1=tq[:], op=ALU.subtract), [zi, tq], [zi])
            V(lambda: nc.vector.tensor_tensor(out=zi[:], in0=zi[:], in1=den[:], op=ALU.mult), [zi, den], [zi])
            TC = P.sb([128, NS, CH]); TS = P.sb([128, NS, CH]); TzR = P.sb([128, NS, CH]); TzI = P.sb([128, NS, CH])
            ang = P.sb([128, CH])
            for s in range(NS):
                V(lambda: nc.vector.tensor_scalar(out=ang[:], in0=tau[:], scalar1=th[:, s:s + 1], scalar2=None, op0=ALU.mult), [tau, th, "sr_a"], ["sr_ang"])
                sin_reduced(P, nc, TS[:, s, :], ang[:], 0.0, None, ta[:], tk[:], tf[:])
                sin_reduced(P, nc, TC[:, s, :], ang[:], 1.5707963267948966, None, ta[:], tk[:], tf[:])
                V(lambda: nc.vector.tensor_scalar(out=ta[:], in0=TS[:, s, :], scalar1=zi[:, s:s + 1], scalar2=None, op0=ALU.mult), ["sr_out", zi], ["sr_a"])
                V(lambda: nc.vector.scalar_tensor_tensor(out=TzR[:, s, :], in0=TC[:, s, :], scalar=zr[:, s:s + 1], in1=ta[:], op0=ALU.mult, op1=ALU.add),
                  ["sr_a", zr, "sr_out"], [TzR])
                V(lambda: nc.vector.tensor_scalar(out=ta[:], in0=TS[:, s, :], scalar1=zr[:, s:s + 1], scalar2=None, op0=ALU.mult), ["sr_out", zr], ["sr_a"])
                V(lambda: nc.vector.scalar_tensor_tensor(out=TzI[:, s, :], in0=TC[:, s, :], scalar=zi[:, s:s + 1], in1=ta[:], op0=ALU.mult, op1=ALU.subtract),
                  ["sr_a", zi, "sr_out"], [TzI])
            tabs = ["sr_out", TzR, TzI]
            carry = P.sb([128, NS, 2])
            P.op("dve", lambda: nc.vector.memset(carry[:], 0.0), writes=[(carry, s_) for s_ in range(NS)])
            psB = rot_ps(P, 4, [128, 512]); psY = rot_ps(P, 2, [128, 512]); psT = rot_ps(P, 2, [128, 512])
            tmps = [rot_sb(P, 8, [128, CH]) for _ in range(4)]
            xsets = [[(P.sb([128, CH]), P.sb([128, CH])) for _ in range(4)] for _ in range(2)]
            ucr = rot_sb(P, 2, [128, 4, CH]); zin = rot_sb(P, 2, [128, 512])
            yo = rot_sb(P, 2, [128, CH]); y1 = rot_sb(P, 2, [128, CH]); y2 = rot_sb(P, 2, [128, CH])
            Pl = lambda fn, r, w: P.op("pool", fn, reads=r, writes=w)

            def st_gen(s, kb, cs_, uc, tmp, x_r, x_i):
                br = psB.n(); bi = psB.n()
                P.pe_group([lambda: nc.tensor.matmul(br[:, 0:CH], lhsT=Bre[:, s, :], rhs=uc[:, kb, :], start=True, stop=True)], reads=[Bre, uc], writes=[br])
                P.pe_group([lambda: nc.tensor.matmul(bi[:, 0:CH], lhsT=Bim[:, s, :], rhs=uc[:, kb, :], start=True, stop=True)], reads=[Bim, uc], writes=[bi])
                m1 = tmp.n(); m2 = tmp.n(); m3 = tmp.n(); m4 = tmp.n()
                V(lambda: nc.vector.tensor_tensor(out=m1[:], in0=br[:, 0:CH], in1=TzR[:, s, :], op=ALU.mult), [br] + tabs, [m1])
                V(lambda: nc.vector.tensor_tensor(out=m2[:], in0=bi[:, 0:CH], in1=TzI[:, s, :], op=ALU.mult), [bi] + tabs, [m2])
                V(lambda: nc.vector.tensor_tensor(out=m3[:], in0=bi[:, 0:CH], in1=TzR[:, s, :], op=ALU.mult), [bi] + tabs, [m3])
                V(lambda: nc.vector.tensor_tensor(out=m4[:], in0=br[:, 0:CH], in1=TzI[:, s, :], op=ALU.mult), [br] + tabs, [m4])
                yield
                bre_ = tmp.n(); bim_ = tmp.n()
                V(lambda: nc.vector.tensor_tensor(out=bre_[:], in0=m1[:], in1=m2[:], op=ALU.subtract), [m1, m2], [bre_])
                V(lambda: nc.vector.tensor_tensor(out=bim_[:], in0=m3[:], in1=m4[:], op=ALU.add), [m3, m4], [bim_])
                yield
                w_r = tmp.n(); w_i = tmp.n()
                magb = mag[:, s:s + 1].to_broadcast([128, CH])
                V(lambda: nc.vector.tensor_tensor_scan(out=w_r[:], data0=magb, data1=bre_[:], initial=carry[:, s, 0:1], op0=ALU.mult, op1=ALU.add),
                  [mag, bre_, (carry, s)], [w_r])
                yield
                V(lambda: nc.vector.tensor_tensor_scan(out=w_i[:], data0=magb, data1=bim_[:], initial=carry[:, s, 1:2], op0=ALU.mult, op1=ALU.add),
                  [mag, bim_, (carry, s)], [w_i])
                yield
                n1 = tmp.n(); n2 = tmp.n()
                Pl(lambda: nc.gpsimd.tensor_tensor(out=n1[:], in0=w_r[:], in1=TC[:, s, :], op=ALU.mult), [w_r] + tabs, [n1])
                Pl(lambda: nc.gpsimd.tensor_tensor(out=n2[:], in0=w_i[:], in1=TS[:, s, :], op=ALU.mult), [w_i] + tabs, [n2])
                Pl(lambda: nc.gpsimd.tensor_tensor(out=x_r[:], in0=n1[:], in1=n2[:], op=ALU.subtract), [n1, n2], [x_r])
                yield
                n3 = tmp.n(); n4 = tmp.n()
                V(lambda: nc.vector.tensor_tensor(out=n3[:], in0=w_r[:], in1=TS[:, s, :], op=ALU.mult), [w_r] + tabs, [n3])
                Pl(lambda: nc.gpsimd.tensor_tensor(out=n4[:], in0=w_i[:], in1=TC[:, s, :], op=ALU.mult), [w_i] + tabs, [n4])
                yield
                V(lambda: nc.vector.tensor_tensor(out=x_i[:], in0=n3[:], in1=n4[:], op=ALU.add), [n3, n4], [x_i])
                yield
                P.op("act", lambda: nc.scalar.copy(out=carry[:, s, 0:1], in_=x_r[:, CH - 1:CH]), reads=[x_r], writes=[(carry, s)])
                P.op("act", lambda: nc.scalar.copy(out=carry[:, s, 1:2], in_=x_i[:, CH - 1:CH]), reads=[x_i], writes=[(carry, s)])
                yield

            it = 0
            for ci in range(NCH):
                cs_ = slice(ci * CH, (ci + 1) * CH)
                uc = ucr.n()
                for tt in range(CH // 128):
                    t = ci * (CH // 128) + tt
                    zt = zin.n()
                    P.dma("sp", zt[:], z[t * 128:(t + 1) * 128, O_SU + half * 512:O_SU + (half + 1) * 512], reads=[("z", t)], writes=[zt])
                    pt = psT.n()
                    P.pe_group([(lambda kk=kk: nc.tensor.transpose(out=pt[:, kk * 128:(kk + 1) * 128], in_=zt[:, kk * 128:(kk + 1) * 128],
                                                                   identity=C["ident"][:])) for kk in range(4)], reads=[zt, C["ident"]], writes=[pt])
                    P.op("act", lambda: nc.scalar.copy(out=uc[:, :, tt * 128:(tt + 1) * 128], in_=pt[:].rearrange("p (a b) -> p a b", a=4)),
                         reads=[pt], writes=[uc])
                for kb in range(4):
                    xs = xsets[it % 2]; it += 1
                    gens = [st_gen(kb * 4 + sl, kb, cs_, uc, tmps[sl], xs[sl][0], xs[sl][1]) for sl in range(4)]
                    while gens:
                        for g_ in list(gens):
                            try:
                                next(g_)
                            except StopIteration:
                                gens.remove(g_)
                    yp = psY.n()
                    fns = []
                    for j in range(4):
                        s_ = kb * 4 + j
                        fns.append(lambda s_=s_, j=j: nc.tensor.matmul(yp[:, 0:CH], lhsT=Cre[:, s_, :], rhs=xs[j][0][:], start=(j == 0), stop=False))
                        fns.append(lambda s_=s_, j=j: nc.tensor.matmul(yp[:, 0:CH], lhsT=Cim[:, s_, :], rhs=xs[j][1][:], start=False, stop=(j == 3)))
                    P.pe_group(fns, reads=[Cre, Cim] + [xs[j][0] for j in range(4)] + [xs[j][1] for j in range(4)], writes=[yp])
                    yv = yo.n(); t1 = y1.n(); t2 = y2.n()
                    V(lambda: nc.vector.scalar_tensor_tensor(out=yv[:], in0=uc[:, kb, :], scalar=dcol[:, kb:kb + 1], in1=yp[:, 0:CH], op0=ALU.mult, op1=ALU.add),
                      [uc, dcol, yp], [yv])
                    V(lambda: nc.vector.tensor_tensor(out=t1[:], in0=yv[:], in1=yv[:], op=ALU.mult), [yv], [t1])
                    V(lambda: nc.vector.tensor_scalar(out=t1[:], in0=t1[:], scalar1=0.044715, scalar2=1.0, op0=ALU.mult, op1=ALU.add), [t1], [t1])
                    Pl(lambda: nc.gpsimd.tensor_tensor(out=t1[:], in0=t1[:], in1=yv[:], op=ALU.mult), [t1, yv], [t1])
                    P.op("act", lambda: nc.scalar.activation(out=t1[:], in_=t1[:], func=AF.Sigmoid, scale=1.5957691216057308), reads=[t1], writes=[t1])
                    Pl(lambda: nc.gpsimd.tensor_tensor(out=t2[:], in0=t1[:], in1=yv[:], op=ALU.mult), [t1, yv], [t2])
                    r0 = (half * 4 + kb) * 128
                    P.dma("sp", ysT[r0:r0 + 128, cs_], t2[:], reads=[t2], writes=[("ysT", half * 4 + kb)])


def load_T_to_F(P, nc, C, z, col0, ncols, dst_fn, psT, zin, eng_flip=[0]):
    for t in range(SEQ // 128):
        zt = zin.n()
        P.dma("sp", zt[:, 0:ncols], z[t * 128:(t + 1) * 128, col0:col0 + ncols], reads=[("z", t)], writes=[zt])
        pt = psT.n()
        P.pe_group([lambda: nc.tensor.transpose(out=pt[0:ncols, 0:128], in_=zt[:, 0:ncols], identity=C["ident"][:])],
                   reads=[zt, C["ident"]], writes=[pt])
        d, wk = dst_fn(t)
        eng_flip[0] ^= 1
        if eng_flip[0]:
            P.op("act", lambda: nc.scalar.copy(out=d, in_=pt[0:ncols, 0:128]), reads=[pt], writes=[wk])
        else:
            P.op("dve", lambda: nc.vector.tensor_copy(out=d, in_=pt[0:ncols, 0:128]), reads=[pt], writes=[wk])


def gla_stage(P, nc, C, z, ob, par):
    H, DK, DV = 4, 128, 256
    NP = SEQ // 128
    with P.scope():
        maskT = P.sb([128, 128]); rmask = P.sb([128, SEQ]); gw = P.sb([128, DV]); wup = P.sb([16, 512]); bcol = P.sb([128, 4])
        lowT = P.sb([16, SEQ])
        P.dma("sp", maskT[:], par["maskT01"], writes=[maskT])
        P.dma("sp", rmask[:], par["rmask"], writes=[rmask])
        P.dma("sp", gw[:], par["gla_norm_w"].partition_broadcast(128), writes=[gw])
        P.dma("sp", wup[:], par["gla_w_up"], writes=[wup])
        P.dma("sp", bcol[:], par["gla_b_col"], writes=[bcol])
        P.op("dve", lambda: nc.vector.tensor_scalar(out=bcol[:], in0=bcol[:], scalar1=-1.0, scalar2=None, op0=ALU.mult), reads=[bcol], writes=[bcol])
        zin = rot_sb(P, 3, [128, 128]); psT = rot_ps(P, 2, [128, 512])
        load_T_to_F(P, nc, C, z, O_GLOW, 16, lambda t: (lowT[:, t * 128:(t + 1) * 128], lowT), psT, zin)
        la = P.sb([128, SEQ]); eb = P.sb([128, SEQ])
        QT = [P.sb([128, SEQ]) for _ in range(2)]; KT = [P.sb([128, SEQ]) for _ in range(2)]; AL = [P.sb([128, SEQ // 64]) for _ in range(2)]
        PS = rot_ps(P, 6, [128, 512])
        ex = rot_sb(P, 2, [128, 512]); sq = P.sb([128, DV])
        pools = [dict(aq=rot_sb(P, 2, [128, 128]), kt=rot_sb(P, 2, [128, 128]), vt=rot_sb(P, 2, [128, DV]), rt=rot_sb(P, 2, [128, DV]),
                      osb=rot_sb(P, 2, [128, DV]), S=rot_sb(P, 3, [128, DV]), ts=rot_sb(P, 2, [128, DV]), ss=rot_sb(P, 2, [128, 1]), rs=rot_sb(P, 2, [128, 1]))
                 for _ in range(2)]

        def prep(h, qT, kT, al):
            load_T_to_F(P, nc, C, z, O_GQ + h * DK, DK, lambda t: (qT[:, t * 128:(t + 1) * 128], qT), psT, zin)
            load_T_to_F(P, nc, C, z, O_GK + h * DK, DK, lambda t: (kT[:, t * 128:(t + 1) * 128], kT), psT, zin)
            for c in range(SEQ // 512):
                cs_ = slice(c * 512, (c + 1) * 512)
                pl = PS.n(); e1 = ex.n()
                P.pe_group([lambda: nc.tensor.matmul(pl[:], lhsT=wup[:, h * DK:(h + 1) * DK], rhs=lowT[:, cs_], start=True, stop=True)],
                           reads=[wup, lowT], writes=[pl])
                P.op("act", lambda: nc.scalar.activation(out=e1[:], in_=pl[:], func=AF.Exp, bias=bcol[:, h:h + 1], scale=-1.0), reads=[pl, bcol], writes=[e1])
                P.op("act", lambda: nc.scalar.activation(out=la[:, cs_], in_=e1[:], func=AF.Ln, bias=1.0), reads=[e1], writes=[la])
            P.op("dve", lambda: nc.vector.tensor_tensor_scan(out=la[:], data0=rmask[:], data1=la[:], initial=0.0, op0=ALU.mult, op1=ALU.add),
                 reads=[rmask, la], writes=[la])
            P.op("act", lambda: nc.scalar.activation(out=eb[:], in_=la[:], func=AF.Exp, scale=-1.0 / 16.0), reads=[la], writes=[eb])
            P.op("dve", lambda: nc.vector.scalar_tensor_tensor(out=qT[:], in0=qT[:], scalar=float(DK) ** -0.5, in1=eb[:], op0=ALU.mult, op1=ALU.mult),
                 reads=[qT, eb], writes=[qT])
            P.op("pool", lambda: nc.gpsimd.tensor_copy(out=al[:], in_=eb[:].rearrange("p (c i) -> p c i", i=64)[:, :, 63]), reads=[eb], writes=[al])
            P.op("act", lambda: nc.scalar.activation(out=la[:], in_=la[:], func=AF.Exp, scale=1.0 / 16.0), reads=[la], writes=[la])
            P.op("pool", lambda: nc.gpsimd.tensor_tensor(out=kT[:], in0=kT[:], in1=la[:], op=ALU.mult), reads=[kT, la], writes=[kT])

        def pairs(h, qT, kT, al, pl_):
            S = pl_["S"].n()
            P.op("dve", lambda: nc.vector.memset(S[:], 0.0), writes=[S])
            yield
            for pr in range(NP):
                ps_ = slice(pr * 128, (pr + 1) * 128)
                pa = PS.n(); a_sb = pl_["aq"].n()
                P.pe_group([lambda: nc.tensor.matmul(pa[:, 0:128], lhsT=kT[:, ps_], rhs=qT[:, ps_], start=True, stop=True)], reads=[kT, qT], writes=[pa])
                P.op("dve", lambda: nc.vector.tensor_tensor(out=a_sb[:], in0=pa[:, 0:128], in1=maskT[:], op=ALU.mult), reads=[pa, maskT], writes=[a_sb])
                yield
                pk = PS.n(); k_sb = pl_["kt"].n()
                P.pe_group([lambda: nc.tensor.transpose(out=pk[:, 0:128], in_=kT[:, ps_], identity=C["ident"][:])], reads=[kT, C["ident"]], writes=[pk])
                P.op("act", lambda: nc.scalar.copy(out=k_sb[:], in_=pk[:, 0:128]), reads=[pk], writes=[k_sb])
                v = pl_["vt"].n(); r = pl_["rt"].n(); o_sb = pl_["osb"].n()
                P.dma("sp", v[:], z[ps_, O_GV + h * DV:O_GV + (h + 1) * DV], reads=[("z", pr)], writes=[v])
                P.dma("sp", r[:], z[ps_, O_GR + h * DV:O_GR + (h + 1) * DV], reads=[("z", pr)], writes=[r])
                yield
                for cc in range(2):
                    rr = slice(cc * 64, (cc + 1) * 64)
                    c_ = pr * 2 + cc
                    pkv = PS.n()
                    P.pe_group([lambda: nc.tensor.matmul(pkv[:, 0:DV], lhsT=k_sb[rr, :], rhs=v[rr, :], start=True, stop=True)], reads=[k_sb, v], writes=[pkv])
                    ts_ = pl_["ts"].n(); S2 = pl_["S"].n()
                    P.op("dve", lambda: nc.vector.tensor_tensor(out=ts_[:], in0=pkv[:, 0:DV], in1=S[:], op=ALU.add), reads=[pkv, S], writes=[ts_])
                    yield
                    P.op("dve", lambda: nc.vector.tensor_scalar(out=S2[:], in0=ts_[:], scalar1=al[:, c_:c_ + 1], scalar2=None, op0=ALU.mult),
                         reads=[ts_, al], writes=[S2])
                    yield
                    po = PS.n()
                    P.pe_group([lambda: nc.tensor.matmul(po[:, 0:DV], lhsT=qT[:, ps_], rhs=S[:], start=True, stop=False),
                                lambda: nc.tensor.matmul(po[:, 0:DV], lhsT=a_sb[rr, :], rhs=v[rr, :], start=False, stop=True)],
                               reads=[qT, S, a_sb, v], writes=[po])
                    P.op("act", lambda: nc.scalar.copy(out=o_sb[rr, :], in_=po[rr, 0:DV]), reads=[po], writes=[o_sb])
                    yield
                    S = S2
                ss = pl_["ss"].n(); rs = pl_["rs"].n()
                rms_scale(P, nc, o_sb[:], DV, sq, ss, rs, [o_sb])
                yield
                P.op("dve", lambda: nc.vector.scalar_tensor_tensor(out=o_sb[:], in0=o_sb[:], scalar=rs[:, 0:1], in1=gw[:], op0=ALU.mult, op1=ALU.mult),
                     reads=[o_sb, rs, gw], writes=[o_sb])
                P.op("act", lambda: nc.scalar.activation(out=r[:], in_=r[:], func=AF.Silu), reads=[r], writes=[r])
                yield
                P.op("pool", lambda: nc.gpsimd.tensor_tensor(out=o_sb[:], in0=o_sb[:], in1=r[:], op=ALU.mult), reads=[o_sb, r], writes=[o_sb])
                P.dma("sp", ob[ps_, h * DV:(h + 1) * DV], o_sb[:], reads=[o_sb], writes=[("ob", pr)])
                yield

        for hp in range(H // 2):
            for k in range(2):
                prep(2 * hp + k, QT[k], KT[k], AL[k])
            gens = [pairs(2 * hp + k, QT[k], KT[k], AL[k], pools[k]) for k in range(2)]
            while gens:
                for g_ in list(gens):
                    try:
                        next(g_)
                    except StopIteration:
                        gens.remove(g_)


def dn_stage(P, nc, C, z, oa, par, stop=0, nheads=8, npairs=None):
    H, DK = 8, 128
    NP = SEQ // 128
    V = lambda fn, r, w: P.op("dve", fn, reads=r, writes=w)
    A = lambda fn, r, w: P.op("act", fn, reads=r, writes=w)
    G = lambda fn, r, w: P.op("pool", fn, reads=r, writes=w)
    with P.scope():
        maskbig = P.sb([128, 128]); strict = P.sb([128, 128]); nw = P.sb([128, DK]); ones = P.sb([128, 128])
        dpar = P.sb([8, 2]); cwt = P.sb([128, 3 * H, 4])
        P.dma("sp", maskbig[:], par["maskbig"], writes=[maskbig])
        P.dma("sp", strict[:], par["strict01"], writes=[strict])
        P.dma("sp", nw[:], par["dn_norm_w"].partition_broadcast(128), writes=[nw])
        P.dma("sp", dpar[:], par["dn_par"], writes=[dpar])
        P.dma("sp", cwt[:], par["dn_conv_w"].rearrange("p (m j) -> p m j", j=4), writes=[cwt])
        V(lambda: nc.vector.memset(ones[:], 1.0), [], [ones])
        Rg = P.sb([8, SEQ])
        COLb = P.sb([128, NP, 8]); COLg = P.sb([128, NP, 8]); COLl = P.sb([128, NP, 8])
        nbeta = P.sb([128, NP, H]); egc = P.sb([128, NP, H]); bE = P.sb([128, NP, H]); dkc = P.sb([128, NP, H])
        with P.scope():
            Rb = P.sb([8, SEQ]); Rl = P.sb([8, SEQ]); rmask = P.sb([8, SEQ])
            P.dma("sp", rmask[:], par["rmask"][0:8, :], writes=[rmask])
            zt_r = rot_sb(P, 2, [128, 16]); psT = rot_ps(P, 2, [128, 512])
            for t in range(NP):
                zt = zt_r.n()
                P.dma("sp", zt[:], z[t * 128:(t + 1) * 128, O_DB:O_DB + 16], reads=[("z", t)], writes=[zt])
                pt = psT.n()
                P.pe_group([lambda: nc.tensor.transpose(out=pt[0:8, 0:128], in_=zt[:, 0:8], identity=C["ident"][:]),
                            lambda: nc.tensor.transpose(out=pt[0:8, 128:256], in_=zt[:, 8:16], identity=C["ident"][:])],
                           reads=[zt, C["ident"]], writes=[pt])
                V(lambda: nc.vector.tensor_copy(out=Rb[:, t * 128:(t + 1) * 128], in_=pt[0:8, 0:128]), [pt], [Rb])
                V(lambda: nc.vector.tensor_copy(out=Rg[:, t * 128:(t + 1) * 128], in_=pt[0:8, 128:256]), [pt], [Rg])
            A(lambda: nc.scalar.activation(out=Rb[:], in_=Rb[:], func=AF.Sigmoid), [Rb], [Rb])
            A(lambda: nc.scalar.activation(out=Rg[:], in_=Rg[:], func=AF.Exp, bias=dpar[:, 0:1], scale=1.0), [Rg, dpar], [Rg])
            A(lambda: nc.scalar.activation(out=Rg[:], in_=Rg[:], func=AF.Ln, bias=1.0), [Rg], [Rg])
            A(lambda: nc.scalar.activation(out=dpar[:, 1:2], in_=dpar[:, 1:2], func=AF.Exp), [dpar], [dpar])
            V(lambda: nc.vector.tensor_scalar(out=dpar[:, 1:2], in0=dpar[:, 1:2], scalar1=-1.0, scalar2=None, op0=ALU.mult), [dpar], [dpar])
            V(lambda: nc.vector.tensor_scalar(out=Rg[:], in0=Rg[:], scalar1=dpar[:, 1:2], scalar2=None, op0=ALU.mult), [Rg, dpar], [Rg])
            V(lambda: nc.vector.tensor_tensor_scan(out=Rg[:], data0=rmask[0:8, :], data1=Rg[:], initial=0.0, op0=ALU.mult, op1=ALU.add),
              [Rg, rmask], [Rg])
            V(lambda: nc.vector.tensor_copy(out=Rl[:].rearrange("p (c i) -> p c i", i=64),
                                            in_=Rg[:].rearrange("p (c i) -> p c i", i=64)[:, :, 63:64].to_broadcast([8, SEQ // 64, 64])),
              [Rg], [Rl])
            for t in range(NP):
                for (src, dst) in ((Rb, COLb), (Rg, COLg), (Rl, COLl)):
                    pt = psT.n()
                    P.pe_group([lambda: nc.tensor.transpose(out=pt[:, 0:8], in_=src[:, t * 128:(t + 1) * 128], identity=C["ident"][0:8, 0:8])],
                               reads=[src, C["ident"]], writes=[pt])
                    A(lambda: nc.scalar.copy(out=dst[:, t, :], in_=pt[:, 0:8]), [pt], [dst])
            V(lambda: nc.vector.tensor_scalar(out=nbeta[:], in0=COLb[:], scalar1=-1.0, scalar2=None, op0=ALU.mult), [COLb], [nbeta])
            A(lambda: nc.scalar.activation(out=egc[:], in_=COLg[:], func=AF.Exp), [COLg], [egc])
            V(lambda: nc.vector.tensor_tensor(out=bE[:], in0=egc[:], in1=COLb[:], op=ALU.mult), [egc, COLb], [bE])
            V(lambda: nc.vector.tensor_tensor(out=dkc[:], in0=COLl[:], in1=COLg[:], op=ALU.subtract), [COLl, COLg], [dkc])
            A(lambda: nc.scalar.activation(out=dkc[:], in_=dkc[:], func=AF.Exp), [dkc], [dkc])
        if stop == 1:
            return
        xin = P.sb([128, 3 + SEQ]); qT = P.sb([128, SEQ]); kT = P.sb([128, SEQ]); vT = P.sb([128, SEQ])
        gcb = P.sb([128, SEQ]); egb = P.sb([128, SEQ]); sel = P.sb([8, 128])
        V(lambda: nc.vector.memset(xin[:, 0:3], 0.0), [], [xin])
        zin = rot_sb(P, 3, [128, 128])
        PS = rot_ps(P, 8, [128, 512]); psA = PS; psB = PS
        GP = 4
        temps = [rot_sb(P, 12, [128, 128]) for _ in range(GP)]
        persist = [[{n: P.sb([128, 128]) for n in ("AqT", "QdT", "WT", "U", "Kdc", "AT0", "AT1", "B0", "B1")} for _ in range(GP)] for _ in range(2)]
        sm1 = rot_sb(P, 8, [128, 1]); vnr = rot_sb(P, 2, [128, 128])
        Sr = rot_sb(P, 4, [128, 128]); osr = rot_sb(P, 2, [128, 128]); ztr = rot_sb(P, 2, [128, 128]); sq = P.sb([128, 128])
        for h in range(nheads):
            P.dma("sp", sel[:], par["dn_sel"][h], writes=[sel])
            for wi_, dstT in enumerate((qT, kT, vT)):
                m = wi_ * H + h
                load_T_to_F(P, nc, C, z, (O_DQ, O_DK, O_DV)[wi_] + h * DK, DK, lambda t: (xin[:, 3 + t * 128:3 + (t + 1) * 128], xin), psA, zin)
                V(lambda: nc.vector.tensor_scalar(out=dstT[:], in0=xin[:, 0:SEQ], scalar1=cwt[:, m, 0:1], scalar2=None, op0=ALU.mult), [xin, cwt], [dstT])
                for j in range(1, 4):
                    V(lambda: nc.vector.scalar_tensor_tensor(out=dstT[:], in0=xin[:, j:j + SEQ], scalar=cwt[:, m, j:j + 1], in1=dstT[:],
                                                              op0=ALU.mult, op1=ALU.add), [xin, cwt, dstT], [dstT])
                A(lambda: nc.scalar.activation(out=dstT[:], in_=dstT[:], func=AF.Silu), [dstT], [dstT])
                if wi_ < 2:
                    for c in range(SEQ // 512):
                        cs_ = slice(c * 512, (c + 1) * 512)
                        G(lambda: nc.gpsimd.tensor_tensor(out=xin[:, cs_], in0=dstT[:, cs_], in1=dstT[:, cs_], op=ALU.mult), [dstT, xin], [xin])
                        pn = psB.n()
                        P.pe_group([lambda: nc.tensor.matmul(pn[:], lhsT=ones[:], rhs=xin[:, cs_], start=True, stop=True)], reads=[ones, xin], writes=[pn])
                        V(lambda: nc.vector.tensor_scalar(out=xin[:, cs_], in0=pn[:], scalar1=EPS, scalar2=None, op0=ALU.add), [pn, xin], [xin])
                        A(lambda: nc.scalar.activation(out=xin[:, cs_], in_=xin[:, cs_], func=AF.Sqrt), [xin], [xin])
                        V(lambda: nc.vector.reciprocal(out=xin[:, cs_], in_=xin[:, cs_]), [xin], [xin])
                        scl = float(DK) ** -0.5 if wi_ == 0 else 1.0
                        V(lambda: nc.vector.scalar_tensor_tensor(out=dstT[:, cs_], in0=dstT[:, cs_], scalar=scl, in1=xin[:, cs_], op0=ALU.mult, op1=ALU.mult),
                          [dstT, xin], [dstT])
                    V(lambda: nc.vector.memset(xin[:, 0:3], 0.0), [xin], [xin])
            for c in range(SEQ // 512):
                cs_ = slice(c * 512, (c + 1) * 512)
                pn = psB.n()
                P.pe_group([lambda: nc.tensor.matmul(pn[:], lhsT=sel[:], rhs=Rg[:, cs_], start=True, stop=True)], reads=[sel, Rg], writes=[pn])
                V(lambda: nc.vector.tensor_copy(out=gcb[:, cs_], in_=pn[:]), [pn], [gcb])
            A(lambda: nc.scalar.activation(out=egb[:], in_=gcb[:], func=AF.Exp), [gcb], [egb])
            Sh = [Sr.n()]
            V(lambda: nc.vector.memset(Sh[0][:], 0.0), [], [Sh[0]])
            if stop == 2:
                continue
            NPR = NP if npairs is None else npairs

            def phaseA(pr, tp, pp):
                ps_ = slice(pr * 128, (pr + 1) * 128)
                dd = tp.n(); E = tp.n()
                V(lambda: nc.vector.scalar_tensor_tensor(out=dd[:], in0=gcb[:, ps_], scalar=COLg[:, pr, h:h + 1], in1=maskbig[:],
                                                          op0=ALU.subtract, op1=ALU.max), [gcb, COLg, maskbig], [dd])
                yield
                A(lambda: nc.scalar.activation(out=E[:], in_=dd[:], func=AF.Exp, scale=-1.0), [dd], [E])
                yield
                Ln = tp.n(); X = tp.n(); Aqk = tp.n()
                pkk = PS.n()
                P.pe_group([lambda: nc.tensor.matmul(pkk[:, 0:128], lhsT=kT[:, ps_], rhs=kT[:, ps_], start=True, stop=True)], reads=[kT], writes=[pkk])
                V(lambda: nc.vector.scalar_tensor_tensor(out=Ln[:], in0=pkk[:, 0:128], scalar=nbeta[:, pr, h:h + 1], in1=E[:], op0=ALU.mult, op1=ALU.mult),
                  [pkk, nbeta, E], [Ln])
                yield
                G(lambda: nc.gpsimd.tensor_tensor(out=Ln[:], in0=Ln[:], in1=strict[:], op=ALU.mult), [Ln, strict], [Ln])
                yield
                px = PS.n()
                P.pe_group([lambda: nc.tensor.transpose(out=px[:, 0:128], in_=Ln[:], identity=C["ident"][:])], reads=[Ln, C["ident"]], writes=[px])
                A(lambda: nc.scalar.copy(out=X[:], in_=px[:, 0:128]), [px], [X])
                yield
                pqk = PS.n()
                P.pe_group([lambda: nc.tensor.matmul(pqk[:, 0:128], lhsT=qT[:, ps_], rhs=kT[:, ps_], start=True, stop=True)], reads=[qT, kT], writes=[pqk])
                V(lambda: nc.vector.tensor_tensor(out=Aqk[:], in0=pqk[:, 0:128], in1=E[:], op=ALU.mult), [pqk, E], [Aqk])
                yield
                pa = PS.n()
                P.pe_group([lambda: nc.tensor.transpose(out=pa[:, 0:128], in_=Aqk[:], identity=C["ident"][:])], reads=[Aqk, C["ident"]], writes=[pa])
                A(lambda: nc.scalar.copy(out=pp["AqT"][:], in_=pa[:, 0:128]), [pa], [pp["AqT"]])
                yield
                G(lambda: nc.gpsimd.tensor_tensor(out=pp["QdT"][:], in0=qT[:, ps_], in1=egb[:, ps_], op=ALU.mult), [qT, egb], [pp["QdT"]])
                yield
                Pk, PkT = X, Ln
                Rm = tp.n()
                V(lambda: nc.vector.tensor_tensor(out=Rm[:], in0=X[:], in1=C["ident"][:], op=ALU.add), [X, C["ident"]], [Rm])
                yield
                for lev in range(5):
                    last = (lev == 4)
                    pT2 = PS.n()
                    P.pe_group([lambda: nc.tensor.matmul(pT2[:, 0:128], lhsT=Pk[:], rhs=PkT[:], start=True, stop=True)], reads=[Pk, PkT], writes=[pT2])
                    nPT = tp.n()
                    A(lambda: nc.scalar.copy(out=nPT[:], in_=pT2[:, 0:128]), [pT2], [nPT])
                    yield
                    if not last:
                        p2 = PS.n()
                        P.pe_group([lambda: nc.tensor.matmul(p2[:, 0:128], lhsT=PkT[:], rhs=Pk[:], start=True, stop=True)], reads=[Pk, PkT], writes=[p2])
                        nP = tp.n()
                        V(lambda: nc.vector.tensor_copy(out=nP[:], in_=p2[:, 0:128]), [p2], [nP])
                        yield
                    pr_ = PS.n()
                    P.pe_group([lambda: nc.tensor.matmul(pr_[:, 0:128], lhsT=nPT[:], rhs=Rm[:], start=True, stop=True)], reads=[nPT, Rm], writes=[pr_])
                    nR = tp.n()
                    V(lambda: nc.vector.tensor_tensor(out=nR[:], in0=pr_[:, 0:128], in1=Rm[:], op=ALU.add), [pr_, Rm], [nR])
                    yield
                    Rm = nR
                    PkT = nPT
                    if not last:
                        Pk = nP
                KbE = tp.n(); bV = tp.n()
                pk1 = PS.n()
                P.pe_group([lambda: nc.tensor.transpose(out=pk1[:, 0:128], in_=kT[:, ps_], identity=C["ident"][:])], reads=[kT, C["ident"]], writes=[pk1])
                V(lambda: nc.vector.tensor_scalar(out=KbE[:], in0=pk1[:, 0:128], scalar1=bE[:, pr, h:h + 1], scalar2=None, op0=ALU.mult), [pk1, bE], [KbE])
                V(lambda: nc.vector.tensor_scalar(out=pp["Kdc"][:], in0=pk1[:, 0:128], scalar1=dkc[:, pr, h:h + 1], scalar2=None, op0=ALU.mult),
                  [pk1, dkc], [pp["Kdc"]])
                yield
                pk2 = PS.n()
                P.pe_group([lambda: nc.tensor.transpose(out=pk2[:, 0:128], in_=vT[:, ps_], identity=C["ident"][:])], reads=[vT, C["ident"]], writes=[pk2])
                V(lambda: nc.vector.tensor_scalar(out=bV[:], in0=pk2[:, 0:128], scalar1=COLb[:, pr, h:h + 1], scalar2=None, op0=ALU.mult), [pk2, COLb], [bV])
                yield
                pw1 = PS.n()
                P.pe_group([lambda: nc.tensor.matmul(pw1[:, 0:128], lhsT=KbE[:], rhs=Rm[:], start=True, stop=True)], reads=[KbE, Rm], writes=[pw1])
                A(lambda: nc.scalar.copy(out=pp["WT"][:], in_=pw1[:, 0:128]), [pw1], [pp["WT"]])
                yield
                pw2 = PS.n()
                P.pe_group([lambda: nc.tensor.matmul(pw2[:, 0:128], lhsT=Rm[:], rhs=bV[:], start=True, stop=True)], reads=[Rm, bV], writes=[pw2])
                V(lambda: nc.vector.tensor_copy(out=pp["U"][:], in_=pw2[:, 0:128]), [pw2], [pp["U"]])
                yield
                Wt = tp.n()
                pw3 = PS.n()
                P.pe_group([lambda: nc.tensor.matmul(pw3[:, 0:128], lhsT=Rm[:], rhs=KbE[:], start=True, stop=True)], reads=[Rm, KbE], writes=[pw3])
                A(lambda: nc.scalar.copy(out=Wt[:], in_=pw3[:, 0:128]), [pw3], [Wt])
                yield
                for cc in range(2):
                    rr = slice(cc * 64, (cc + 1) * 64)
                    tl = pr * 128 + cc * 64 + 63
                    pm = PS.n()
                    P.pe_group([lambda: nc.tensor.matmul(pm[:, 0:128], lhsT=Wt[rr, :], rhs=pp["Kdc"][rr, :], start=True, stop=True)],
                               reads=[Wt, pp["Kdc"]], writes=[pm])
                    V(lambda: nc.vector.scalar_tensor_tensor(out=pp["AT%d" % cc][:], in0=C["ident"][:], scalar=egb[:, tl:tl + 1], in1=pm[:, 0:128],
                                                              op0=ALU.mult, op1=ALU.subtract), [C["ident"], egb, pm], [pp["AT%d" % cc]])
                    yield
                    pb = PS.n()
                    P.pe_group([lambda: nc.tensor.matmul(pb[:, 0:128], lhsT=pp["Kdc"][rr, :], rhs=pp["U"][rr, :], start=True, stop=True)],
                               reads=[pp["Kdc"], pp["U"]], writes=[pb])
                    A(lambda: nc.scalar.copy(out=pp["B%d" % cc][:], in_=pb[:, 0:128]), [pb], [pp["B%d" % cc]])
                    yield

            def recur(prs, pps):
                for pr, pp in zip(prs, pps):
                    ps_ = slice(pr * 128, (pr + 1) * 128)
                    WT, U, AqT, QdT = pp["WT"], pp["U"], pp["AqT"], pp["QdT"]
                    o_sb = osr.n(); vn = vnr.n()
                    for cc in range(2):
                        rr = slice(cc * 64, (cc + 1) * 64)
                        S = Sh[0]
                        pas = PS.n()
                        P.pe_group([lambda: nc.tensor.matmul(pas[:, 0:128], lhsT=pp["AT%d" % cc][:], rhs=S[:], start=True, stop=True)],
                                   reads=[pp["AT%d" % cc], S], writes=[pas])
                        S2 = Sr.n()
                        V(lambda: nc.vector.tensor_tensor(out=S2[:], in0=pas[:, 0:128], in1=pp["B%d" % cc][:], op=ALU.add), [pas, pp["B%d" % cc]], [S2])
                        yield
                        pws = PS.n()
                        P.pe_group([lambda: nc.tensor.matmul(pws[:, 0:128], lhsT=WT[:], rhs=S[:], start=True, stop=True)], reads=[WT, S], writes=[pws])
                        V(lambda: nc.vector.tensor_tensor(out=vn[rr, :], in0=U[rr, :], in1=pws[rr, 0:128], op=ALU.subtract), [U, pws], [vn])
                        yield
                        po = PS.n()
                        P.pe_group([lambda: nc.tensor.matmul(po[:, 0:128], lhsT=QdT[:], rhs=S[:], start=True, stop=False),
                                    lambda: nc.tensor.matmul(po[:, 0:128], lhsT=AqT[rr, :], rhs=vn[rr, :], start=False, stop=True)],
                                   reads=[QdT, S, AqT, vn], writes=[po])
                        A(lambda: nc.scalar.copy(out=o_sb[rr, :], in_=po[rr, 0:128]), [po], [o_sb])
                        yield
                        Sh[0] = S2
                    zt = ztr.n(); ss = sm1.n(); rs = sm1.n()
                    P.dma("sp", zt[:], z[ps_, O_DZ + h * DK:O_DZ + (h + 1) * DK], reads=[("z", pr)], writes=[zt])
                    rms_scale(P, nc, o_sb[:], DK, sq, ss, rs, [o_sb])
                    yield
                    V(lambda: nc.vector.scalar_tensor_tensor(out=o_sb[:], in0=o_sb[:], scalar=rs[:, 0:1], in1=nw[:], op0=ALU.mult, op1=ALU.mult),
                      [o_sb, rs, nw], [o_sb])
                    A(lambda: nc.scalar.activation(out=zt[:], in_=zt[:], func=AF.Silu), [zt], [zt])
                    yield
                    G(lambda: nc.gpsimd.tensor_tensor(out=o_sb[:], in0=o_sb[:], in1=zt[:], op=ALU.mult), [o_sb, zt], [o_sb])
                    P.dma("sp", oa[ps_, h * DK:(h + 1) * DK], o_sb[:], reads=[o_sb], writes=[("oa", pr)])
                    yield

            groups = [list(range(g0, min(g0 + GP, NPR))) for g0 in range(0, NPR, GP)]
            prev = None
            for gi, grp in enumerate(groups + [None]):
                gens = []
                if grp is not None:
                    pps = [persist[gi % 2][k] for k in range(len(grp))]
                    gens += [phaseA(pr, temps[k], pps[k]) for k, pr in enumerate(grp)]
                if prev is not None:
                    gens.append(prev)
                while gens:
                    for g_ in list(gens):
                        try:
                            next(g_)
                        except StopIteration:
                            gens.remove(g_)
                prev = recur(grp, pps) if grp is not None else None


def make_consts():
    c = {}
    c["ident"] = np.eye(128, dtype=np.float32)
    i = np.arange(128)[:, None]; j = np.arange(128)[None, :]
    same = (i // 64) == (j // 64)
    c["maskbig"] = np.where(same & (i >= j), 0.0, 1e30).astype(np.float32)
    c["strict01"] = (same & (i > j)).astype(np.float32)
    c["maskT01"] = (same & (i <= j)).astype(np.float32)
    r = np.ones((128, SEQ), np.float32); r[:, ::64] = 0.0
    c["rmask"] = r
    c["tau"] = np.tile(np.arange(1, 257, dtype=np.float32)[None, :], (128, 1))
    sel = np.zeros((8, 8, 128), np.float32)
    for h in range(8):
        sel[h, h, :] = 1.0
    c["dn_sel"] = sel
    return c


def prep_layer_params(inp, l):
    f = lambda a: np.ascontiguousarray(a, dtype=np.float32)
    p = {}
    p["dn_norm_w"] = f(inp["dn_norm_w"][l][None, :])
    dpar = np.zeros((8, 2), np.float32)
    dpar[:, 0] = inp["dn_dt_bias"][l]
    dpar[:, 1] = inp["dn_a_log"][l]
    p["dn_par"] = dpar
    p["dn_conv_w"] = f(inp["dn_conv_w"][l].reshape(3, 8, 128, 4).transpose(2, 0, 1, 3).reshape(128, 96))
    p["gla_norm_w"] = f(inp["gla_norm_w"][l][None, :])
    p["gla_w_up"] = f(inp["gla_w_up"][l])
    p["gla_b_col"] = f(inp["gla_b_up"][l].reshape(4, 128).T)
    lay = lambda a: f(a.reshape(32, 2, 64).transpose(1, 2, 0).reshape(128, 32))
    p["s5_lre"] = lay(inp["s5_lam_re"][l]); p["s5_lim"] = lay(inp["s5_lam_im"][l])
    p["s5_lstep"] = lay(np.repeat(inp["s5_log_step"][l][:, None], 64, axis=1))
    p["s5_d"] = f(inp["s5_d"][l].reshape(8, 128).T)
    Bre = np.zeros((32, 128, 128), np.float32); Bim = np.zeros_like(Bre); Cre = np.zeros_like(Bre); Cim = np.zeros_like(Bre)
    for g in range(64):
        st = g // 2; p0 = (g % 2) * 64; k0 = (g % 8) * 16
        Bre[st, k0:k0 + 16, p0:p0 + 64] = inp["s5_b_re"][l][g].T
        Bim[st, k0:k0 + 16, p0:p0 + 64] = inp["s5_b_im"][l][g].T
        Cre[st, p0:p0 + 64, k0:k0 + 16] = inp["s5_c_re"][l][g].T
        Cim[st, p0:p0 + 64, k0:k0 + 16] = inp["s5_c_im"][l][g].T
    p["s5_Bre"] = Bre; p["s5_Bim"] = Bim; p["s5_Cre"] = Cre; p["s5_Cim"] = Cim
    return p


PAR_SHAPES = {
    "ident": [128, 128], "maskbig": [128, 128], "strict01": [128, 128], "maskT01": [128, 128], "rmask": [128, SEQ], "tau": [128, 256],
    "dn_sel": [8, 8, 128], "dn_norm_w": [1, 128], "dn_par": [8, 2], "dn_conv_w": [128, 96], "gla_norm_w": [1, 256], "gla_w_up": [16, 512],
    "gla_b_col": [128, 4], "s5_lre": [128, 32], "s5_lim": [128, 32], "s5_lstep": [128, 32], "s5_d": [128, 8],
    "s5_Bre": [32, 128, 128], "s5_Bim": [32, 128, 128], "s5_Cre": [32, 128, 128], "s5_Cim": [32, 128, 128],
}
CONST_NAMES = ["ident", "maskbig", "strict01", "maskT01", "rmask", "tau", "dn_sel"]


def linear_F(P, nc, srcT, skey, c0, NTOK, K, w, wc0, N, dst, dkey, dr0):
    KT = K // 128
    with P.scope():
        actT = P.sb([128, KT, NTOK], BF16)
        for kt in range(KT):
            P.dma("pool", actT[:, kt, :], srcT[kt * 128:(kt + 1) * 128, c0:c0 + NTOK], reads=[(skey, kt)],
                  writes=[(actT, t) for t in range(NTOK // 128)])
        gemm_T(P, nc, actT, NTOK, K, w, wc0, N, dst, dkey, dr0, 0)


WNAMES = {
    "w_in": [D_MODEL, IN_WIDTH], "w_dn_out": [1024, D_MODEL], "w_gla_out": [1024, D_MODEL], "w_s5_glu": [1024, 2 * D_MODEL],
    "w_mix_out": [D_MODEL, D_MODEL], "w_xa_q": [D_MODEL, 512], "w_xa_kv": [D_MODEL, 1024], "w_xa_out": [512, D_MODEL],
    "w_ffn_up": [D_MODEL, 2 * FFN_H], "w_ffn_down": [FFN_H, D_MODEL], "ffn_conv_w": [128, 132],
    "norm_mix_pre": [1, D_MODEL], "norm_mix_post": [1, D_MODEL], "norm_xa_pre": [1, D_MODEL], "norm_xa_post": [1, D_MODEL],
    "norm_mem": [1, D_MODEL], "norm_ffn_pre": [1, D_MODEL], "norm_ffn_post": [1, D_MODEL],
}
LAYER_PAR = [n for n in PAR_SHAPES if n not in CONST_NAMES]


def build_full(depth=DEPTH):
    nc = bass.Bass("TRN2", target_bir_lowering=False)
    P = Prog(nc)
    D = D_MODEL
    dr = lambda n, s, k="ExternalInput": P.dram(n, s, F32, k)
    x_in = dr("x", [SEQ, D]); mem = dr("mem", [N_MEM, D])
    cst = {n: dr(n, PAR_SHAPES[n]) for n in CONST_NAMES}
    W = [{n: dr("%s_l%d" % (n, l), s) for n, s in WNAMES.items()} for l in range(depth)]
    LP = [{n: dr("%s_l%d" % (n, l), PAR_SHAPES[n]) for n in LAYER_PAR} for l in range(depth)]
    out = dr("out", [SEQ, D], "ExternalOutput")
    I = "Internal"
    z = dr("z", [SEQ, IN_WIDTH], I); oa = dr("oa", [SEQ, 1024], I); ob = dr("ob", [SEQ, 1024], I); ysT = dr("ysT", [1024, SEQ], I)
    ya = dr("ya", [SEQ, D], I); yb = dr("yb", [SEQ, D], I); yg = dr("yg", [SEQ, 2 * D], I); merged = dr("merged", [SEQ, D], I)
    y1 = dr("y1", [SEQ, D], I); x1 = dr("x1", [SEQ, D], I); q = dr("q", [SEQ, 512], I); kv = dr("kv", [N_MEM, 1024], I)
    o = dr("o", [SEQ, 512], I); y2 = dr("y2", [SEQ, D], I); x2h = dr("x2h", [128 + SEQ, D], I); y3 = dr("y3", [SEQ, D], I)
    xmid = dr("xmid", [SEQ, D], I)
    C = load_consts(P, cst["ident"])
    with P.scope():
        zt = P.sb([128, D])
        P.op("dve", lambda: nc.vector.memset(zt[:], 0.0), writes=[zt])
        P.dma("sp", x2h[0:128, :], zt[:], reads=[zt], writes=[("x2h", 0)])
    NT = 2048
    xcur, xkey = x_in, "x"
    for l in range(depth):
        w = W[l]; par = dict(cst); par.update(LP[l])
        xnext, nkey = (out, "out") if l == depth - 1 else (xmid, "xmid")
        for r0 in range(0, SEQ, NT):
            linear_T(P, nc, C, xcur, xkey, r0, NT, D, w["w_in"], 0, IN_WIDTH, z, "z", dr0=r0, norm_g=w["norm_mix_pre"])
        dn_stage(P, nc, C, z, oa, par)
        gla_stage(P, nc, C, z, ob, par)
        s5_stage(P, nc, C, z, ysT, par)
        linear_T(P, nc, C, mem, "mem", 0, N_MEM, D, w["w_xa_kv"], 0, 1024, kv, "kv", norm_g=w["norm_mem"])
        for r0 in range(0, SEQ, NT):
            linear_T(P, nc, C, oa, "oa", r0, NT, 1024, w["w_dn_out"], 0, D, ya, "ya", dr0=r0)
            linear_T(P, nc, C, ob, "ob", r0, NT, 1024, w["w_gla_out"], 0, D, yb, "yb", dr0=r0)
            linear_F(P, nc, ysT, "ysT", r0, NT, 1024, w["w_s5_glu"], 0, 2 * D, yg, "yg", r0)
            merge_stage(P, nc, ya, yb, yg, z, "z", O_GATES, merged, NT, r0=r0)
            linear_T(P, nc, C, merged, "merged", r0, NT, D, w["w_mix_out"], 0, D, y1, "y1", dr0=r0)
            residual_stage(P, nc, xcur, xkey, y1, "y1", w["norm_mix_post"], x1, "x1", NT, xr0=r0, yr0=r0, dr0=r0)
            linear_T(P, nc, C, x1, "x1", r0, NT, D, w["w_xa_q"], 0, 512, q, "q", dr0=r0, norm_g=w["norm_xa_pre"])
            attn_stage(P, nc, C, q, kv, o, NT, r0=r0)
            linear_T(P, nc, C, o, "o", r0, NT, 512, w["w_xa_out"], 0, D, y2, "y2", dr0=r0)
            residual_stage(P, nc, x1, "x1", y2, "y2", w["norm_xa_post"], x2h, "x2h", NT, xr0=r0, yr0=r0, dr0=128 + r0)
        with P.scope():
            ffn_stage(P, nc, C, x2h, w["w_ffn_up"], w["w_ffn_down"], w["ffn_conv_w"], w["norm_ffn_pre"], y3)
        for r0 in range(0, SEQ, NT):
            residual_stage(P, nc, x2h, "x2h", y3, "y3", w["norm_ffn_post"], xnext, nkey, NT, xr0=128 + r0, yr0=r0, dr0=r0)
        xcur, xkey = xnext, nkey
    P.finish([("out", t) for t in range(SEQ // 128)])
    P.close()
    return nc, P


def host_inputs(inp, b, depth=DEPTH):
    f = lambda a: np.ascontiguousarray(a, dtype=np.float32)
    m = dict(make_consts())
    m["x"] = f(inp["x"][b]); m["mem"] = f(inp["mem"][b])
    for l in range(depth):
        for n in WNAMES:
            if n == "ffn_conv_w":
                a = inp[n][l].reshape(44, 128, 3).transpose(1, 0, 2).reshape(128, 132)
            elif n.startswith("norm_"):
                a = inp[n][l][None, :]
            else:
                a = inp[n][l]
            m["%s_l%d" % (n, l)] = f(a)
        for n, a in prep_layer_params(inp, l).items():
            m["%s_l%d" % (n, l)] = a
    return m


_FULL = {}


def kernel(**inputs):
    inp = {k: np.asarray(v) for k, v in inputs.items()}
    if "nc" not in _FULL:
        _FULL["nc"] = build_full()[0]
    nc = _FULL["nc"]
    in_maps = [host_inputs(inp, b) for b in range(BATCH)]
    res = run_bass_kernel_spmd(nc, in_maps, core_ids=list(range(BATCH)))
    out = np.stack([res.results[b]["out"] for b in range(BATCH)], axis=0)
    return out.astype(np.float32)
```

```python
import numpy as np
from contextlib import ExitStack
import concourse.bass as bass
import concourse.mybir as mybir
from concourse.bass_utils import run_bass_kernel_spmd

F32 = mybir.dt.float32
BF16 = mybir.dt.bfloat16
I32 = mybir.dt.int32
ALU = mybir.AluOpType
AF = mybir.ActivationFunctionType
AX = mybir.AxisListType


class Prog:
    NDMA = 40

    def __init__(self, nc):
        self.nc = nc
        self.es = ExitStack()
        self.E = {"pe": nc.tensor, "dve": nc.vector, "act": nc.scalar,
                  "pool": nc.gpsimd, "sp": nc.sync}
        self.sem = {}
        self.cnt = {}
        for e in ("pe", "dve", "act", "pool"):
            self.sem[e] = self.es.enter_context(nc.semaphore("sem_" + e))
            self.cnt[e] = 0
        self.dsem = [self.es.enter_context(nc.semaphore("dsem%d" % i)) for i in range(self.NDMA)]
        self.dcnt = [0] * self.NDMA
        self.drr = 0
        self.seen = {e: {} for e in self.E}
        self.lastw = {}
        self.readers = {}
        self.nid = 0
        self.n_inst = 0

    def sb(self, shape, dtype=F32, name=None):
        self.nid += 1
        name = name or "t%d" % self.nid
        return self.es.enter_context(self.nc.sbuf_tensor(name, list(shape), dtype))

    def ps(self, shape, dtype=F32, name=None):
        self.nid += 1
        name = name or "p%d" % self.nid
        return self.es.enter_context(self.nc.psum_tensor(name, list(shape), dtype))

    def dram(self, name, shape, dtype=F32, kind="Internal"):
        return self.nc.dram_tensor(name, list(shape), dtype, kind=kind).ap()

    @staticmethod
    def _k(k):
        if isinstance(k, str):
            return k
        if isinstance(k, tuple):
            return (Prog._k(k[0]),) + tuple(k[1:])
        return k.name

    def _need(self, eng, reads, writes):
        need = {}
        reads = [self._k(k) for k in reads]
        writes = [self._k(k) for k in writes]

        def add(ev):
            if ev is None:
                return
            skey, val, src = ev
            if src == "pe" and eng == "pe":
                return
            if need.get(skey, 0) < val:
                need[skey] = val

        for k in reads:
            add(self.lastw.get(k))
        for k in writes:
            add(self.lastw.get(k))
            for ev in self.readers.get(k, ()):
                add(ev)
        return need

    def _semobj(self, skey):
        return self.sem[skey] if isinstance(skey, str) else self.dsem[skey]

    def _emit_waits(self, eng, need):
        seen = self.seen[eng]
        for skey, val in need.items():
            if seen.get(skey, 0) >= val:
                continue
            self.E[eng].wait_ge(self._semobj(skey), val)
            self.n_inst += 1
            seen[skey] = val

    def _record(self, ev, reads, writes):
        reads = [self._k(k) for k in reads]
        writes = [self._k(k) for k in writes]
        for k in writes:
            self.lastw[k] = ev
            self.readers[k] = []
        for k in reads:
            if k in writes:
                continue
            lst = self.readers.setdefault(k, [])
            lst[:] = [x for x in lst if x[0] != ev[0]] + [ev]

    def op(self, eng, fn, reads=(), writes=()):
        need = self._need(eng, reads, writes)
        self._emit_waits(eng, need)
        inst = fn()
        inst.then_inc(self.sem[eng], 1)
        self.cnt[eng] += 1
        self.n_inst += 1
        ev = (eng, self.cnt[eng], eng)
        self._record(ev, reads, writes)
        return inst

    def pe_group(self, fns, reads=(), writes=()):
        need = self._need("pe", reads, writes)
        self._emit_waits("pe", need)
        inst = None
        for fn in fns:
            inst = fn()
            self.n_inst += 1
        inst.then_inc(self.sem["pe"], 1)
        self.cnt["pe"] += 1
        ev = ("pe", self.cnt["pe"], "pe")
        self._record(ev, reads, writes)

    def dma(self, q, out, in_, reads=(), writes=(), **kw):
        i = self.drr
        self.drr = (self.drr + 1) % self.NDMA
        need = self._need(q, reads, writes)
        if self.dcnt[i] > 0:
            need[i] = max(need.get(i, 0), 16 * self.dcnt[i])
        self._emit_waits(q, need)
        inst = self.E[q].dma_start(out=out, in_=in_, **kw)
        self.dcnt[i] += 1
        inst.then_inc(self.dsem[i], 16)
        self.n_inst += 1
        ev = (i, 16 * self.dcnt[i], "dma")
        self._record(ev, reads, writes)
        return ev

    def finish(self, keys):
        need = {}
        for k in keys:
            ev = self.lastw.get(self._k(k))
            if ev is not None:
                need[ev[0]] = max(need.get(ev[0], 0), ev[1])
        self._emit_waits("sp", need)

    def close(self):
        self.es.close()


    def barrier(self):
        need = {e: self.cnt[e] for e in self.cnt if self.cnt[e] > 0}
        for i in range(self.NDMA):
            if self.dcnt[i] > 0:
                need[i] = 16 * self.dcnt[i]
        for e in ("sp", "pool", "act", "dve", "pe"):
            self._emit_waits(e, dict(need))

    class _Scope:
        def __init__(self, P):
            self.P = P

        def __enter__(self):
            self.saved = self.P.es
            self.P.es = ExitStack()
            return self

        def __exit__(self, *a):
            self.P.barrier()
            self.P.es.close()
            self.P.es = self.saved
            return False

    def scope(self):
        return Prog._Scope(self)


class Rot:
    def __init__(self, tiles):
        self.t = tiles
        self.i = 0

    def n(self):
        x = self.t[self.i]
        self.i = (self.i + 1) % len(self.t)
        return x


def rot_sb(P, n, shape, dtype=F32):
    return Rot([P.sb(shape, dtype) for _ in range(n)])


def rot_ps(P, n, shape, dtype=F32):
    return Rot([P.ps(shape, dtype) for _ in range(n)])


EPS = 1e-6


def load_consts(P, ident_d):
    C = {}
    C["ident"] = P.sb([128, 128], F32, name="ident_sb")
    P.dma("sp", C["ident"][:], ident_d, writes=[C["ident"]])
    return C


def rms_scale(P, nc, xt, D, sq, ss, rs, reads):
    P.op("act", lambda: nc.scalar.activation(out=sq[:, 0:D], in_=xt, func=AF.Square, accum_out=ss[:]),
         reads=reads, writes=[sq, ss])
    P.op("dve", lambda: nc.vector.tensor_scalar(out=rs[:], in0=ss[:], scalar1=1.0 / D, scalar2=EPS,
                                                op0=ALU.mult, op1=ALU.add), reads=[ss], writes=[rs])
    P.op("act", lambda: nc.scalar.activation(out=rs[:], in_=rs[:], func=AF.Sqrt), reads=[rs], writes=[rs])
    P.op("dve", lambda: nc.vector.reciprocal(out=rs[:], in_=rs[:]), reads=[rs], writes=[rs])


def build_actT(P, nc, C, src, skey, r0, NTOK, K, actT, norm_g=None):
    KT = K // 128
    with P.scope():
        xin = rot_sb(P, 2, [128, K])
        psT = rot_ps(P, 2, [128, 512])
        if norm_g is not None:
            gb = P.sb([128, K])
            P.dma("sp", gb[:], norm_g.partition_broadcast(128), writes=[gb])
            sq = P.sb([128, K])
            ssr = rot_sb(P, 2, [128, 1])
            rsr = rot_sb(P, 2, [128, 1])
            hnr = rot_sb(P, 2, [128, K])
        cnt = 0
        for t in range(NTOK // 128):
            xt = xin.n()
            P.dma("sp", xt[:], src[r0 + t * 128:r0 + (t + 1) * 128, 0:K], reads=[(skey, (r0 // 128) + t)], writes=[xt])
            if norm_g is not None:
                ss = ssr.n(); rs = rsr.n(); hn = hnr.n()
                rms_scale(P, nc, xt[:], K, sq, ss, rs, [xt])
                P.op("dve", lambda: nc.vector.scalar_tensor_tensor(out=hn[:], in0=xt[:], scalar=rs[:, 0:1], in1=gb[:],
                                                                    op0=ALU.mult, op1=ALU.mult),
                     reads=[xt, rs, gb], writes=[hn])
                xt = hn
            for k0 in range(0, KT, 4):
                pt = psT.n()
                P.pe_group([(lambda kk=kk: nc.tensor.transpose(out=pt[:, kk * 128:(kk + 1) * 128],
                                                               in_=xt[:, (k0 + kk) * 128:(k0 + kk + 1) * 128],
                                                               identity=C["ident"][:])) for kk in range(4)],
                           reads=[xt, C["ident"]], writes=[pt])
                e = "dve" if cnt % 2 == 0 else "act"
                cnt += 1
                dst = actT[:, k0:k0 + 4, t * 128:(t + 1) * 128]
                srcv = pt[:].rearrange("p (a b) -> p a b", a=4)
                if e == "dve":
                    P.op("dve", lambda: nc.vector.tensor_copy(out=dst, in_=srcv), reads=[pt], writes=[(actT, t)])
                else:
                    P.op("act", lambda: nc.scalar.copy(out=dst, in_=srcv), reads=[pt], writes=[(actT, t)])


def gemm_T(P, nc, actT, NTOK, K, w, wc0, N, dst, dkey, dr0, dc0, CW=512):
    KT = K // 128
    with P.scope():
        wbr = rot_sb(P, 2, [128, KT, CW], BF16)
        psO = rot_ps(P, 4, [128, CW])
        otr = rot_sb(P, 4, [128, CW])
        cnt = 0
        for c0 in range(0, N, CW):
            cw = min(CW, N - c0)
            wb = wbr.n()
            P.dma("pool", wb[:, :, 0:cw], w[:, wc0 + c0:wc0 + c0 + cw].rearrange("(kt p) n -> p kt n", p=128),
                  writes=[wb])
            for t in range(NTOK // 128):
                po = psO.n()
                P.pe_group([(lambda kt=kt: nc.tensor.matmul(po[:, 0:cw], lhsT=actT[:, kt, t * 128:(t + 1) * 128],
                                                            rhs=wb[:, kt, 0:cw], start=(kt == 0), stop=(kt == KT - 1)))
                            for kt in range(KT)], reads=[(actT, t), wb], writes=[po])
                ot = otr.n()
                if cnt % 2 == 0:
                    P.op("act", lambda: nc.scalar.copy(out=ot[:, 0:cw], in_=po[:, 0:cw]), reads=[po], writes=[ot])
                else:
                    P.op("dve", lambda: nc.vector.tensor_copy(out=ot[:, 0:cw], in_=po[:, 0:cw]), reads=[po], writes=[ot])
                cnt += 1
                P.dma("sp", dst[dr0 + t * 128:dr0 + (t + 1) * 128, dc0 + c0:dc0 + c0 + cw], ot[:, 0:cw],
                      reads=[ot], writes=[(dkey, dr0 // 128 + t)])


def linear_T(P, nc, C, src, skey, sr0, NTOK, K, w, wc0, N, dst, dkey, dr0=0, dc0=0, norm_g=None):
    with P.scope():
        actT = P.sb([128, K // 128, NTOK], BF16)
        build_actT(P, nc, C, src, skey, sr0, NTOK, K, actT, norm_g)
        gemm_T(P, nc, actT, NTOK, K, w, wc0, N, dst, dkey, dr0, dc0)


def residual_stage(P, nc, xsrc, xkey, ysrc, ykey, g, dst, dkey, NTOK, D=2048, xr0=0, yr0=0, dr0=0):
    with P.scope():
        gb = P.sb([128, D])
        P.dma("sp", gb[:], g.partition_broadcast(128), writes=[gb])
        xr = rot_sb(P, 3, [128, D]); yr = rot_sb(P, 3, [128, D]); sq = P.sb([128, D])
        ssr = rot_sb(P, 2, [128, 1]); rsr = rot_sb(P, 2, [128, 1])
        for t in range(NTOK // 128):
            xt = xr.n(); yt = yr.n(); ss = ssr.n(); rs = rsr.n()
            P.dma("sp", xt[:], xsrc[xr0 + t * 128:xr0 + (t + 1) * 128, :], reads=[(xkey, xr0 // 128 + t)], writes=[xt])
            P.dma("sp", yt[:], ysrc[yr0 + t * 128:yr0 + (t + 1) * 128, :], reads=[(ykey, yr0 // 128 + t)], writes=[yt])
            rms_scale(P, nc, yt[:], D, sq, ss, rs, [yt])
            P.op("dve", lambda: nc.vector.scalar_tensor_tensor(out=yt[:], in0=yt[:], scalar=rs[:, 0:1], in1=gb[:],
                                                                op0=ALU.mult, op1=ALU.mult), reads=[yt, rs, gb], writes=[yt])
            P.op("pool", lambda: nc.gpsimd.tensor_tensor(out=xt[:], in0=xt[:], in1=yt[:], op=ALU.add),
                 reads=[xt, yt], writes=[xt])
            P.dma("pool", dst[dr0 + t * 128:dr0 + (t + 1) * 128, :], xt[:], reads=[xt], writes=[(dkey, dr0 // 128 + t)])


D_MODEL = 2048
SEQ = 4096
BATCH = 4
DEPTH = 2
N_MEM = 256
IN_SIZES = (1024, 1024, 1024, 1024, 8, 8, 512, 512, 1024, 16, 1024, 1024, 6144)
IN_OFF = [0]
for _s in IN_SIZES:
    IN_OFF.append(IN_OFF[-1] + _s)
IN_WIDTH = IN_OFF[-1]
(O_DQ, O_DK, O_DV, O_DZ, O_DB, O_DA, O_GQ, O_GK, O_GV, O_GLOW, O_GR, O_SU, O_GATES) = IN_OFF[:13]
FFN_H = 5632
NTOK = 2048

_NC_CACHE = {}


def build_LA():
    nc = bass.Bass("TRN2", target_bir_lowering=False)
    P = Prog(nc)
    x = P.dram("x", [NTOK, D_MODEL], F32, "ExternalInput")
    w = P.dram("w_in", [D_MODEL, IN_WIDTH], F32, "ExternalInput")
    g = P.dram("g", [1, D_MODEL], F32, "ExternalInput")
    ident = P.dram("ident", [128, 128], F32, "ExternalInput")
    z = P.dram("z", [NTOK, IN_WIDTH], F32, "ExternalOutput")
    C = load_consts(P, ident)
    linear_T(P, nc, C, x, "x", 0, NTOK, D_MODEL, w, 0, IN_WIDTH, z, "z", norm_g=g)
    P.finish([("z", t) for t in range(NTOK // 128)])
    P.close()
    return nc


def run(nc, in_maps):
    res = run_bass_kernel_spmd(nc, in_maps, core_ids=list(range(8)))
    return res.results


def merge_stage(P, nc, ya, yb, yg, gates, gkey, gc0, merged, NTOK, r0=0):
    D = D_MODEL
    with P.scope():
        yar = rot_sb(P, 2, [128, D]); ybr = rot_sb(P, 2, [128, D]); ygr = rot_sb(P, 2, [128, 2 * D])
        gtr = rot_sb(P, 2, [128, 3 * D])
        for t in range(NTOK // 128):
            rows = slice(r0 + t * 128, r0 + (t + 1) * 128)
            tt = r0 // 128 + t
            a = yar.n(); b = ybr.n(); gl = ygr.n(); gt = gtr.n()
            P.dma("sp", a[:], ya[rows, :], reads=[("ya", tt)], writes=[a])
            P.dma("sp", b[:], yb[rows, :], reads=[("yb", tt)], writes=[b])
            P.dma("sp", gl[:], yg[rows, :], reads=[("yg", tt)], writes=[gl])
            P.dma("sp", gt[:], gates[rows, gc0:gc0 + 3 * D], reads=[(gkey, tt)], writes=[gt])
            P.op("act", lambda: nc.scalar.activation(out=gt[:], in_=gt[:], func=AF.Sigmoid), reads=[gt], writes=[gt])
            P.op("act", lambda: nc.scalar.activation(out=gl[:, D:2 * D], in_=gl[:, D:2 * D], func=AF.Sigmoid),
                 reads=[gl], writes=[gl])
            P.op("dve", lambda: nc.vector.tensor_tensor(out=a[:], in0=a[:], in1=gt[:, 0:D], op=ALU.mult),
                 reads=[a, gt], writes=[a])
            P.op("pool", lambda: nc.gpsimd.tensor_tensor(out=b[:], in0=b[:], in1=gt[:, D:2 * D], op=ALU.mult),
                 reads=[b, gt], writes=[b])
            P.op("dve", lambda: nc.vector.tensor_tensor(out=gl[:, 0:D], in0=gl[:, 0:D], in1=gl[:, D:2 * D], op=ALU.mult),
                 reads=[gl], writes=[gl])
            P.op("pool", lambda: nc.gpsimd.tensor_tensor(out=a[:], in0=a[:], in1=b[:], op=ALU.add),
                 reads=[a, b], writes=[a])
            P.op("dve", lambda: nc.vector.tensor_tensor(out=gl[:, 0:D], in0=gl[:, 0:D], in1=gt[:, 2 * D:3 * D], op=ALU.mult),
                 reads=[gl, gt], writes=[gl])
            P.op("dve", lambda: nc.vector.tensor_tensor(out=a[:], in0=a[:], in1=gl[:, 0:D], op=ALU.add),
                 reads=[a, gl], writes=[a])
            P.dma("pool", merged[rows, :], a[:], reads=[a], writes=[("merged", tt)])


def attn_stage(P, nc, C, q, kv, o, NTOK, r0=0):
    H = 4
    sc = 128.0 ** -0.5
    with P.scope():
        kvt = P.sb([128, 2, 1024])
        P.dma("sp", kvt[:], kv.rearrange("(a p) n -> p a n", p=128), reads=[("kv", 0), ("kv", 1)], writes=[kvt])
        kT = P.sb([128, H, 256])
        psT = rot_ps(P, 2, [128, 512])
        psS = [P.ps([128, 512]) for _ in range(H)]
        psO = rot_ps(P, 2, [128, 512])
        for h in range(H):
            pt = psT.n()
            P.pe_group([(lambda a=a: nc.tensor.transpose(out=pt[:, a * 128:(a + 1) * 128],
                                                         in_=kvt[:, a, h * 128:(h + 1) * 128], identity=C["ident"][:]))
                        for a in range(2)], reads=[kvt, C["ident"]], writes=[pt])
            P.op("dve", lambda: nc.vector.tensor_copy(out=kT[:, h, :], in_=pt[:, 0:256]), reads=[pt], writes=[kT])
        qr = rot_sb(P, 2, [128, 512]); qTr = rot_sb(P, 2, [128, 512]); otr = rot_sb(P, 2, [128, 512])
        hp = [dict(p=rot_sb(P, 2, [128, 256]), pT=rot_sb(P, 2, [128, 256]), mx=rot_sb(P, 2, [128, 1]), sm=rot_sb(P, 2, [128, 1])) for _ in range(H)]

        def head(h, qT, ot):
            s_ps = psS[h]
            mx = hp[h]["mx"].n(); sm = hp[h]["sm"].n(); p = hp[h]["p"].n(); pT = hp[h]["pT"].n()
            P.pe_group([lambda: nc.tensor.matmul(s_ps[:, 0:256], lhsT=qT[:, h * 128:(h + 1) * 128], rhs=kT[:, h, :],
                                                 start=True, stop=True)], reads=[qT, kT], writes=[s_ps])
            P.op("dve", lambda: nc.vector.tensor_reduce(out=mx[:], in_=s_ps[:, 0:256], axis=AX.X, op=ALU.max),
                 reads=[s_ps], writes=[mx])
            yield
            P.op("dve", lambda: nc.vector.tensor_scalar(out=mx[:], in0=mx[:], scalar1=-sc, scalar2=None, op0=ALU.mult),
                 reads=[mx], writes=[mx])
            yield
            P.op("act", lambda: nc.scalar.activation(out=p[:], in_=s_ps[:, 0:256], func=AF.Exp, bias=mx[:, 0:1], scale=sc,
                                                     accum_out=sm[:]), reads=[s_ps, mx], writes=[p, sm])
            yield
            P.op("dve", lambda: nc.vector.reciprocal(out=sm[:], in_=sm[:]), reads=[sm], writes=[sm])
            pt2 = psT.n()
            P.pe_group([(lambda a=a: nc.tensor.transpose(out=pt2[:, a * 128:(a + 1) * 128], in_=p[:, a * 128:(a + 1) * 128],
                                                         identity=C["ident"][:])) for a in range(2)],
                       reads=[p, C["ident"]], writes=[pt2])
            P.op("dve", lambda: nc.vector.tensor_copy(out=pT[:], in_=pt2[:, 0:256]), reads=[pt2], writes=[pT])
            yield
            o_ps = psO.n()
            P.pe_group([(lambda a=a: nc.tensor.matmul(o_ps[:, 0:128], lhsT=pT[:, a * 128:(a + 1) * 128],
                                                      rhs=kvt[:, a, 512 + h * 128:512 + (h + 1) * 128],
                                                      start=(a == 0), stop=(a == 1))) for a in range(2)],
                       reads=[pT, kvt], writes=[o_ps])
            P.op("dve", lambda: nc.vector.tensor_scalar(out=ot[:, h * 128:(h + 1) * 128], in0=o_ps[:, 0:128], scalar1=sm[:, 0:1], scalar2=None,
                                                        op0=ALU.mult), reads=[o_ps, sm], writes=[(ot, h)])
            yield

        for t in range(NTOK // 128):
            rows = slice(r0 + t * 128, r0 + (t + 1) * 128)
            tt = r0 // 128 + t
            qt = qr.n(); qT = qTr.n(); ot = otr.n()
            P.dma("sp", qt[:], q[rows, :], reads=[("q", tt)], writes=[qt])
            pt = psT.n()
            P.pe_group([(lambda h=h: nc.tensor.transpose(out=pt[:, h * 128:(h + 1) * 128], in_=qt[:, h * 128:(h + 1) * 128],
                                                         identity=C["ident"][:])) for h in range(H)],
                       reads=[qt, C["ident"]], writes=[pt])
            P.op("act", lambda: nc.scalar.copy(out=qT[:], in_=pt[:]), reads=[pt], writes=[qT])
            gens = [head(h, qT, ot) for h in range(H)]
            while gens:
                for g_ in list(gens):
                    try:
                        next(g_)
                    except StopIteration:
                        gens.remove(g_)
            P.dma("sp", o[rows, :], ot[:], reads=[(ot, h) for h in range(H)], writes=[("o", tt)])


def build_LC():
    nc = bass.Bass("TRN2", target_bir_lowering=False)
    P = Prog(nc)
    D = D_MODEL
    dr = lambda n, s, k="ExternalInput": P.dram(n, s, F32, k)
    x = dr("x", [NTOK, D]); oa = dr("oa", [NTOK, 1024]); ob = dr("ob", [NTOK, 1024]); ys = dr("ys", [NTOK, 1024])
    gates = dr("gates", [NTOK, 3 * D]); mem = dr("mem", [N_MEM, D])
    w_dn = dr("w_dn_out", [1024, D]); w_gla = dr("w_gla_out", [1024, D]); w_glu = dr("w_s5_glu", [1024, 2 * D])
    w_mix = dr("w_mix_out", [D, D]); w_q = dr("w_xa_q", [D, 512]); w_kv = dr("w_xa_kv", [D, 1024]); w_o = dr("w_xa_out", [512, D])
    g_post = dr("g_mix_post", [1, D]); g_xpre = dr("g_xa_pre", [1, D]); g_xpost = dr("g_xa_post", [1, D]); g_mem = dr("g_mem", [1, D])
    ident = dr("ident", [128, 128])
    x2 = dr("x2", [NTOK, D], "ExternalOutput")
    ya = dr("ya", [NTOK, D], "Internal"); yb = dr("yb", [NTOK, D], "Internal"); yg = dr("yg", [NTOK, 2 * D], "Internal")
    merged = dr("merged", [NTOK, D], "Internal"); y1 = dr("y1", [NTOK, D], "Internal"); x1 = dr("x1", [NTOK, D], "Internal")
    q = dr("q", [NTOK, 512], "Internal"); kv = dr("kv", [N_MEM, 1024], "Internal"); o = dr("o", [NTOK, 512], "Internal")
    y2 = dr("y2", [NTOK, D], "Internal")
    C = load_consts(P, ident)
    linear_T(P, nc, C, oa, "oa", 0, NTOK, 1024, w_dn, 0, D, ya, "ya")
    linear_T(P, nc, C, ob, "ob", 0, NTOK, 1024, w_gla, 0, D, yb, "yb")
    linear_T(P, nc, C, ys, "ys", 0, NTOK, 1024, w_glu, 0, 2 * D, yg, "yg")
    merge_stage(P, nc, ya, yb, yg, gates, merged, NTOK)
    linear_T(P, nc, C, merged, "merged", 0, NTOK, D, w_mix, 0, D, y1, "y1")
    residual_stage(P, nc, x, "x", y1, "y1", g_post, x1, "x1", NTOK)
    linear_T(P, nc, C, x1, "x1", 0, NTOK, D, w_q, 0, 512, q, "q", norm_g=g_xpre)
    linear_T(P, nc, C, mem, "mem", 0, N_MEM, D, w_kv, 0, 1024, kv, "kv", norm_g=g_mem)
    attn_stage(P, nc, C, q, kv, o, NTOK)
    linear_T(P, nc, C, o, "o", 0, NTOK, 512, w_o, 0, D, y2, "y2")
    residual_stage(P, nc, x1, "x1", y2, "y2", g_xpost, x2, "x2", NTOK)
    P.finish([("x2", t) for t in range(NTOK // 128)])
    P.close()
    return nc


def gelu_tanh_mul(P, nc, cbuf, up_ps, out_ap, tmp1, tmp2, wkey):
    P.op("pool", lambda: nc.gpsimd.tensor_tensor(out=tmp1[:], in0=cbuf[:], in1=cbuf[:], op=ALU.mult),
         reads=[cbuf], writes=[tmp1])
    P.op("dve", lambda: nc.vector.tensor_scalar(out=tmp1[:], in0=tmp1[:], scalar1=0.044715, scalar2=1.0,
                                                op0=ALU.mult, op1=ALU.add), reads=[tmp1], writes=[tmp1])
    P.op("pool", lambda: nc.gpsimd.tensor_tensor(out=tmp1[:], in0=tmp1[:], in1=cbuf[:], op=ALU.mult),
         reads=[tmp1, cbuf], writes=[tmp1])
    P.op("act", lambda: nc.scalar.activation(out=tmp1[:], in_=tmp1[:], func=AF.Sigmoid, scale=1.5957691216057308),
         reads=[tmp1], writes=[tmp1])
    P.op("dve", lambda: nc.vector.tensor_tensor(out=tmp2[:], in0=cbuf[:], in1=up_ps, op=ALU.mult),
         reads=[cbuf, wkey], writes=[tmp2])
    P.op("dve", lambda: nc.vector.tensor_tensor(out=out_ap, in0=tmp1[:], in1=tmp2[:], op=ALU.mult),
         reads=[tmp1, tmp2], writes=[wkey + "_o"])


def ffn_stage(P, nc, C, x2h, w_up, w_dn, cwd, g_pre, y3):
    D = D_MODEL
    FT = FFN_H // 128
    cw = P.sb([128, FT, 3])
    P.dma("sp", cw[:], cwd.rearrange("p (f j) -> p f j", j=3), writes=[cw])
    HALF = 1024
    for half in range(SEQ // HALF):
        with P.scope():
            AT = P.sb([128, FT, HALF], BF16)
            with P.scope():
                hT = P.sb([128, 16, 128 + HALF], BF16)
                build_actT(P, nc, C, x2h, "x2h", half * HALF, 128 + HALF, D, hT, norm_g=g_pre)
                wgr = rot_sb(P, 2, [128, 16, 256], BF16); wur = rot_sb(P, 2, [128, 16, 256], BF16)
                psH = rot_ps(P, 1, [128, 512]); psG = rot_ps(P, 2, [128, 512]); psU = rot_ps(P, 2, [128, 512])
                gbr = rot_sb(P, 2, [128, 514]); cbr = rot_sb(P, 2, [128, 512]); t1r = rot_sb(P, 2, [128, 512]); t2r = rot_sb(P, 2, [128, 512])
                for fc in range(FFN_H // 256):
                    wg = wgr.n(); wu = wur.n()
                    P.dma("pool", wg[:], w_up[:, fc * 256:(fc + 1) * 256].rearrange("(kt p) n -> p kt n", p=128), writes=[wg])
                    P.dma("pool", wu[:], w_up[:, FFN_H + fc * 256:FFN_H + (fc + 1) * 256].rearrange("(kt p) n -> p kt n", p=128), writes=[wu])
                    for sub in range(2):
                        f = fc * 2 + sub
                        ph = psH.n()
                        P.pe_group([(lambda kt=kt: nc.tensor.matmul(ph[:, 0:128], lhsT=wg[:, kt, sub * 128:(sub + 1) * 128], rhs=hT[:, kt, 0:128],
                                                                    start=(kt == 0), stop=(kt == 15))) for kt in range(16)],
                                   reads=[wg, (hT, 0)], writes=[ph])
                        gb = gbr.n()
                        P.op("act", lambda: nc.scalar.copy(out=gb[:, 0:2], in_=ph[:, 126:128]), reads=[ph], writes=[gb])
                        for tb in range(HALF // 512):
                            c0 = 128 + tb * 512
                            pg = psG.n(); pu = psU.n()
                            rk = [(hT, (c0 // 128) + i) for i in range(4)]
                            P.pe_group([(lambda kt=kt: nc.tensor.matmul(pg[:], lhsT=wg[:, kt, sub * 128:(sub + 1) * 128], rhs=hT[:, kt, c0:c0 + 512],
                                                                        start=(kt == 0), stop=(kt == 15))) for kt in range(16)],
                                       reads=[wg] + rk, writes=[pg])
                            P.pe_group([(lambda kt=kt: nc.tensor.matmul(pu[:], lhsT=wu[:, kt, sub * 128:(sub + 1) * 128], rhs=hT[:, kt, c0:c0 + 512],
                                                                        start=(kt == 0), stop=(kt == 15))) for kt in range(16)],
                                       reads=[wu] + rk, writes=[pu])
                            P.op("act", lambda: nc.scalar.copy(out=gb[:, 2:514], in_=pg[:]), reads=[pg], writes=[gb])
                            cb = cbr.n(); t1 = t1r.n(); t2 = t2r.n()
                            P.op("dve", lambda: nc.vector.tensor_scalar(out=cb[:], in0=gb[:, 0:512], scalar1=cw[:, f, 0:1], scalar2=None, op0=ALU.mult),
                                 reads=[gb, cw], writes=[cb])
                            P.op("dve", lambda: nc.vector.scalar_tensor_tensor(out=cb[:], in0=gb[:, 1:513], scalar=cw[:, f, 1:2], in1=cb[:],
                                                                                op0=ALU.mult, op1=ALU.add), reads=[gb, cw, cb], writes=[cb])
                            P.op("dve", lambda: nc.vector.scalar_tensor_tensor(out=cb[:], in0=gb[:, 2:514], scalar=cw[:, f, 2:3], in1=cb[:],
                                                                                op0=ALU.mult, op1=ALU.add), reads=[gb, cw, cb], writes=[cb])
                            if tb + 1 < HALF // 512:
                                gb2 = gbr.n()
                                P.op("act", lambda: nc.scalar.copy(out=gb2[:, 0:2], in_=gb[:, 512:514]), reads=[gb], writes=[gb2])
                            P.op("pool", lambda: nc.gpsimd.tensor_tensor(out=t1[:], in0=cb[:], in1=cb[:], op=ALU.mult), reads=[cb], writes=[t1])
                            P.op("dve", lambda: nc.vector.tensor_scalar(out=t1[:], in0=t1[:], scalar1=0.044715, scalar2=1.0, op0=ALU.mult, op1=ALU.add),
                                 reads=[t1], writes=[t1])
                            P.op("pool", lambda: nc.gpsimd.tensor_tensor(out=t1[:], in0=t1[:], in1=cb[:], op=ALU.mult), reads=[t1, cb], writes=[t1])
                            P.op("act", lambda: nc.scalar.activation(out=t1[:], in_=t1[:], func=AF.Sigmoid, scale=1.5957691216057308),
                                 reads=[t1], writes=[t1])
                            P.op("dve", lambda: nc.vector.tensor_tensor(out=t2[:], in0=cb[:], in1=pu[:], op=ALU.mult), reads=[cb, pu], writes=[t2])
                            P.op("dve", lambda: nc.vector.tensor_tensor(out=AT[:, f, tb * 512:(tb + 1) * 512], in0=t1[:], in1=t2[:], op=ALU.mult),
                                 reads=[t1, t2], writes=[(AT, f)])
                            if tb + 1 < HALF // 512:
                                gb = gb2
            with P.scope():
                wdr = rot_sb(P, 2, [128, FT, 256], BF16)
                psO = rot_ps(P, 3, [128, 512]); otr = rot_sb(P, 3, [128, 256])
                akeys = [(AT, f) for f in range(FT)]
                cnt = 0
                for cbk in range(D // 256):
                    wd = wdr.n()
                    P.dma("pool", wd[:], w_dn[:, cbk * 256:(cbk + 1) * 256].rearrange("(ft p) n -> p ft n", p=128), writes=[wd])
                    for t in range(HALF // 128):
                        po = psO.n()
                        P.pe_group([(lambda f=f: nc.tensor.matmul(po[:, 0:256], lhsT=AT[:, f, t * 128:(t + 1) * 128], rhs=wd[:, f, :],
                                                                  start=(f == 0), stop=(f == FT - 1))) for f in range(FT)],
                                   reads=[wd] + akeys, writes=[po])
                        ot = otr.n()
                        if cnt % 2 == 0:
                            P.op("act", lambda: nc.scalar.copy(out=ot[:], in_=po[:, 0:256]), reads=[po], writes=[ot])
                        else:
                            P.op("dve", lambda: nc.vector.tensor_copy(out=ot[:], in_=po[:, 0:256]), reads=[po], writes=[ot])
                        cnt += 1
                        r0 = half * HALF + t * 128
                        P.dma("sp", y3[r0:r0 + 128, cbk * 256:(cbk + 1) * 256], ot[:], reads=[ot], writes=[("y3", r0 // 128)])


TWO_PI = 6.283185307179586


def sin_reduced(P, nc, out, ang, shift, shape, tmpa, tmpk, tmpf):
    P.op("dve", lambda: nc.vector.tensor_scalar(out=tmpa, in0=ang, scalar1=float(shift), scalar2=None, op0=ALU.add),
         reads=["sr_ang"], writes=["sr_a"])
    P.op("dve", lambda: nc.vector.tensor_scalar(out=tmpk, in0=tmpa, scalar1=1.0 / TWO_PI, scalar2=None, op0=ALU.mult),
         reads=["sr_a"], writes=["sr_k"])
    P.op("dve", lambda: nc.vector.tensor_copy(out=tmpf, in_=tmpk), reads=["sr_k"], writes=["sr_f"])
    P.op("dve", lambda: nc.vector.scalar_tensor_tensor(out=tmpa, in0=tmpf, scalar=-TWO_PI, in1=tmpa, op0=ALU.mult, op1=ALU.add),
         reads=["sr_f", "sr_a"], writes=["sr_a"])
    P.op("dve", lambda: nc.vector.tensor_scalar(out=tmpf, in0=tmpa, scalar1=3.141592653589793, scalar2=None, op0=ALU.is_gt),
         reads=["sr_a"], writes=["sr_f"])
    P.op("dve", lambda: nc.vector.scalar_tensor_tensor(out=tmpa, in0=tmpf, scalar=-TWO_PI, in1=tmpa, op0=ALU.mult, op1=ALU.add),
         reads=["sr_f", "sr_a"], writes=["sr_a"])
    P.op("dve", lambda: nc.vector.tensor_scalar(out=tmpf, in0=tmpa, scalar1=-3.141592653589793, scalar2=None, op0=ALU.is_lt),
         reads=["sr_a"], writes=["sr_f"])
    P.op("dve", lambda: nc.vector.scalar_tensor_tensor(out=tmpa, in0=tmpf, scalar=TWO_PI, in1=tmpa, op0=ALU.mult, op1=ALU.add),
         reads=["sr_f", "sr_a"], writes=["sr_a"])
    P.op("act", lambda: nc.scalar.activation(out=out, in_=tmpa, func=AF.Sin), reads=["sr_a"], writes=["sr_out"])


def s5_stage(P, nc, C, z, ysT, par):
    CH = 256
    NCH = SEQ // CH
    for half in range(2):
        with P.scope():
            NS = 16
            s0 = half * NS
            lre = P.sb([128, NS]); lim = P.sb([128, NS]); dt = P.sb([128, NS]); mag = P.sb([128, NS]); th = P.sb([128, NS])
            zr = P.sb([128, NS]); zi = P.sb([128, NS]); ar = P.sb([128, NS]); ai = P.sb([128, NS]); den = P.sb([128, NS]); tq = P.sb([128, NS])
            dcol = P.sb([128, 4])
            Bre = P.sb([128, NS, 128]); Bim = P.sb([128, NS, 128]); Cre = P.sb([128, NS, 128]); Cim = P.sb([128, NS, 128])
            tau = P.sb([128, CH])
            P.dma("sp", lre[:], par["s5_lre"][:, s0:s0 + NS], writes=[lre])
            P.dma("sp", lim[:], par["s5_lim"][:, s0:s0 + NS], writes=[lim])
            P.dma("sp", dt[:], par["s5_lstep"][:, s0:s0 + NS], writes=[dt])
            P.dma("sp", dcol[:], par["s5_d"][:, half * 4:(half + 1) * 4], writes=[dcol])
            P.dma("sp", tau[:], par["tau"], writes=[tau])
            for nm, tl in (("s5_Bre", Bre), ("s5_Bim", Bim), ("s5_Cre", Cre), ("s5_Cim", Cim)):
                P.dma("sp", tl[:], par[nm][s0:s0 + NS].rearrange("s p m -> p s m"), writes=[tl])
            P.op("pool", lambda: nc.gpsimd.tensor_scalar(out=Cim[:], in0=Cim[:], scalar1=-1.0, scalar2=None, op0=ALU.mult), reads=[Cim], writes=[Cim])
            V = lambda fn, r, w: P.op("dve", fn, reads=r, writes=w)
            P.op("act", lambda: nc.scalar.activation(out=dt[:], in_=dt[:], func=AF.Exp), reads=[dt], writes=[dt])
            V(lambda: nc.vector.tensor_scalar(out=lre[:], in0=lre[:], scalar1=-1e-4, scalar2=None, op0=ALU.min), [lre], [lre])
            V(lambda: nc.vector.tensor_tensor(out=mag[:], in0=lre[:], in1=dt[:], op=ALU.mult), [lre, dt], [mag])
            P.op("act", lambda: nc.scalar.activation(out=mag[:], in_=mag[:], func=AF.Exp), reads=[mag], writes=[mag])
            V(lambda: nc.vector.tensor_tensor(out=th[:], in0=lim[:], in1=dt[:], op=ALU.mult), [lim, dt], [th])
            ta = P.sb([128, CH]); tk = P.sb([128, CH], I32); tf = P.sb([128, CH])
            sn = P.sb([128, NS]); cs = P.sb([128, NS])
            P._record(("dve", P.cnt["dve"], "dve"), [], ["sr_ang"])
            sin_reduced(P, nc, sn[:], th[:], 0.0, None, ta[:, 0:NS], tk[:, 0:NS], tf[:, 0:NS])
            sin_reduced(P, nc, cs[:], th[:], 1.5707963267948966, None, ta[:, 0:NS], tk[:, 0:NS], tf[:, 0:NS])
            V(lambda: nc.vector.tensor_tensor(out=ar[:], in0=mag[:], in1=cs[:], op=ALU.mult), [mag, "sr_out"], [ar])
            V(lambda: nc.vector.tensor_tensor(out=ai[:], in0=mag[:], in1=sn[:], op=ALU.mult), [mag, "sr_out"], [ai])
            V(lambda: nc.vector.tensor_scalar(out=ar[:], in0=ar[:], scalar1=-1.0, scalar2=None, op0=ALU.add), [ar], [ar])
            V(lambda: nc.vector.tensor_tensor(out=den[:], in0=lre[:], in1=lre[:], op=ALU.mult), [lre], [den])
            V(lambda: nc.vector.tensor_tensor(out=tq[:], in0=lim[:], in1=lim[:], op=ALU.mult), [lim], [tq])
            V(lambda: nc.vector.tensor_tensor(out=den[:], in0=den[:], in1=tq[:], op=ALU.add), [den, tq], [den])
            V(lambda: nc.vector.reciprocal(out=den[:], in_=den[:]), [den], [den])
            V(lambda: nc.vector.tensor_tensor(out=zr[:], in0=ar[:], in1=lre[:], op=ALU.mult), [ar, lre], [zr])
            V(lambda: nc.vector.tensor_tensor(out=tq[:], in0=ai[:], in1=lim[:], op=ALU.mult), [ai, lim], [tq])
            V(lambda: nc.vector.tensor_tensor(out=zr[:], in0=zr[:], in1=tq[:], op=ALU.add), [zr, tq], [zr])
            V(lambda: nc.vector.tensor_tensor(out=zr[:], in0=zr[:], in1=den[:], op=ALU.mult), [zr, den], [zr])
            V(lambda: nc.vector.tensor_tensor(out=zi[:], in0=ai[:], in1=lre[:], op=ALU.mult), [ai, lre], [zi])
            V(lambda: nc.vector.tensor_tensor(out=tq[:], in0=ar[:], in1=lim[:], op=ALU.mult), [ar, lim], [tq])
            V(lambda: nc.vector.tensor_tensor(out=zi[:], in0=zi[:], in1=tq[:], op=ALU.subtract), [zi, tq], [zi])
            V(lambda: nc.vector.tensor_tensor(out=zi[:], in0=zi[:], in1=den[:], op=ALU.mult), [zi, den], [zi])
            TC = P.sb([128, NS, CH]); TS = P.sb([128, NS, CH]); TzR = P.sb([128, NS, CH]); TzI = P.sb([128, NS, CH])
            ang = P.sb([128, CH])
            for s in range(NS):
                V(lambda: nc.vector.tensor_scalar(out=ang[:], in0=tau[:], scalar1=th[:, s:s + 1], scalar2=None, op0=ALU.mult), [tau, th, "sr_a"], ["sr_ang"])
                sin_reduced(P, nc, TS[:, s, :], ang[:], 0.0, None, ta[:], tk[:], tf[:])
                sin_reduced(P, nc, TC[:, s, :], ang[:], 1.5707963267948966, None, ta[:], tk[:], tf[:])
                V(lambda: nc.vector.tensor_scalar(out=ta[:], in0=TS[:, s, :], scalar1=zi[:, s:s + 1], scalar2=None, op0=ALU.mult), ["sr_out", zi], ["sr_a"])
                V(lambda: nc.vector.scalar_tensor_tensor(out=TzR[:, s, :], in0=TC[:, s, :], scalar=zr[:, s:s + 1], in1=ta[:], op0=ALU.mult, op1=ALU.add),
                  ["sr_a", zr, "sr_out"], [TzR])
                V(lambda: nc.vector.tensor_scalar(out=ta[:], in0=TS[:, s, :], scalar1=zr[:, s:s + 1], scalar2=None, op0=ALU.mult), ["sr_out", zr], ["sr_a"])
                V(lambda: nc.vector.scalar_tensor_tensor(out=TzI[:, s, :], in0=TC[:, s, :], scalar=zi[:, s:s + 1], in1=ta[:], op0=ALU.mult, op1=ALU.subtract),
                  ["sr_a", zi, "sr_out"], [TzI])
            tabs = ["sr_out", TzR, TzI]
            carry = P.sb([128, NS, 2])
            P.op("dve", lambda: nc.vector.memset(carry[:], 0.0), writes=[(carry, s_) for s_ in range(NS)])
            psB = rot_ps(P, 4, [128, 512]); psY = rot_ps(P, 2, [128, 512]); psT = rot_ps(P, 2, [128, 512])
            tmps = [rot_sb(P, 8, [128, CH]) for _ in range(4)]
            xsets = [[(P.sb([128, CH]), P.sb([128, CH])) for _ in range(4)] for _ in range(2)]
            ucr = rot_sb(P, 2, [128, 4, CH]); zin = rot_sb(P, 2, [128, 512])
            yo = rot_sb(P, 2, [128, CH]); y1 = rot_sb(P, 2, [128, CH]); y2 = rot_sb(P, 2, [128, CH])
            Pl = lambda fn, r, w: P.op("pool", fn, reads=r, writes=w)

            def st_gen(s, kb, cs_, uc, tmp, x_r, x_i):
                br = psB.n(); bi = psB.n()
                P.pe_group([lambda: nc.tensor.matmul(br[:, 0:CH], lhsT=Bre[:, s, :], rhs=uc[:, kb, :], start=True, stop=True)], reads=[Bre, uc], writes=[br])
                P.pe_group([lambda: nc.tensor.matmul(bi[:, 0:CH], lhsT=Bim[:, s, :], rhs=uc[:, kb, :], start=True, stop=True)], reads=[Bim, uc], writes=[bi])
                m1 = tmp.n(); m2 = tmp.n(); m3 = tmp.n(); m4 = tmp.n()
                V(lambda: nc.vector.tensor_tensor(out=m1[:], in0=br[:, 0:CH], in1=TzR[:, s, :], op=ALU.mult), [br] + tabs, [m1])
                V(lambda: nc.vector.tensor_tensor(out=m2[:], in0=bi[:, 0:CH], in1=TzI[:, s, :], op=ALU.mult), [bi] + tabs, [m2])
                V(lambda: nc.vector.tensor_tensor(out=m3[:], in0=bi[:, 0:CH], in1=TzR[:, s, :], op=ALU.mult), [bi] + tabs, [m3])
                V(lambda: nc.vector.tensor_tensor(out=m4[:], in0=br[:, 0:CH], in1=TzI[:, s, :], op=ALU.mult), [br] + tabs, [m4])
                yield
                bre_ = tmp.n(); bim_ = tmp.n()
                Pl(lambda: nc.gpsimd.tensor_tensor(out=bre_[:], in0=m1[:], in1=m2[:], op=ALU.subtract), [m1, m2], [bre_])
                Pl(lambda: nc.gpsimd.tensor_tensor(out=bim_[:], in0=m3[:], in1=m4[:], op=ALU.add), [m3, m4], [bim_])
                yield
                w_r = tmp.n(); w_i = tmp.n()
                magb = mag[:, s:s + 1].to_broadcast([128, CH])
                V(lambda: nc.vector.tensor_tensor_scan(out=w_r[:], data0=magb, data1=bre_[:], initial=carry[:, s, 0:1], op0=ALU.mult, op1=ALU.add),
                  [mag, bre_, (carry, s)], [w_r])
                yield
                V(lambda: nc.vector.tensor_tensor_scan(out=w_i[:], data0=magb, data1=bim_[:], initial=carry[:, s, 1:2], op0=ALU.mult, op1=ALU.add),
                  [mag, bim_, (carry, s)], [w_i])
                yield
                n1 = tmp.n(); n2 = tmp.n()
                Pl(lambda: nc.gpsimd.tensor_tensor(out=n1[:], in0=w_r[:], in1=TC[:, s, :], op=ALU.mult), [w_r] + tabs, [n1])
                Pl(lambda: nc.gpsimd.tensor_tensor(out=n2[:], in0=w_i[:], in1=TS[:, s, :], op=ALU.mult), [w_i] + tabs, [n2])
                Pl(lambda: nc.gpsimd.tensor_tensor(out=x_r[:], in0=n1[:], in1=n2[:], op=ALU.subtract), [n1, n2], [x_r])
                yield
                n3 = tmp.n(); n4 = tmp.n()
                V(lambda: nc.vector.tensor_tensor(out=n3[:], in0=w_r[:], in1=TS[:, s, :], op=ALU.mult), [w_r] + tabs, [n3])
                Pl(lambda: nc.gpsimd.tensor_tensor(out=n4[:], in0=w_i[:], in1=TC[:, s, :], op=ALU.mult), [w_i] + tabs, [n4])
                yield
                V(lambda: nc.vector.tensor_tensor(out=x_i[:], in0=n3[:], in1=n4[:], op=ALU.add), [n3, n4], [x_i])
                yield
                P.op("act", lambda: nc.scalar.copy(out=carry[:, s, 0:1], in_=x_r[:, CH - 1:CH]), reads=[x_r], writes=[(carry, s)])
                P.op("act", lambda: nc.scalar.copy(out=carry[:, s, 1:2], in_=x_i[:, CH - 1:CH]), reads=[x_i], writes=[(carry, s)])
                yield

            it = 0
            for ci in range(NCH):
                cs_ = slice(ci * CH, (ci + 1) * CH)
                uc = ucr.n()
                for tt in range(CH // 128):
                    t = ci * (CH // 128) + tt
                    zt = zin.n()
                    P.dma("sp", zt[:], z[t * 128:(t + 1) * 128, O_SU + half * 512:O_SU + (half + 1) * 512], reads=[("z", t)], writes=[zt])
                    pt = psT.n()
                    P.pe_group([(lambda kk=kk: nc.tensor.transpose(out=pt[:, kk * 128:(kk + 1) * 128], in_=zt[:, kk * 128:(kk + 1) * 128],
                                                                   identity=C["ident"][:])) for kk in range(4)], reads=[zt, C["ident"]], writes=[pt])
                    P.op("act", lambda: nc.scalar.copy(out=uc[:, :, tt * 128:(tt + 1) * 128], in_=pt[:].rearrange("p (a b) -> p a b", a=4)),
                         reads=[pt], writes=[uc])
                for kb in range(4):
                    xs = xsets[it % 2]; it += 1
                    gens = [st_gen(kb * 4 + sl, kb, cs_, uc, tmps[sl], xs[sl][0], xs[sl][1]) for sl in range(4)]
                    while gens:
                        for g_ in list(gens):
                            try:
                                next(g_)
                            except StopIteration:
                                gens.remove(g_)
                    yp = psY.n()
                    fns = []
                    for j in range(4):
                        s_ = kb * 4 + j
                        fns.append(lambda s_=s_, j=j: nc.tensor.matmul(yp[:, 0:CH], lhsT=Cre[:, s_, :], rhs=xs[j][0][:], start=(j == 0), stop=False))
                        fns.append(lambda s_=s_, j=j: nc.tensor.matmul(yp[:, 0:CH], lhsT=Cim[:, s_, :], rhs=xs[j][1][:], start=False, stop=(j == 3)))
                    P.pe_group(fns, reads=[Cre, Cim] + [xs[j][0] for j in range(4)] + [xs[j][1] for j in range(4)], writes=[yp])
                    yv = yo.n(); t1 = y1.n(); t2 = y2.n()
                    V(lambda: nc.vector.scalar_tensor_tensor(out=yv[:], in0=uc[:, kb, :], scalar=dcol[:, kb:kb + 1], in1=yp[:, 0:CH], op0=ALU.mult, op1=ALU.add),
                      [uc, dcol, yp], [yv])
                    Pl(lambda: nc.gpsimd.tensor_tensor(out=t1[:], in0=yv[:], in1=yv[:], op=ALU.mult), [yv], [t1])
                    Pl(lambda: nc.gpsimd.tensor_scalar(out=t1[:], in0=t1[:], scalar1=0.044715, scalar2=1.0, op0=ALU.mult, op1=ALU.add), [t1], [t1])
                    Pl(lambda: nc.gpsimd.tensor_tensor(out=t1[:], in0=t1[:], in1=yv[:], op=ALU.mult), [t1, yv], [t1])
                    P.op("act", lambda: nc.scalar.activation(out=t1[:], in_=t1[:], func=AF.Sigmoid, scale=1.5957691216057308), reads=[t1], writes=[t1])
                    Pl(lambda: nc.gpsimd.tensor_tensor(out=t2[:], in0=t1[:], in1=yv[:], op=ALU.mult), [t1, yv], [t2])
                    r0 = (half * 4 + kb) * 128
                    P.dma("sp", ysT[r0:r0 + 128, cs_], t2[:], reads=[t2], writes=[("ysT", half * 4 + kb)])


def load_T_to_F(P, nc, C, z, col0, ncols, dst_fn, psT, zin, eng_flip=[0]):
    for t in range(SEQ // 128):
        zt = zin.n()
        P.dma("sp", zt[:, 0:ncols], z[t * 128:(t + 1) * 128, col0:col0 + ncols], reads=[("z", t)], writes=[zt])
        pt = psT.n()
        P.pe_group([lambda: nc.tensor.transpose(out=pt[0:ncols, 0:128], in_=zt[:, 0:ncols], identity=C["ident"][:])],
                   reads=[zt, C["ident"]], writes=[pt])
        d, wk = dst_fn(t)
        eng_flip[0] ^= 1
        if eng_flip[0]:
            P.op("act", lambda: nc.scalar.copy(out=d, in_=pt[0:ncols, 0:128]), reads=[pt], writes=[wk])
        else:
            P.op("dve", lambda: nc.vector.tensor_copy(out=d, in_=pt[0:ncols, 0:128]), reads=[pt], writes=[wk])


def gla_stage(P, nc, C, z, ob, par):
    H, DK, DV = 4, 128, 256
    NP = SEQ // 128
    with P.scope():
        maskT = P.sb([128, 128]); rmask = P.sb([128, SEQ]); gw = P.sb([128, DV]); wup = P.sb([16, 512]); bcol = P.sb([128, 4])
        lowT = P.sb([16, SEQ])
        P.dma("sp", maskT[:], par["maskT01"], writes=[maskT])
        P.dma("sp", rmask[:], par["rmask"], writes=[rmask])
        P.dma("sp", gw[:], par["gla_norm_w"].partition_broadcast(128), writes=[gw])
        P.dma("sp", wup[:], par["gla_w_up"], writes=[wup])
        P.dma("sp", bcol[:], par["gla_b_col"], writes=[bcol])
        P.op("dve", lambda: nc.vector.tensor_scalar(out=bcol[:], in0=bcol[:], scalar1=-1.0, scalar2=None, op0=ALU.mult), reads=[bcol], writes=[bcol])
        zin = rot_sb(P, 3, [128, 128]); psT = rot_ps(P, 2, [128, 512])
        load_T_to_F(P, nc, C, z, O_GLOW, 16, lambda t: (lowT[:, t * 128:(t + 1) * 128], lowT), psT, zin)
        la = P.sb([128, SEQ]); eb = P.sb([128, SEQ])
        QT = [P.sb([128, SEQ]) for _ in range(2)]; KT = [P.sb([128, SEQ]) for _ in range(2)]; AL = [P.sb([128, SEQ // 64]) for _ in range(2)]
        PS = rot_ps(P, 6, [128, 512])
        ex = rot_sb(P, 2, [128, 512]); sq = P.sb([128, DV])
        pools = [dict(aq=rot_sb(P, 2, [128, 128]), kt=rot_sb(P, 2, [128, 128]), vt=rot_sb(P, 2, [128, DV]), rt=rot_sb(P, 2, [128, DV]),
                      osb=rot_sb(P, 2, [128, DV]), S=rot_sb(P, 3, [128, DV]), ts=rot_sb(P, 2, [128, DV]), ss=rot_sb(P, 2, [128, 1]), rs=rot_sb(P, 2, [128, 1]))
                 for _ in range(2)]

        def prep(h, qT, kT, al):
            load_T_to_F(P, nc, C, z, O_GQ + h * DK, DK, lambda t: (qT[:, t * 128:(t + 1) * 128], qT), psT, zin)
            load_T_to_F(P, nc, C, z, O_GK + h * DK, DK, lambda t: (kT[:, t * 128:(t + 1) * 128], kT), psT, zin)
            for c in range(SEQ // 512):
                cs_ = slice(c * 512, (c + 1) * 512)
                pl = PS.n(); e1 = ex.n()
                P.pe_group([lambda: nc.tensor.matmul(pl[:], lhsT=wup[:, h * DK:(h + 1) * DK], rhs=lowT[:, cs_], start=True, stop=True)],
                           reads=[wup, lowT], writes=[pl])
                P.op("act", lambda: nc.scalar.activation(out=e1[:], in_=pl[:], func=AF.Exp, bias=bcol[:, h:h + 1], scale=-1.0), reads=[pl, bcol], writes=[e1])
                P.op("act", lambda: nc.scalar.activation(out=la[:, cs_], in_=e1[:], func=AF.Ln, bias=1.0), reads=[e1], writes=[la])
            P.op("dve", lambda: nc.vector.tensor_tensor_scan(out=la[:], data0=rmask[:], data1=la[:], initial=0.0, op0=ALU.mult, op1=ALU.add),
                 reads=[rmask, la], writes=[la])
            P.op("act", lambda: nc.scalar.activation(out=eb[:], in_=la[:], func=AF.Exp, scale=-1.0 / 16.0), reads=[la], writes=[eb])
            P.op("dve", lambda: nc.vector.scalar_tensor_tensor(out=qT[:], in0=qT[:], scalar=float(DK) ** -0.5, in1=eb[:], op0=ALU.mult, op1=ALU.mult),
                 reads=[qT, eb], writes=[qT])
            P.op("pool", lambda: nc.gpsimd.tensor_copy(out=al[:], in_=eb[:].rearrange("p (c i) -> p c i", i=64)[:, :, 63]), reads=[eb], writes=[al])
            P.op("act", lambda: nc.scalar.activation(out=la[:], in_=la[:], func=AF.Exp, scale=1.0 / 16.0), reads=[la], writes=[la])
            P.op("pool", lambda: nc.gpsimd.tensor_tensor(out=kT[:], in0=kT[:], in1=la[:], op=ALU.mult), reads=[kT, la], writes=[kT])

        def pairs(h, qT, kT, al, pl_):
            S = pl_["S"].n()
            P.op("dve", lambda: nc.vector.memset(S[:], 0.0), writes=[S])
            yield
            for pr in range(NP):
                ps_ = slice(pr * 128, (pr + 1) * 128)
                pa = PS.n(); a_sb = pl_["aq"].n()
                P.pe_group([lambda: nc.tensor.matmul(pa[:, 0:128], lhsT=kT[:, ps_], rhs=qT[:, ps_], start=True, stop=True)], reads=[kT, qT], writes=[pa])
                P.op("dve", lambda: nc.vector.tensor_tensor(out=a_sb[:], in0=pa[:, 0:128], in1=maskT[:], op=ALU.mult), reads=[pa, maskT], writes=[a_sb])
                yield
                pk = PS.n(); k_sb = pl_["kt"].n()
                P.pe_group([lambda: nc.tensor.transpose(out=pk[:, 0:128], in_=kT[:, ps_], identity=C["ident"][:])], reads=[kT, C["ident"]], writes=[pk])
                P.op("act", lambda: nc.scalar.copy(out=k_sb[:], in_=pk[:, 0:128]), reads=[pk], writes=[k_sb])
                v = pl_["vt"].n(); r = pl_["rt"].n(); o_sb = pl_["osb"].n()
                P.dma("sp", v[:], z[ps_, O_GV + h * DV:O_GV + (h + 1) * DV], reads=[("z", pr)], writes=[v])
                P.dma("sp", r[:], z[ps_, O_GR + h * DV:O_GR + (h + 1) * DV], reads=[("z", pr)], writes=[r])
                yield
                for cc in range(2):
                    rr = slice(cc * 64, (cc + 1) * 64)
                    c_ = pr * 2 + cc
                    pkv = PS.n()
                    P.pe_group([lambda: nc.tensor.matmul(pkv[:, 0:DV], lhsT=k_sb[rr, :], rhs=v[rr, :], start=True, stop=True)], reads=[k_sb, v], writes=[pkv])
                    ts_ = pl_["ts"].n(); S2 = pl_["S"].n()
                    P.op("dve", lambda: nc.vector.tensor_tensor(out=ts_[:], in0=pkv[:, 0:DV], in1=S[:], op=ALU.add), reads=[pkv, S], writes=[ts_])
                    yield
                    P.op("dve", lambda: nc.vector.tensor_scalar(out=S2[:], in0=ts_[:], scalar1=al[:, c_:c_ + 1], scalar2=None, op0=ALU.mult),
                         reads=[ts_, al], writes=[S2])
                    yield
                    po = PS.n()
                    P.pe_group([lambda: nc.tensor.matmul(po[:, 0:DV], lhsT=qT[:, ps_], rhs=S[:], start=True, stop=False),
                                lambda: nc.tensor.matmul(po[:, 0:DV], lhsT=a_sb[rr, :], rhs=v[rr, :], start=False, stop=True)],
                               reads=[qT, S, a_sb, v], writes=[po])
                    P.op("act", lambda: nc.scalar.copy(out=o_sb[rr, :], in_=po[rr, 0:DV]), reads=[po], writes=[o_sb])
                    yield
                    S = S2
                ss = pl_["ss"].n(); rs = pl_["rs"].n()
                rms_scale(P, nc, o_sb[:], DV, sq, ss, rs, [o_sb])
                yield
                P.op("dve", lambda: nc.vector.scalar_tensor_tensor(out=o_sb[:], in0=o_sb[:], scalar=rs[:, 0:1], in1=gw[:], op0=ALU.mult, op1=ALU.mult),
                     reads=[o_sb, rs, gw], writes=[o_sb])
                P.op("act", lambda: nc.scalar.activation(out=r[:], in_=r[:], func=AF.Silu), reads=[r], writes=[r])
                yield
                P.op("pool", lambda: nc.gpsimd.tensor_tensor(out=o_sb[:], in0=o_sb[:], in1=r[:], op=ALU.mult), reads=[o_sb, r], writes=[o_sb])
                P.dma("sp", ob[ps_, h * DV:(h + 1) * DV], o_sb[:], reads=[o_sb], writes=[("ob", pr)])
                yield

        for hp in range(H // 2):
            for k in range(2):
                prep(2 * hp + k, QT[k], KT[k], AL[k])
            gens = [pairs(2 * hp + k, QT[k], KT[k], AL[k], pools[k]) for k in range(2)]
            while gens:
                for g_ in list(gens):
                    try:
                        next(g_)
                    except StopIteration:
                        gens.remove(g_)


def dn_stage(P, nc, C, z, oa, par, stop=0, nheads=8, npairs=None):
    H, DK = 8, 128
    NP = SEQ // 128
    V = lambda fn, r, w: P.op("dve", fn, reads=r, writes=w)
    A = lambda fn, r, w: P.op("act", fn, reads=r, writes=w)
    G = lambda fn, r, w: P.op("pool", fn, reads=r, writes=w)
    with P.scope():
        maskbig = P.sb([128, 128]); strict = P.sb([128, 128]); nw = P.sb([128, DK]); ones = P.sb([128, 128])
        dpar = P.sb([8, 2]); cwt = P.sb([128, 3 * H, 4])
        P.dma("sp", maskbig[:], par["maskbig"], writes=[maskbig])
        P.dma("sp", strict[:], par["strict01"], writes=[strict])
        P.dma("sp", nw[:], par["dn_norm_w"].partition_broadcast(128), writes=[nw])
        P.dma("sp", dpar[:], par["dn_par"], writes=[dpar])
        P.dma("sp", cwt[:], par["dn_conv_w"].rearrange("p (m j) -> p m j", j=4), writes=[cwt])
        V(lambda: nc.vector.memset(ones[:], 1.0), [], [ones])
        Rg = P.sb([8, SEQ])
        COLb = P.sb([128, NP, 8]); COLg = P.sb([128, NP, 8]); COLl = P.sb([128, NP, 8])
        nbeta = P.sb([128, NP, H]); egc = P.sb([128, NP, H]); bE = P.sb([128, NP, H]); dkc = P.sb([128, NP, H])
        with P.scope():
            Rb = P.sb([8, SEQ]); Rl = P.sb([8, SEQ]); rmask = P.sb([8, SEQ])
            P.dma("sp", rmask[:], par["rmask"][0:8, :], writes=[rmask])
            zt_r = rot_sb(P, 2, [128, 16]); psT = rot_ps(P, 2, [128, 512])
            for t in range(NP):
                zt = zt_r.n()
                P.dma("sp", zt[:], z[t * 128:(t + 1) * 128, O_DB:O_DB + 16], reads=[("z", t)], writes=[zt])
                pt = psT.n()
                P.pe_group([lambda: nc.tensor.transpose(out=pt[0:8, 0:128], in_=zt[:, 0:8], identity=C["ident"][:]),
                            lambda: nc.tensor.transpose(out=pt[0:8, 128:256], in_=zt[:, 8:16], identity=C["ident"][:])],
                           reads=[zt, C["ident"]], writes=[pt])
                V(lambda: nc.vector.tensor_copy(out=Rb[:, t * 128:(t + 1) * 128], in_=pt[0:8, 0:128]), [pt], [Rb])
                V(lambda: nc.vector.tensor_copy(out=Rg[:, t * 128:(t + 1) * 128], in_=pt[0:8, 128:256]), [pt], [Rg])
            A(lambda: nc.scalar.activation(out=Rb[:], in_=Rb[:], func=AF.Sigmoid), [Rb], [Rb])
            A(lambda: nc.scalar.activation(out=Rg[:], in_=Rg[:], func=AF.Exp, bias=dpar[:, 0:1], scale=1.0), [Rg, dpar], [Rg])
            A(lambda: nc.scalar.activation(out=Rg[:], in_=Rg[:], func=AF.Ln, bias=1.0), [Rg], [Rg])
            A(lambda: nc.scalar.activation(out=dpar[:, 1:2], in_=dpar[:, 1:2], func=AF.Exp), [dpar], [dpar])
            V(lambda: nc.vector.tensor_scalar(out=dpar[:, 1:2], in0=dpar[:, 1:2], scalar1=-1.0, scalar2=None, op0=ALU.mult), [dpar], [dpar])
            V(lambda: nc.vector.tensor_scalar(out=Rg[:], in0=Rg[:], scalar1=dpar[:, 1:2], scalar2=None, op0=ALU.mult), [Rg, dpar], [Rg])
            V(lambda: nc.vector.tensor_tensor_scan(out=Rg[:], data0=rmask[0:8, :], data1=Rg[:], initial=0.0, op0=ALU.mult, op1=ALU.add),
              [Rg, rmask], [Rg])
            V(lambda: nc.vector.tensor_copy(out=Rl[:].rearrange("p (c i) -> p c i", i=64),
                                            in_=Rg[:].rearrange("p (c i) -> p c i", i=64)[:, :, 63:64].to_broadcast([8, SEQ // 64, 64])),
              [Rg], [Rl])
            for t in range(NP):
                for (src, dst) in ((Rb, COLb), (Rg, COLg), (Rl, COLl)):
                    pt = psT.n()
                    P.pe_group([lambda: nc.tensor.transpose(out=pt[:, 0:8], in_=src[:, t * 128:(t + 1) * 128], identity=C["ident"][0:8, 0:8])],
                               reads=[src, C["ident"]], writes=[pt])
                    A(lambda: nc.scalar.copy(out=dst[:, t, :], in_=pt[:, 0:8]), [pt], [dst])
            V(lambda: nc.vector.tensor_scalar(out=nbeta[:], in0=COLb[:], scalar1=-1.0, scalar2=None, op0=ALU.mult), [COLb], [nbeta])
            A(lambda: nc.scalar.activation(out=egc[:], in_=COLg[:], func=AF.Exp), [COLg], [egc])
            V(lambda: nc.vector.tensor_tensor(out=bE[:], in0=egc[:], in1=COLb[:], op=ALU.mult), [egc, COLb], [bE])
            V(lambda: nc.vector.tensor_tensor(out=dkc[:], in0=COLl[:], in1=COLg[:], op=ALU.subtract), [COLl, COLg], [dkc])
            A(lambda: nc.scalar.activation(out=dkc[:], in_=dkc[:], func=AF.Exp), [dkc], [dkc])
        if stop == 1:
            return
        xin = P.sb([128, 3 + SEQ]); qT = P.sb([128, SEQ]); kT = P.sb([128, SEQ]); vT = P.sb([128, SEQ])
        gcb = P.sb([128, SEQ]); egb = P.sb([128, SEQ]); sel = P.sb([8, 128])
        V(lambda: nc.vector.memset(xin[:, 0:3], 0.0), [], [xin])
        zin = rot_sb(P, 3, [128, 128])
        PS = rot_ps(P, 8, [128, 512]); psA = PS; psB = PS
        GP = 4
        temps = [rot_sb(P, 12, [128, 128]) for _ in range(GP)]
        persist = [[{n: P.sb([128, 128]) for n in ("AqT", "QdT", "WT", "U", "Kdc", "AT0", "AT1", "B0", "B1")} for _ in range(GP)] for _ in range(2)]
        sm1 = rot_sb(P, 8, [128, 1]); vnr = rot_sb(P, 2, [128, 128])
        Sr = rot_sb(P, 4, [128, 128]); osr = rot_sb(P, 2, [128, 128]); ztr = rot_sb(P, 2, [128, 128]); sq = P.sb([128, 128])
        for h in range(nheads):
            P.dma("sp", sel[:], par["dn_sel"][h], writes=[sel])
            for wi_, dstT in enumerate((qT, kT, vT)):
                m = wi_ * H + h
                load_T_to_F(P, nc, C, z, (O_DQ, O_DK, O_DV)[wi_] + h * DK, DK, lambda t: (xin[:, 3 + t * 128:3 + (t + 1) * 128], xin), psA, zin)
                V(lambda: nc.vector.tensor_scalar(out=dstT[:], in0=xin[:, 0:SEQ], scalar1=cwt[:, m, 0:1], scalar2=None, op0=ALU.mult), [xin, cwt], [dstT])
                for j in range(1, 4):
                    V(lambda: nc.vector.scalar_tensor_tensor(out=dstT[:], in0=xin[:, j:j + SEQ], scalar=cwt[:, m, j:j + 1], in1=dstT[:],
                                                              op0=ALU.mult, op1=ALU.add), [xin, cwt, dstT], [dstT])
                A(lambda: nc.scalar.activation(out=dstT[:], in_=dstT[:], func=AF.Silu), [dstT], [dstT])
                if wi_ < 2:
                    for c in range(SEQ // 512):
                        cs_ = slice(c * 512, (c + 1) * 512)
                        G(lambda: nc.gpsimd.tensor_tensor(out=xin[:, cs_], in0=dstT[:, cs_], in1=dstT[:, cs_], op=ALU.mult), [dstT, xin], [xin])
                        pn = psB.n()
                        P.pe_group([lambda: nc.tensor.matmul(pn[:], lhsT=ones[:], rhs=xin[:, cs_], start=True, stop=True)], reads=[ones, xin], writes=[pn])
                        V(lambda: nc.vector.tensor_scalar(out=xin[:, cs_], in0=pn[:], scalar1=EPS, scalar2=None, op0=ALU.add), [pn, xin], [xin])
                        A(lambda: nc.scalar.activation(out=xin[:, cs_], in_=xin[:, cs_], func=AF.Sqrt), [xin], [xin])
                        V(lambda: nc.vector.reciprocal(out=xin[:, cs_], in_=xin[:, cs_]), [xin], [xin])
                        scl = float(DK) ** -0.5 if wi_ == 0 else 1.0
                        V(lambda: nc.vector.scalar_tensor_tensor(out=dstT[:, cs_], in0=dstT[:, cs_], scalar=scl, in1=xin[:, cs_], op0=ALU.mult, op1=ALU.mult),
                          [dstT, xin], [dstT])
                    V(lambda: nc.vector.memset(xin[:, 0:3], 0.0), [xin], [xin])
            for c in range(SEQ // 512):
                cs_ = slice(c * 512, (c + 1) * 512)
                pn = psB.n()
                P.pe_group([lambda: nc.tensor.matmul(pn[:], lhsT=sel[:], rhs=Rg[:, cs_], start=True, stop=True)], reads=[sel, Rg], writes=[pn])
                V(lambda: nc.vector.tensor_copy(out=gcb[:, cs_], in_=pn[:]), [pn], [gcb])
            A(lambda: nc.scalar.activation(out=egb[:], in_=gcb[:], func=AF.Exp), [gcb], [egb])
            Sh = [Sr.n()]
            V(lambda: nc.vector.memset(Sh[0][:], 0.0), [], [Sh[0]])
            if stop == 2:
                continue
            NPR = NP if npairs is None else npairs

            def phaseA(pr, tp, pp):
                ps_ = slice(pr * 128, (pr + 1) * 128)
                dd = tp.n(); E = tp.n()
                V(lambda: nc.vector.scalar_tensor_tensor(out=dd[:], in0=gcb[:, ps_], scalar=COLg[:, pr, h:h + 1], in1=maskbig[:],
                                                          op0=ALU.subtract, op1=ALU.max), [gcb, COLg, maskbig], [dd])
                yield
                A(lambda: nc.scalar.activation(out=E[:], in_=dd[:], func=AF.Exp, scale=-1.0), [dd], [E])
                yield
                Ln = tp.n(); X = tp.n(); Aqk = tp.n()
                pkk = PS.n()
                P.pe_group([lambda: nc.tensor.matmul(pkk[:, 0:128], lhsT=kT[:, ps_], rhs=kT[:, ps_], start=True, stop=True)], reads=[kT], writes=[pkk])
                V(lambda: nc.vector.scalar_tensor_tensor(out=Ln[:], in0=pkk[:, 0:128], scalar=nbeta[:, pr, h:h + 1], in1=E[:], op0=ALU.mult, op1=ALU.mult),
                  [pkk, nbeta, E], [Ln])
                yield
                G(lambda: nc.gpsimd.tensor_tensor(out=Ln[:], in0=Ln[:], in1=strict[:], op=ALU.mult), [Ln, strict], [Ln])
                yield
                px = PS.n()
                P.pe_group([lambda: nc.tensor.transpose(out=px[:, 0:128], in_=Ln[:], identity=C["ident"][:])], reads=[Ln, C["ident"]], writes=[px])
                A(lambda: nc.scalar.copy(out=X[:], in_=px[:, 0:128]), [px], [X])
                yield
                pqk = PS.n()
                P.pe_group([lambda: nc.tensor.matmul(pqk[:, 0:128], lhsT=qT[:, ps_], rhs=kT[:, ps_], start=True, stop=True)], reads=[qT, kT], writes=[pqk])
                V(lambda: nc.vector.tensor_tensor(out=Aqk[:], in0=pqk[:, 0:128], in1=E[:], op=ALU.mult), [pqk, E], [Aqk])
                yield
                pa = PS.n()
                P.pe_group([lambda: nc.tensor.transpose(out=pa[:, 0:128], in_=Aqk[:], identity=C["ident"][:])], reads=[Aqk, C["ident"]], writes=[pa])
                A(lambda: nc.scalar.copy(out=pp["AqT"][:], in_=pa[:, 0:128]), [pa], [pp["AqT"]])
                yield
                G(lambda: nc.gpsimd.tensor_tensor(out=pp["QdT"][:], in0=qT[:, ps_], in1=egb[:, ps_], op=ALU.mult), [qT, egb], [pp["QdT"]])
                yield
                Pk, PkT = X, Ln
                Rm = tp.n()
                V(lambda: nc.vector.tensor_tensor(out=Rm[:], in0=X[:], in1=C["ident"][:], op=ALU.add), [X, C["ident"]], [Rm])
                yield
                for lev in range(5):
                    last = (lev == 4)
                    pT2 = PS.n()
                    P.pe_group([lambda: nc.tensor.matmul(pT2[:, 0:128], lhsT=Pk[:], rhs=PkT[:], start=True, stop=True)], reads=[Pk, PkT], writes=[pT2])
                    nPT = tp.n()
                    A(lambda: nc.scalar.copy(out=nPT[:], in_=pT2[:, 0:128]), [pT2], [nPT])
                    yield
                    if not last:
                        p2 = PS.n()
                        P.pe_group([lambda: nc.tensor.matmul(p2[:, 0:128], lhsT=PkT[:], rhs=Pk[:], start=True, stop=True)], reads=[Pk, PkT], writes=[p2])
                        nP = tp.n()
                        V(lambda: nc.vector.tensor_copy(out=nP[:], in_=p2[:, 0:128]), [p2], [nP])
                        yield
                    pr_ = PS.n()
                    P.pe_group([lambda: nc.tensor.matmul(pr_[:, 0:128], lhsT=nPT[:], rhs=Rm[:], start=True, stop=True)], reads=[nPT, Rm], writes=[pr_])
                    nR = tp.n()
                    V(lambda: nc.vector.tensor_tensor(out=nR[:], in0=pr_[:, 0:128], in1=Rm[:], op=ALU.add), [pr_, Rm], [nR])
                    yield
                    Rm = nR
                    PkT = nPT
                    if not last:
                        Pk = nP
                KbE = tp.n(); bV = tp.n()
                pk1 = PS.n()
                P.pe_group([lambda: nc.tensor.transpose(out=pk1[:, 0:128], in_=kT[:, ps_], identity=C["ident"][:])], reads=[kT, C["ident"]], writes=[pk1])
                V(lambda: nc.vector.tensor_scalar(out=KbE[:], in0=pk1[:, 0:128], scalar1=bE[:, pr, h:h + 1], scalar2=None, op0=ALU.mult), [pk1, bE], [KbE])
                V(lambda: nc.vector.tensor_scalar(out=pp["Kdc"][:], in0=pk1[:, 0:128], scalar1=dkc[:, pr, h:h + 1], scalar2=None, op0=ALU.mult),
                  [pk1, dkc], [pp["Kdc"]])
                yield
                pk2 = PS.n()
                P.pe_group([lambda: nc.tensor.transpose(out=pk2[:, 0:128], in_=vT[:, ps_], identity=C["ident"][:])], reads=[vT, C["ident"]], writes=[pk2])
                V(lambda: nc.vector.tensor_scalar(out=bV[:], in0=pk2[:, 0:128], scalar1=COLb[:, pr, h:h + 1], scalar2=None, op0=ALU.mult), [pk2, COLb], [bV])
                yield
                pw1 = PS.n()
                P.pe_group([lambda: nc.tensor.matmul(pw1[:, 0:128], lhsT=KbE[:], rhs=Rm[:], start=True, stop=True)], reads=[KbE, Rm], writes=[pw1])
                A(lambda: nc.scalar.copy(out=pp["WT"][:], in_=pw1[:, 0:128]), [pw1], [pp["WT"]])
                yield
                pw2 = PS.n()
                P.pe_group([lambda: nc.tensor.matmul(pw2[:, 0:128], lhsT=Rm[:], rhs=bV[:], start=True, stop=True)], reads=[Rm, bV], writes=[pw2])
                V(lambda: nc.vector.tensor_copy(out=pp["U"][:], in_=pw2[:, 0:128]), [pw2], [pp["U"]])
                yield
                Wt = tp.n()
                pw3 = PS.n()
                P.pe_group([lambda: nc.tensor.matmul(pw3[:, 0:128], lhsT=Rm[:], rhs=KbE[:], start=True, stop=True)], reads=[Rm, KbE], writes=[pw3])
                A(lambda: nc.scalar.copy(out=Wt[:], in_=pw3[:, 0:128]), [pw3], [Wt])
                yield
                for cc in range(2):
                    rr = slice(cc * 64, (cc + 1) * 64)
                    tl = pr * 128 + cc * 64 + 63
                    pm = PS.n()
                    P.pe_group([lambda: nc.tensor.matmul(pm[:, 0:128], lhsT=Wt[rr, :], rhs=pp["Kdc"][rr, :], start=True, stop=True)],
                               reads=[Wt, pp["Kdc"]], writes=[pm])
                    V(lambda: nc.vector.scalar_tensor_tensor(out=pp["AT%d" % cc][:], in0=C["ident"][:], scalar=egb[:, tl:tl + 1], in1=pm[:, 0:128],
                                                              op0=ALU.mult, op1=ALU.subtract), [C["ident"], egb, pm], [pp["AT%d" % cc]])
                    yield
                    pb = PS.n()
                    P.pe_group([lambda: nc.tensor.matmul(pb[:, 0:128], lhsT=pp["Kdc"][rr, :], rhs=pp["U"][rr, :], start=True, stop=True)],
                               reads=[pp["Kdc"], pp["U"]], writes=[pb])
                    A(lambda: nc.scalar.copy(out=pp["B%d" % cc][:], in_=pb[:, 0:128]), [pb], [pp["B%d" % cc]])
                    yield

            def recur(prs, pps):
                for pr, pp in zip(prs, pps):
                    ps_ = slice(pr * 128, (pr + 1) * 128)
                    WT, U, AqT, QdT = pp["WT"], pp["U"], pp["AqT"], pp["QdT"]
                    o_sb = osr.n(); vn = vnr.n()
                    for cc in range(2):
                        rr = slice(cc * 64, (cc + 1) * 64)
                        S = Sh[0]
                        pas = PS.n()
                        P.pe_group([lambda: nc.tensor.matmul(pas[:, 0:128], lhsT=pp["AT%d" % cc][:], rhs=S[:], start=True, stop=True)],
                                   reads=[pp["AT%d" % cc], S], writes=[pas])
                        S2 = Sr.n()
                        V(lambda: nc.vector.tensor_tensor(out=S2[:], in0=pas[:, 0:128], in1=pp["B%d" % cc][:], op=ALU.add), [pas, pp["B%d" % cc]], [S2])
                        yield
                        pws = PS.n()
                        P.pe_group([lambda: nc.tensor.matmul(pws[:, 0:128], lhsT=WT[:], rhs=S[:], start=True, stop=True)], reads=[WT, S], writes=[pws])
                        V(lambda: nc.vector.tensor_tensor(out=vn[rr, :], in0=U[rr, :], in1=pws[rr, 0:128], op=ALU.subtract), [U, pws], [vn])
                        yield
                        po = PS.n()
                        P.pe_group([lambda: nc.tensor.matmul(po[:, 0:128], lhsT=QdT[:], rhs=S[:], start=True, stop=False),
                                    lambda: nc.tensor.matmul(po[:, 0:128], lhsT=AqT[rr, :], rhs=vn[rr, :], start=False, stop=True)],
                                   reads=[QdT, S, AqT, vn], writes=[po])
                        A(lambda: nc.scalar.copy(out=o_sb[rr, :], in_=po[rr, 0:128]), [po], [o_sb])
                        yield
                        Sh[0] = S2
                    zt = ztr.n(); ss = sm1.n(); rs = sm1.n()
                    P.dma("sp", zt[:], z[ps_, O_DZ + h * DK:O_DZ + (h + 1) * DK], reads=[("z", pr)], writes=[zt])
                    rms_scale(P, nc, o_sb[:], DK, sq, ss, rs, [o_sb])
                    yield
                    V(lambda: nc.vector.scalar_tensor_tensor(out=o_sb[:], in0=o_sb[:], scalar=rs[:, 0:1], in1=nw[:], op0=ALU.mult, op1=ALU.mult),
                      [o_sb, rs, nw], [o_sb])
                    A(lambda: nc.scalar.activation(out=zt[:], in_=zt[:], func=AF.Silu), [zt], [zt])
                    yield
                    G(lambda: nc.gpsimd.tensor_tensor(out=o_sb[:], in0=o_sb[:], in1=zt[:], op=ALU.mult), [o_sb, zt], [o_sb])
                    P.dma("sp", oa[ps_, h * DK:(h + 1) * DK], o_sb[:], reads=[o_sb], writes=[("oa", pr)])
                    yield

            groups = [list(range(g0, min(g0 + GP, NPR))) for g0 in range(0, NPR, GP)]
            prev = None
            for gi, grp in enumerate(groups + [None]):
                gens = []
                if grp is not None:
                    pps = [persist[gi % 2][k] for k in range(len(grp))]
                    gens += [phaseA(pr, temps[k], pps[k]) for k, pr in enumerate(grp)]
                if prev is not None:
                    gens.append(prev)
                while gens:
                    for g_ in list(gens):
                        try:
                            next(g_)
                        except StopIteration:
                            gens.remove(g_)
                prev = recur(grp, pps) if grp is not None else None


def make_consts():
    c = {}
    c["ident"] = np.eye(128, dtype=np.float32)
    i = np.arange(128)[:, None]; j = np.arange(128)[None, :]
    same = (i // 64) == (j // 64)
    c["maskbig"] = np.where(same & (i >= j), 0.0, 1e30).astype(np.float32)
    c["strict01"] = (same & (i > j)).astype(np.float32)
    c["maskT01"] = (same & (i <= j)).astype(np.float32)
    r = np.ones((128, SEQ), np.float32); r[:, ::64] = 0.0
    c["rmask"] = r
    c["tau"] = np.tile(np.arange(1, 257, dtype=np.float32)[None, :], (128, 1))
    sel = np.zeros((8, 8, 128), np.float32)
    for h in range(8):
        sel[h, h, :] = 1.0
    c["dn_sel"] = sel
    return c


def prep_layer_params(inp, l):
    f = lambda a: np.ascontiguousarray(a, dtype=np.float32)
    p = {}
    p["dn_norm_w"] = f(inp["dn_norm_w"][l][None, :])
    dpar = np.zeros((8, 2), np.float32)
    dpar[:, 0] = inp["dn_dt_bias"][l]
    dpar[:, 1] = inp["dn_a_log"][l]
    p["dn_par"] = dpar
    p["dn_conv_w"] = f(inp["dn_conv_w"][l].reshape(3, 8, 128, 4).transpose(2, 0, 1, 3).reshape(128, 96))
    p["gla_norm_w"] = f(inp["gla_norm_w"][l][None, :])
    p["gla_w_up"] = f(inp["gla_w_up"][l])
    p["gla_b_col"] = f(inp["gla_b_up"][l].reshape(4, 128).T)
    lay = lambda a: f(a.reshape(32, 2, 64).transpose(1, 2, 0).reshape(128, 32))
    p["s5_lre"] = lay(inp["s5_lam_re"][l]); p["s5_lim"] = lay(inp["s5_lam_im"][l])
    p["s5_lstep"] = lay(np.repeat(inp["s5_log_step"][l][:, None], 64, axis=1))
    p["s5_d"] = f(inp["s5_d"][l].reshape(8, 128).T)
    Bre = np.zeros((32, 128, 128), np.float32); Bim = np.zeros_like(Bre); Cre = np.zeros_like(Bre); Cim = np.zeros_like(Bre)
    for g in range(64):
        st = g // 2; p0 = (g % 2) * 64; k0 = (g % 8) * 16
        Bre[st, k0:k0 + 16, p0:p0 + 64] = inp["s5_b_re"][l][g].T
        Bim[st, k0:k0 + 16, p0:p0 + 64] = inp["s5_b_im"][l][g].T
        Cre[st, p0:p0 + 64, k0:k0 + 16] = inp["s5_c_re"][l][g].T
        Cim[st, p0:p0 + 64, k0:k0 + 16] = inp["s5_c_im"][l][g].T
    p["s5_Bre"] = Bre; p["s5_Bim"] = Bim; p["s5_Cre"] = Cre; p["s5_Cim"] = Cim
    return p


PAR_SHAPES = {
    "ident": [128, 128], "maskbig": [128, 128], "strict01": [128, 128], "maskT01": [128, 128], "rmask": [128, SEQ], "tau": [128, 256],
    "dn_sel": [8, 8, 128], "dn_norm_w": [1, 128], "dn_par": [8, 2], "dn_conv_w": [128, 96], "gla_norm_w": [1, 256], "gla_w_up": [16, 512],
    "gla_b_col": [128, 4], "s5_lre": [128, 32], "s5_lim": [128, 32], "s5_lstep": [128, 32], "s5_d": [128, 8],
    "s5_Bre": [32, 128, 128], "s5_Bim": [32, 128, 128], "s5_Cre": [32, 128, 128], "s5_Cim": [32, 128, 128],
}
CONST_NAMES = ["ident", "maskbig", "strict01", "maskT01", "rmask", "tau", "dn_sel"]


def linear_F(P, nc, srcT, skey, c0, NTOK, K, w, wc0, N, dst, dkey, dr0):
    KT = K // 128
    with P.scope():
        actT = P.sb([128, KT, NTOK], BF16)
        for kt in range(KT):
            P.dma("pool", actT[:, kt, :], srcT[kt * 128:(kt + 1) * 128, c0:c0 + NTOK], reads=[(skey, kt)],
                  writes=[(actT, t) for t in range(NTOK // 128)])
        gemm_T(P, nc, actT, NTOK, K, w, wc0, N, dst, dkey, dr0, 0)


WNAMES = {
    "w_in": [D_MODEL, IN_WIDTH], "w_dn_out": [1024, D_MODEL], "w_gla_out": [1024, D_MODEL], "w_s5_glu": [1024, 2 * D_MODEL],
    "w_mix_out": [D_MODEL, D_MODEL], "w_xa_q": [D_MODEL, 512], "w_xa_kv": [D_MODEL, 1024], "w_xa_out": [512, D_MODEL],
    "w_ffn_up": [D_MODEL, 2 * FFN_H], "w_ffn_down": [FFN_H, D_MODEL], "ffn_conv_w": [128, 132],
    "norm_mix_pre": [1, D_MODEL], "norm_mix_post": [1, D_MODEL], "norm_xa_pre": [1, D_MODEL], "norm_xa_post": [1, D_MODEL],
    "norm_mem": [1, D_MODEL], "norm_ffn_pre": [1, D_MODEL], "norm_ffn_post": [1, D_MODEL],
}
LAYER_PAR = [n for n in PAR_SHAPES if n not in CONST_NAMES]


def build_full(depth=DEPTH):
    nc = bass.Bass("TRN2", target_bir_lowering=False)
    P = Prog(nc)
    D = D_MODEL
    dr = lambda n, s, k="ExternalInput": P.dram(n, s, F32, k)
    x_in = dr("x", [SEQ, D]); mem = dr("mem", [N_MEM, D])
    cst = {n: dr(n, PAR_SHAPES[n]) for n in CONST_NAMES}
    W = [{n: dr("%s_l%d" % (n, l), s) for n, s in WNAMES.items()} for l in range(depth)]
    LP = [{n: dr("%s_l%d" % (n, l), PAR_SHAPES[n]) for n in LAYER_PAR} for l in range(depth)]
    out = dr("out", [SEQ, D], "ExternalOutput")
    I = "Internal"
    z = dr("z", [SEQ, IN_WIDTH], I); oa = dr("oa", [SEQ, 1024], I); ob = dr("ob", [SEQ, 1024], I); ysT = dr("ysT", [1024, SEQ], I)
    ya = dr("ya", [SEQ, D], I); yb = dr("yb", [SEQ, D], I); yg = dr("yg", [SEQ, 2 * D], I); merged = dr("merged", [SEQ, D], I)
    y1 = dr("y1", [SEQ, D], I); x1 = dr("x1", [SEQ, D], I); q = dr("q", [SEQ, 512], I); kv = dr("kv", [N_MEM, 1024], I)
    o = dr("o", [SEQ, 512], I); y2 = dr("y2", [SEQ, D], I); x2h = dr("x2h", [128 + SEQ, D], I); y3 = dr("y3", [SEQ, D], I)
    xmid = dr("xmid", [SEQ, D], I)
    C = load_consts(P, cst["ident"])
    with P.scope():
        zt = P.sb([128, D])
        P.op("dve", lambda: nc.vector.memset(zt[:], 0.0), writes=[zt])
        P.dma("sp", x2h[0:128, :], zt[:], reads=[zt], writes=[("x2h", 0)])
    NT = 2048
    xcur, xkey = x_in, "x"
    for l in range(depth):
        w = W[l]; par = dict(cst); par.update(LP[l])
        xnext, nkey = (out, "out") if l == depth - 1 else (xmid, "xmid")
        for r0 in range(0, SEQ, NT):
            linear_T(P, nc, C, xcur, xkey, r0, NT, D, w["w_in"], 0, IN_WIDTH, z, "z", dr0=r0, norm_g=w["norm_mix_pre"])
        dn_stage(P, nc, C, z, oa, par)
        gla_stage(P, nc, C, z, ob, par)
        s5_stage(P, nc, C, z, ysT, par)
        linear_T(P, nc, C, mem, "mem", 0, N_MEM, D, w["w_xa_kv"], 0, 1024, kv, "kv", norm_g=w["norm_mem"])
        for r0 in range(0, SEQ, NT):
            linear_T(P, nc, C, oa, "oa", r0, NT, 1024, w["w_dn_out"], 0, D, ya, "ya", dr0=r0)
            linear_T(P, nc, C, ob, "ob", r0, NT, 1024, w["w_gla_out"], 0, D, yb, "yb", dr0=r0)
            linear_F(P, nc, ysT, "ysT", r0, NT, 1024, w["w_s5_glu"], 0, 2 * D, yg, "yg", r0)
            merge_stage(P, nc, ya, yb, yg, z, "z", O_GATES, merged, NT, r0=r0)
            linear_T(P, nc, C, merged, "merged", r0, NT, D, w["w_mix_out"], 0, D, y1, "y1", dr0=r0)
            residual_stage(P, nc, xcur, xkey, y1, "y1", w["norm_mix_post"], x1, "x1", NT, xr0=r0, yr0=r0, dr0=r0)
            linear_T(P, nc, C, x1, "x1", r0, NT, D, w["w_xa_q"], 0, 512, q, "q", dr0=r0, norm_g=w["norm_xa_pre"])
            attn_stage(P, nc, C, q, kv, o, NT, r0=r0)
            linear_T(P, nc, C, o, "o", r0, NT, 512, w["w_xa_out"], 0, D, y2, "y2", dr0=r0)
            residual_stage(P, nc, x1, "x1", y2, "y2", w["norm_xa_post"], x2h, "x2h", NT, xr0=r0, yr0=r0, dr0=128 + r0)
        with P.scope():
            ffn_stage(P, nc, C, x2h, w["w_ffn_up"], w["w_ffn_down"], w["ffn_conv_w"], w["norm_ffn_pre"], y3)
        for r0 in range(0, SEQ, NT):
            residual_stage(P, nc, x2h, "x2h", y3, "y3", w["norm_ffn_post"], xnext, nkey, NT, xr0=128 + r0, yr0=r0, dr0=r0)
        xcur, xkey = xnext, nkey
    P.finish([("out", t) for t in range(SEQ // 128)])
    P.close()
    return nc, P


def host_inputs(inp, b, depth=DEPTH):
    f = lambda a: np.ascontiguousarray(a, dtype=np.float32)
    m = dict(make_consts())
    m["x"] = f(inp["x"][b]); m["mem"] = f(inp["mem"][b])
    for l in range(depth):
        for n in WNAMES:
            if n == "ffn_conv_w":
                a = inp[n][l].reshape(44, 128, 3).transpose(1, 0, 2).reshape(128, 132)
            elif n.startswith("norm_"):
                a = inp[n][l][None, :]
            else:
                a = inp[n][l]
            m["%s_l%d" % (n, l)] = f(a)
        for n, a in prep_layer_params(inp, l).items():
            m["%s_l%d" % (n, l)] = a
    return m


_FULL = {}


def kernel(**inputs):
    inp = {k: np.asarray(v) for k, v in inputs.items()}
    if "nc" not in _FULL:
        _FULL["nc"] = build_full()[0]
    nc = _FULL["nc"]
    in_maps = [host_inputs(inp, b) for b in range(BATCH)]
    res = run_bass_kernel_spmd(nc, in_maps, core_ids=list(range(BATCH)))
    out = np.stack([res.results[b]["out"] for b in range(BATCH)], axis=0)
    return out.astype(np.float32)
```

```python
import numpy as np
from contextlib import ExitStack
import concourse.bass as bass
import concourse.mybir as mybir
from concourse.bass_utils import run_bass_kernel_spmd

F32 = mybir.dt.float32
BF16 = mybir.dt.bfloat16
I32 = mybir.dt.int32
ALU = mybir.AluOpType
AF = mybir.ActivationFunctionType
AX = mybir.AxisListType


class Prog:
    NDMA = 40

    def __init__(self, nc):
        self.nc = nc
        self.es = ExitStack()
        self.E = {"pe": nc.tensor, "dve": nc.vector, "act": nc.scalar,
                  "pool": nc.gpsimd, "sp": nc.sync}
        self.sem = {}
        self.cnt = {}
        for e in ("pe", "dve", "act", "pool"):
            self.sem[e] = self.es.enter_context(nc.semaphore("sem_" + e))
            self.cnt[e] = 0
        self.dsem = [self.es.enter_context(nc.semaphore("dsem%d" % i)) for i in range(self.NDMA)]
        self.dcnt = [0] * self.NDMA
        self.drr = 0
        self.seen = {e: {} for e in self.E}
        self.lastw = {}
        self.readers = {}
        self.nid = 0
        self.n_inst = 0

    def sb(self, shape, dtype=F32, name=None):
        self.nid += 1
        name = name or "t%d" % self.nid
        return self.es.enter_context(self.nc.sbuf_tensor(name, list(shape), dtype))

    def ps(self, shape, dtype=F32, name=None):
        self.nid += 1
        name = name or "p%d" % self.nid
        return self.es.enter_context(self.nc.psum_tensor(name, list(shape), dtype))

    def dram(self, name, shape, dtype=F32, kind="Internal"):
        return self.nc.dram_tensor(name, list(shape), dtype, kind=kind).ap()

    @staticmethod
    def _k(k):
        if isinstance(k, str):
            return k
        if isinstance(k, tuple):
            return (Prog._k(k[0]),) + tuple(k[1:])
        return k.name

    def _need(self, eng, reads, writes):
        need = {}
        reads = [self._k(k) for k in reads]
        writes = [self._k(k) for k in writes]

        def add(ev):
            if ev is None:
                return
            skey, val, src = ev
            if src == "pe" and eng == "pe":
                return
            if need.get(skey, 0) < val:
                need[skey] = val

        for k in reads:
            add(self.lastw.get(k))
        for k in writes:
            add(self.lastw.get(k))
            for ev in self.readers.get(k, ()):
                add(ev)
        return need

    def _semobj(self, skey):
        return self.sem[skey] if isinstance(skey, str) else self.dsem[skey]

    def _emit_waits(self, eng, need):
        seen = self.seen[eng]
        for skey, val in need.items():
            if seen.get(skey, 0) >= val:
                continue
            self.E[eng].wait_ge(self._semobj(skey), val)
            self.n_inst += 1
            seen[skey] = val

    def _record(self, ev, reads, writes):
        reads = [self._k(k) for k in reads]
        writes = [self._k(k) for k in writes]
        for k in writes:
            self.lastw[k] = ev
            self.readers[k] = []
        for k in reads:
            if k in writes:
                continue
            lst = self.readers.setdefault(k, [])
            lst[:] = [x for x in lst if x[0] != ev[0]] + [ev]

    def op(self, eng, fn, reads=(), writes=()):
        need = self._need(eng, reads, writes)
        self._emit_waits(eng, need)
        inst = fn()
        inst.then_inc(self.sem[eng], 1)
        self.cnt[eng] += 1
        self.n_inst += 1
        ev = (eng, self.cnt[eng], eng)
        self._record(ev, reads, writes)
        return inst

    def pe_group(self, fns, reads=(), writes=()):
        need = self._need("pe", reads, writes)
        self._emit_waits("pe", need)
        inst = None
        for fn in fns:
            inst = fn()
            self.n_inst += 1
        inst.then_inc(self.sem["pe"], 1)
        self.cnt["pe"] += 1
        ev = ("pe", self.cnt["pe"], "pe")
        self._record(ev, reads, writes)

    def dma(self, q, out, in_, reads=(), writes=(), **kw):
        i = self.drr
        self.drr = (self.drr + 1) % self.NDMA
        need = self._need(q, reads, writes)
        if self.dcnt[i] > 0:
            need[i] = max(need.get(i, 0), 16 * self.dcnt[i])
        self._emit_waits(q, need)
        inst = self.E[q].dma_start(out=out, in_=in_, **kw)
        self.dcnt[i] += 1
        inst.then_inc(self.dsem[i], 16)
        self.n_inst += 1
        ev = (i, 16 * self.dcnt[i], "dma")
        self._record(ev, reads, writes)
        return ev

    def finish(self, keys):
        need = {}
        for k in keys:
            ev = self.lastw.get(self._k(k))
            if ev is not None:
                need[ev[0]] = max(need.get(ev[0], 0), ev[1])
        self._emit_waits("sp", need)

    def close(self):
        self.es.close()


    def barrier(self):
        need = {e: self.cnt[e] for e in self.cnt if self.cnt[e] > 0}
        for i in range(self.NDMA):
            if self.dcnt[i] > 0:
                need[i] = 16 * self.dcnt[i]
        for e in ("sp", "pool", "act", "dve", "pe"):
            self._emit_waits(e, dict(need))

    class _Scope:
        def __init__(self, P):
            self.P = P

        def __enter__(self):
            self.saved = self.P.es
            self.P.es = ExitStack()
            return self

        def __exit__(self, *a):
            self.P.barrier()
            self.P.es.close()
            self.P.es = self.saved
            return False

    def scope(self):
        return Prog._Scope(self)


class Rot:
    def __init__(self, tiles):
        self.t = tiles
        self.i = 0

    def n(self):
        x = self.t[self.i]
        self.i = (self.i + 1) % len(self.t)
        return x


def rot_sb(P, n, shape, dtype=F32):
    return Rot([P.sb(shape, dtype) for _ in range(n)])


def rot_ps(P, n, shape, dtype=F32):
    return Rot([P.ps(shape, dtype) for _ in range(n)])


EPS = 1e-6


def load_consts(P, ident_d):
    C = {}
    C["ident"] = P.sb([128, 128], F32, name="ident_sb")
    P.dma("sp", C["ident"][:], ident_d, writes=[C["ident"]])
    return C


def rms_scale(P, nc, xt, D, sq, ss, rs, reads):
    P.op("act", lambda: nc.scalar.activation(out=sq[:, 0:D], in_=xt, func=AF.Square, accum_out=ss[:]),
         reads=reads, writes=[sq, ss])
    P.op("dve", lambda: nc.vector.tensor_scalar(out=rs[:], in0=ss[:], scalar1=1.0 / D, scalar2=EPS,
                                                op0=ALU.mult, op1=ALU.add), reads=[ss], writes=[rs])
    P.op("act", lambda: nc.scalar.activation(out=rs[:], in_=rs[:], func=AF.Sqrt), reads=[rs], writes=[rs])
    P.op("dve", lambda: nc.vector.reciprocal(out=rs[:], in_=rs[:]), reads=[rs], writes=[rs])


def build_actT(P, nc, C, src, skey, r0, NTOK, K, actT, norm_g=None):
    KT = K // 128
    with P.scope():
        xin = rot_sb(P, 2, [128, K])
        psT = rot_ps(P, 2, [128, 512])
        if norm_g is not None:
            gb = P.sb([128, K])
            P.dma("sp", gb[:], norm_g.partition_broadcast(128), writes=[gb])
            sq = P.sb([128, K])
            ssr = rot_sb(P, 2, [128, 1])
            rsr = rot_sb(P, 2, [128, 1])
            hnr = rot_sb(P, 2, [128, K])
        cnt = 0
        for t in range(NTOK // 128):
            xt = xin.n()
            P.dma("sp", xt[:], src[r0 + t * 128:r0 + (t + 1) * 128, 0:K], reads=[(skey, (r0 // 128) + t)], writes=[xt])
            if norm_g is not None:
                ss = ssr.n(); rs = rsr.n(); hn = hnr.n()
                rms_scale(P, nc, xt[:], K, sq, ss, rs, [xt])
                P.op("dve", lambda: nc.vector.scalar_tensor_tensor(out=hn[:], in0=xt[:], scalar=rs[:, 0:1], in1=gb[:],
                                                                    op0=ALU.mult, op1=ALU.mult),
                     reads=[xt, rs, gb], writes=[hn])
                xt = hn
            for k0 in range(0, KT, 4):
                pt = psT.n()
                P.pe_group([(lambda kk=kk: nc.tensor.transpose(out=pt[:, kk * 128:(kk + 1) * 128],
                                                               in_=xt[:, (k0 + kk) * 128:(k0 + kk + 1) * 128],
                                                               identity=C["ident"][:])) for kk in range(4)],
                           reads=[xt, C["ident"]], writes=[pt])
                e = "dve" if cnt % 2 == 0 else "act"
                cnt += 1
                dst = actT[:, k0:k0 + 4, t * 128:(t + 1) * 128]
                srcv = pt[:].rearrange("p (a b) -> p a b", a=4)
                if e == "dve":
                    P.op("dve", lambda: nc.vector.tensor_copy(out=dst, in_=srcv), reads=[pt], writes=[(actT, t)])
                else:
                    P.op("act", lambda: nc.scalar.copy(out=dst, in_=srcv), reads=[pt], writes=[(actT, t)])


def gemm_T(P, nc, actT, NTOK, K, w, wc0, N, dst, dkey, dr0, dc0, CW=512):
    KT = K // 128
    with P.scope():
        wbr = rot_sb(P, 2, [128, KT, CW], BF16)
        psO = rot_ps(P, 4, [128, CW])
        otr = rot_sb(P, 4, [128, CW])
        cnt = 0
        for c0 in range(0, N, CW):
            cw = min(CW, N - c0)
            wb = wbr.n()
            P.dma("pool", wb[:, :, 0:cw], w[:, wc0 + c0:wc0 + c0 + cw].rearrange("(kt p) n -> p kt n", p=128),
                  writes=[wb])
            for t in range(NTOK // 128):
                po = psO.n()
                P.pe_group([(lambda kt=kt: nc.tensor.matmul(po[:, 0:cw], lhsT=actT[:, kt, t * 128:(t + 1) * 128],
                                                            rhs=wb[:, kt, 0:cw], start=(kt == 0), stop=(kt == KT - 1)))
                            for kt in range(KT)], reads=[(actT, t), wb], writes=[po])
                ot = otr.n()
                if cnt % 2 == 0:
                    P.op("act", lambda: nc.scalar.copy(out=ot[:, 0:cw], in_=po[:, 0:cw]), reads=[po], writes=[ot])
                else:
                    P.op("dve", lambda: nc.vector.tensor_copy(out=ot[:, 0:cw], in_=po[:, 0:cw]), reads=[po], writes=[ot])
                cnt += 1
                P.dma("sp", dst[dr0 + t * 128:dr0 + (t + 1) * 128, dc0 + c0:dc0 + c0 + cw], ot[:, 0:cw],
                      reads=[ot], writes=[(dkey, dr0 // 128 + t)])


def linear_T(P, nc, C, src, skey, sr0, NTOK, K, w, wc0, N, dst, dkey, dr0=0, dc0=0, norm_g=None):
    with P.scope():
        actT = P.sb([128, K // 128, NTOK], BF16)
        build_actT(P, nc, C, src, skey, sr0, NTOK, K, actT, norm_g)
        gemm_T(P, nc, actT, NTOK, K, w, wc0, N, dst, dkey, dr0, dc0)


def residual_stage(P, nc, xsrc, xkey, ysrc, ykey, g, dst, dkey, NTOK, D=2048, xr0=0, yr0=0, dr0=0):
    with P.scope():
        gb = P.sb([128, D])
        P.dma("sp", gb[:], g.partition_broadcast(128), writes=[gb])
        xr = rot_sb(P, 3, [128, D]); yr = rot_sb(P, 3, [128, D]); sq = P.sb([128, D])
        ssr = rot_sb(P, 2, [128, 1]); rsr = rot_sb(P, 2, [128, 1])
        for t in range(NTOK // 128):
            xt = xr.n(); yt = yr.n(); ss = ssr.n(); rs = rsr.n()
            P.dma("sp", xt[:], xsrc[xr0 + t * 128:xr0 + (t + 1) * 128, :], reads=[(xkey, xr0 // 128 + t)], writes=[xt])
            P.dma("sp", yt[:], ysrc[yr0 + t * 128:yr0 + (t + 1) * 128, :], reads=[(ykey, yr0 // 128 + t)], writes=[yt])
            rms_scale(P, nc, yt[:], D, sq, ss, rs, [yt])
            P.op("dve", lambda: nc.vector.scalar_tensor_tensor(out=yt[:], in0=yt[:], scalar=rs[:, 0:1], in1=gb[:],
                                                                op0=ALU.mult, op1=ALU.mult), reads=[yt, rs, gb], writes=[yt])
            P.op("pool", lambda: nc.gpsimd.tensor_tensor(out=xt[:], in0=xt[:], in1=yt[:], op=ALU.add),
                 reads=[xt, yt], writes=[xt])
            P.dma("pool", dst[dr0 + t * 128:dr0 + (t + 1) * 128, :], xt[:], reads=[xt], writes=[(dkey, dr0 // 128 + t)])


D_MODEL = 2048
SEQ = 4096
BATCH = 4
DEPTH = 2
N_MEM = 256
IN_SIZES = (1024, 1024, 1024, 1024, 8, 8, 512, 512, 1024, 16, 1024, 1024, 6144)
IN_OFF = [0]
for _s in IN_SIZES:
    IN_OFF.append(IN_OFF[-1] + _s)
IN_WIDTH = IN_OFF[-1]
(O_DQ, O_DK, O_DV, O_DZ, O_DB, O_DA, O_GQ, O_GK, O_GV, O_GLOW, O_GR, O_SU, O_GATES) = IN_OFF[:13]
FFN_H = 5632
NTOK = 2048

_NC_CACHE = {}


def build_LA():
    nc = bass.Bass("TRN2", target_bir_lowering=False)
    P = Prog(nc)
    x = P.dram("x", [NTOK, D_MODEL], F32, "ExternalInput")
    w = P.dram("w_in", [D_MODEL, IN_WIDTH], F32, "ExternalInput")
    g = P.dram("g", [1, D_MODEL], F32, "ExternalInput")
    ident = P.dram("ident", [128, 128], F32, "ExternalInput")
    z = P.dram("z", [NTOK, IN_WIDTH], F32, "ExternalOutput")
    C = load_consts(P, ident)
    linear_T(P, nc, C, x, "x", 0, NTOK, D_MODEL, w, 0, IN_WIDTH, z, "z", norm_g=g)
    P.finish([("z", t) for t in range(NTOK // 128)])
    P.close()
    return nc


def run(nc, in_maps):
    res = run_bass_kernel_spmd(nc, in_maps, core_ids=list(range(8)))
    return res.results


def merge_stage(P, nc, ya, yb, yg, gates, gkey, gc0, merged, NTOK, r0=0):
    D = D_MODEL
    with P.scope():
        yar = rot_sb(P, 2, [128, D]); ybr = rot_sb(P, 2, [128, D]); ygr = rot_sb(P, 2, [128, 2 * D])
        gtr = rot_sb(P, 2, [128, 3 * D])
        for t in range(NTOK // 128):
            rows = slice(r0 + t * 128, r0 + (t + 1) * 128)
            tt = r0 // 128 + t
            a = yar.n(); b = ybr.n(); gl = ygr.n(); gt = gtr.n()
            P.dma("sp", a[:], ya[rows, :], reads=[("ya", tt)], writes=[a])
            P.dma("sp", b[:], yb[rows, :], reads=[("yb", tt)], writes=[b])
            P.dma("sp", gl[:], yg[rows, :], reads=[("yg", tt)], writes=[gl])
            P.dma("sp", gt[:], gates[rows, gc0:gc0 + 3 * D], reads=[(gkey, tt)], writes=[gt])
            P.op("act", lambda: nc.scalar.activation(out=gt[:], in_=gt[:], func=AF.Sigmoid), reads=[gt], writes=[gt])
            P.op("act", lambda: nc.scalar.activation(out=gl[:, D:2 * D], in_=gl[:, D:2 * D], func=AF.Sigmoid),
                 reads=[gl], writes=[gl])
            P.op("dve", lambda: nc.vector.tensor_tensor(out=a[:], in0=a[:], in1=gt[:, 0:D], op=ALU.mult),
                 reads=[a, gt], writes=[a])
            P.op("pool", lambda: nc.gpsimd.tensor_tensor(out=b[:], in0=b[:], in1=gt[:, D:2 * D], op=ALU.mult),
                 reads=[b, gt], writes=[b])
            P.op("dve", lambda: nc.vector.tensor_tensor(out=gl[:, 0:D], in0=gl[:, 0:D], in1=gl[:, D:2 * D], op=ALU.mult),
                 reads=[gl], writes=[gl])
            P.op("pool", lambda: nc.gpsimd.tensor_tensor(out=a[:], in0=a[:], in1=b[:], op=ALU.add),
                 reads=[a, b], writes=[a])
            P.op("dve", lambda: nc.vector.tensor_tensor(out=gl[:, 0:D], in0=gl[:, 0:D], in1=gt[:, 2 * D:3 * D], op=ALU.mult),
                 reads=[gl, gt], writes=[gl])
            P.op("dve", lambda: nc.vector.tensor_tensor(out=a[:], in0=a[:], in1=gl[:, 0:D], op=ALU.add),
                 reads=[a, gl], writes=[a])
            P.dma("pool", merged[rows, :], a[:], reads=[a], writes=[("merged", tt)])


def attn_stage(P, nc, C, q, kv, o, NTOK, r0=0):
    H = 4
    sc = 128.0 ** -0.5
    with P.scope():
        kvt = P.sb([128, 2, 1024])
        P.dma("sp", kvt[:], kv.rearrange("(a p) n -> p a n", p=128), reads=[("kv", 0), ("kv", 1)], writes=[kvt])
        kT = P.sb([128, H, 256])
        psT = rot_ps(P, 2, [128, 512])
        psS = [P.ps([128, 512]) for _ in range(H)]
        psO = rot_ps(P, 2, [128, 512])
        for h in range(H):
            pt = psT.n()
            P.pe_group([(lambda a=a: nc.tensor.transpose(out=pt[:, a * 128:(a + 1) * 128],
                                                         in_=kvt[:, a, h * 128:(h + 1) * 128], identity=C["ident"][:]))
                        for a in range(2)], reads=[kvt, C["ident"]], writes=[pt])
            P.op("dve", lambda: nc.vector.tensor_copy(out=kT[:, h, :], in_=pt[:, 0:256]), reads=[pt], writes=[kT])
        qr = rot_sb(P, 2, [128, 512]); qTr = rot_sb(P, 2, [128, 512]); otr = rot_sb(P, 2, [128, 512])
        hp = [dict(p=rot_sb(P, 2, [128, 256]), pT=rot_sb(P, 2, [128, 256]), mx=rot_sb(P, 2, [128, 1]), sm=rot_sb(P, 2, [128, 1])) for _ in range(H)]

        def head(h, qT, ot):
            s_ps = psS[h]
            mx = hp[h]["mx"].n(); sm = hp[h]["sm"].n(); p = hp[h]["p"].n(); pT = hp[h]["pT"].n()
            P.pe_group([lambda: nc.tensor.matmul(s_ps[:, 0:256], lhsT=qT[:, h * 128:(h + 1) * 128], rhs=kT[:, h, :],
                                                 start=True, stop=True)], reads=[qT, kT], writes=[s_ps])
            P.op("dve", lambda: nc.vector.tensor_reduce(out=mx[:], in_=s_ps[:, 0:256], axis=AX.X, op=ALU.max),
                 reads=[s_ps], writes=[mx])
            yield
            P.op("dve", lambda: nc.vector.tensor_scalar(out=mx[:], in0=mx[:], scalar1=-sc, scalar2=None, op0=ALU.mult),
                 reads=[mx], writes=[mx])
            yield
            P.op("act", lambda: nc.scalar.activation(out=p[:], in_=s_ps[:, 0:256], func=AF.Exp, bias=mx[:, 0:1], scale=sc,
                                                     accum_out=sm[:]), reads=[s_ps, mx], writes=[p, sm])
            yield
            P.op("dve", lambda: nc.vector.reciprocal(out=sm[:], in_=sm[:]), reads=[sm], writes=[sm])
            pt2 = psT.n()
            P.pe_group([(lambda a=a: nc.tensor.transpose(out=pt2[:, a * 128:(a + 1) * 128], in_=p[:, a * 128:(a + 1) * 128],
                                                         identity=C["ident"][:])) for a in range(2)],
                       reads=[p, C["ident"]], writes=[pt2])
            P.op("dve", lambda: nc.vector.tensor_copy(out=pT[:], in_=pt2[:, 0:256]), reads=[pt2], writes=[pT])
            yield
            o_ps = psO.n()
            P.pe_group([(lambda a=a: nc.tensor.matmul(o_ps[:, 0:128], lhsT=pT[:, a * 128:(a + 1) * 128],
                                                      rhs=kvt[:, a, 512 + h * 128:512 + (h + 1) * 128],
                                                      start=(a == 0), stop=(a == 1))) for a in range(2)],
                       reads=[pT, kvt], writes=[o_ps])
            P.op("dve", lambda: nc.vector.tensor_scalar(out=ot[:, h * 128:(h + 1) * 128], in0=o_ps[:, 0:128], scalar1=sm[:, 0:1], scalar2=None,
                                                        op0=ALU.mult), reads=[o_ps, sm], writes=[(ot, h)])
            yield

        for t in range(NTOK // 128):
            rows = slice(r0 + t * 128, r0 + (t + 1) * 128)
            tt = r0 // 128 + t
            qt = qr.n(); qT = qTr.n(); ot = otr.n()
            P.dma("sp", qt[:], q[rows, :], reads=[("q", tt)], writes=[qt])
            pt = psT.n()
            P.pe_group([(lambda h=h: nc.tensor.transpose(out=pt[:, h * 128:(h + 1) * 128], in_=qt[:, h * 128:(h + 1) * 128],
                                                         identity=C["ident"][:])) for h in range(H)],
                       reads=[qt, C["ident"]], writes=[pt])
            P.op("act", lambda: nc.scalar.copy(out=qT[:], in_=pt[:]), reads=[pt], writes=[qT])
            gens = [head(h, qT, ot) for h in range(H)]
            while gens:
                for g_ in list(gens):
                    try:
                        next(g_)
                    except StopIteration:
                        gens.remove(g_)
            P.dma("pool", o[rows, :], ot[:], reads=[(ot, h) for h in range(H)], writes=[("o", tt)])


def build_LC():
    nc = bass.Bass("TRN2", target_bir_lowering=False)
    P = Prog(nc)
    D = D_MODEL
    dr = lambda n, s, k="ExternalInput": P.dram(n, s, F32, k)
    x = dr("x", [NTOK, D]); oa = dr("oa", [NTOK, 1024]); ob = dr("ob", [NTOK, 1024]); ys = dr("ys", [NTOK, 1024])
    gates = dr("gates", [NTOK, 3 * D]); mem = dr("mem", [N_MEM, D])
    w_dn = dr("w_dn_out", [1024, D]); w_gla = dr("w_gla_out", [1024, D]); w_glu = dr("w_s5_glu", [1024, 2 * D])
    w_mix = dr("w_mix_out", [D, D]); w_q = dr("w_xa_q", [D, 512]); w_kv = dr("w_xa_kv", [D, 1024]); w_o = dr("w_xa_out", [512, D])
    g_post = dr("g_mix_post", [1, D]); g_xpre = dr("g_xa_pre", [1, D]); g_xpost = dr("g_xa_post", [1, D]); g_mem = dr("g_mem", [1, D])
    ident = dr("ident", [128, 128])
    x2 = dr("x2", [NTOK, D], "ExternalOutput")
    ya = dr("ya", [NTOK, D], "Internal"); yb = dr("yb", [NTOK, D], "Internal"); yg = dr("yg", [NTOK, 2 * D], "Internal")
    merged = dr("merged", [NTOK, D], "Internal"); y1 = dr("y1", [NTOK, D], "Internal"); x1 = dr("x1", [NTOK, D], "Internal")
    q = dr("q", [NTOK, 512], "Internal"); kv = dr("kv", [N_MEM, 1024], "Internal"); o = dr("o", [NTOK, 512], "Internal")
    y2 = dr("y2", [NTOK, D], "Internal")
    C = load_consts(P, ident)
    linear_T(P, nc, C, oa, "oa", 0, NTOK, 1024, w_dn, 0, D, ya, "ya")
    linear_T(P, nc, C, ob, "ob", 0, NTOK, 1024, w_gla, 0, D, yb, "yb")
    linear_T(P, nc, C, ys, "ys", 0, NTOK, 1024, w_glu, 0, 2 * D, yg, "yg")
    merge_stage(P, nc, ya, yb, yg, gates, merged, NTOK)
    linear_T(P, nc, C, merged, "merged", 0, NTOK, D, w_mix, 0, D, y1, "y1")
    residual_stage(P, nc, x, "x", y1, "y1", g_post, x1, "x1", NTOK)
    linear_T(P, nc, C, x1, "x1", 0, NTOK, D, w_q, 0, 512, q, "q", norm_g=g_xpre)
    linear_T(P, nc, C, mem, "mem", 0, N_MEM, D, w_kv, 0, 1024, kv, "kv", norm_g=g_mem)
    attn_stage(P, nc, C, q, kv, o, NTOK)
    linear_T(P, nc, C, o, "o", 0, NTOK, 512, w_o, 0, D, y2, "y2")
    residual_stage(P, nc, x1, "x1", y2, "y2", g_xpost, x2, "x2", NTOK)
    P.finish([("x2", t) for t in range(NTOK // 128)])
    P.close()
    return nc


def gelu_tanh_mul(P, nc, cbuf, up_ps, out_ap, tmp1, tmp2, wkey):
    P.op("pool", lambda: nc.gpsimd.tensor_tensor(out=tmp1[:], in0=cbuf[:], in1=cbuf[:], op=ALU.mult),
         reads=[cbuf], writes=[tmp1])
    P.op("dve", lambda: nc.vector.tensor_scalar(out=tmp1[:], in0=tmp1[:], scalar1=0.044715, scalar2=1.0,
                                                op0=ALU.mult, op1=ALU.add), reads=[tmp1], writes=[tmp1])
    P.op("pool", lambda: nc.gpsimd.tensor_tensor(out=tmp1[:], in0=tmp1[:], in1=cbuf[:], op=ALU.mult),
         reads=[tmp1, cbuf], writes=[tmp1])
    P.op("act", lambda: nc.scalar.activation(out=tmp1[:], in_=tmp1[:], func=AF.Sigmoid, scale=1.5957691216057308),
         reads=[tmp1], writes=[tmp1])
    P.op("dve", lambda: nc.vector.tensor_tensor(out=tmp2[:], in0=cbuf[:], in1=up_ps, op=ALU.mult),
         reads=[cbuf, wkey], writes=[tmp2])
    P.op("dve", lambda: nc.vector.tensor_tensor(out=out_ap, in0=tmp1[:], in1=tmp2[:], op=ALU.mult),
         reads=[tmp1, tmp2], writes=[wkey + "_o"])


def ffn_stage(P, nc, C, x2h, w_up, w_dn, cwd, g_pre, y3):
    D = D_MODEL
    FT = FFN_H // 128
    cw = P.sb([128, FT, 3])
    P.dma("sp", cw[:], cwd.rearrange("p (f j) -> p f j", j=3), writes=[cw])
    HALF = 1024
    for half in range(SEQ // HALF):
        with P.scope():
            AT = P.sb([128, FT, HALF], BF16)
            with P.scope():
                hT = P.sb([128, 16, 128 + HALF], BF16)
                build_actT(P, nc, C, x2h, "x2h", half * HALF, 128 + HALF, D, hT, norm_g=g_pre)
                wgr = rot_sb(P, 2, [128, 16, 256], BF16); wur = rot_sb(P, 2, [128, 16, 256], BF16)
                psH = rot_ps(P, 1, [128, 512]); psG = rot_ps(P, 2, [128, 512]); psU = rot_ps(P, 2, [128, 512])
                gbr = rot_sb(P, 2, [128, 514]); cbr = rot_sb(P, 2, [128, 512]); t1r = rot_sb(P, 2, [128, 512]); t2r = rot_sb(P, 2, [128, 512])
                for fc in range(FFN_H // 256):
                    wg = wgr.n(); wu = wur.n()
                    P.dma("pool", wg[:], w_up[:, fc * 256:(fc + 1) * 256].rearrange("(kt p) n -> p kt n", p=128), writes=[wg])
                    P.dma("pool", wu[:], w_up[:, FFN_H + fc * 256:FFN_H + (fc + 1) * 256].rearrange("(kt p) n -> p kt n", p=128), writes=[wu])
                    for sub in range(2):
                        f = fc * 2 + sub
                        ph = psH.n()
                        P.pe_group([(lambda kt=kt: nc.tensor.matmul(ph[:, 0:128], lhsT=wg[:, kt, sub * 128:(sub + 1) * 128], rhs=hT[:, kt, 0:128],
                                                                    start=(kt == 0), stop=(kt == 15))) for kt in range(16)],
                                   reads=[wg, (hT, 0)], writes=[ph])
                        gb = gbr.n()
                        P.op("act", lambda: nc.scalar.copy(out=gb[:, 0:2], in_=ph[:, 126:128]), reads=[ph], writes=[gb])
                        for tb in range(HALF // 512):
                            c0 = 128 + tb * 512
                            pg = psG.n(); pu = psU.n()
                            rk = [(hT, (c0 // 128) + i) for i in range(4)]
                            P.pe_group([(lambda kt=kt: nc.tensor.matmul(pg[:], lhsT=wg[:, kt, sub * 128:(sub + 1) * 128], rhs=hT[:, kt, c0:c0 + 512],
                                                                        start=(kt == 0), stop=(kt == 15))) for kt in range(16)],
                                       reads=[wg] + rk, writes=[pg])
                            P.pe_group([(lambda kt=kt: nc.tensor.matmul(pu[:], lhsT=wu[:, kt, sub * 128:(sub + 1) * 128], rhs=hT[:, kt, c0:c0 + 512],
                                                                        start=(kt == 0), stop=(kt == 15))) for kt in range(16)],
                                       reads=[wu] + rk, writes=[pu])
                            P.op("act", lambda: nc.scalar.copy(out=gb[:, 2:514], in_=pg[:]), reads=[pg], writes=[gb])
                            cb = cbr.n(); t1 = t1r.n(); t2 = t2r.n()
                            P.op("dve", lambda: nc.vector.tensor_scalar(out=cb[:], in0=gb[:, 0:512], scalar1=cw[:, f, 0:1], scalar2=None, op0=ALU.mult),
                                 reads=[gb, cw], writes=[cb])
                            P.op("dve", lambda: nc.vector.scalar_tensor_tensor(out=cb[:], in0=gb[:, 1:513], scalar=cw[:, f, 1:2], in1=cb[:],
                                                                                op0=ALU.mult, op1=ALU.add), reads=[gb, cw, cb], writes=[cb])
                            P.op("dve", lambda: nc.vector.scalar_tensor_tensor(out=cb[:], in0=gb[:, 2:514], scalar=cw[:, f, 2:3], in1=cb[:],
                                                                                op0=ALU.mult, op1=ALU.add), reads=[gb, cw, cb], writes=[cb])
                            if tb + 1 < HALF // 512:
                                gb2 = gbr.n()
                                P.op("act", lambda: nc.scalar.copy(out=gb2[:, 0:2], in_=gb[:, 512:514]), reads=[gb], writes=[gb2])
                            P.op("pool", lambda: nc.gpsimd.tensor_tensor(out=t1[:], in0=cb[:], in1=cb[:], op=ALU.mult), reads=[cb], writes=[t1])
                            P.op("dve", lambda: nc.vector.tensor_scalar(out=t1[:], in0=t1[:], scalar1=0.044715, scalar2=1.0, op0=ALU.mult, op1=ALU.add),
                                 reads=[t1], writes=[t1])
                            P.op("pool", lambda: nc.gpsimd.tensor_tensor(out=t1[:], in0=t1[:], in1=cb[:], op=ALU.mult), reads=[t1, cb], writes=[t1])
                            P.op("act", lambda: nc.scalar.activation(out=t1[:], in_=t1[:], func=AF.Sigmoid, scale=1.5957691216057308),
                                 reads=[t1], writes=[t1])
                            P.op("dve", lambda: nc.vector.tensor_tensor(out=t2[:], in0=cb[:], in1=pu[:], op=ALU.mult), reads=[cb, pu], writes=[t2])
                            P.op("dve", lambda: nc.vector.tensor_tensor(out=AT[:, f, tb * 512:(tb + 1) * 512], in0=t1[:], in1=t2[:], op=ALU.mult),
                                 reads=[t1, t2], writes=[(AT, f)])
                            if tb + 1 < HALF // 512:
                                gb = gb2
            with P.scope():
                wdr = rot_sb(P, 2, [128, FT, 256], BF16)
                psO = rot_ps(P, 3, [128, 512]); otr = rot_sb(P, 3, [128, 256])
                akeys = [(AT, f) for f in range(FT)]
                cnt = 0
                for cbk in range(D // 256):
                    wd = wdr.n()
                    P.dma("pool", wd[:], w_dn[:, cbk * 256:(cbk + 1) * 256].rearrange("(ft p) n -> p ft n", p=128), writes=[wd])
                    for t in range(HALF // 128):
                        po = psO.n()
                        P.pe_group([(lambda f=f: nc.tensor.matmul(po[:, 0:256], lhsT=AT[:, f, t * 128:(t + 1) * 128], rhs=wd[:, f, :],
                                                                  start=(f == 0), stop=(f == FT - 1))) for f in range(FT)],
                                   reads=[wd] + akeys, writes=[po])
                        ot = otr.n()
                        if cnt % 2 == 0:
                            P.op("act", lambda: nc.scalar.copy(out=ot[:], in_=po[:, 0:256]), reads=[po], writes=[ot])
                        else:
                            P.op("dve", lambda: nc.vector.tensor_copy(out=ot[:], in_=po[:, 0:256]), reads=[po], writes=[ot])
                        cnt += 1
                        r0 = half * HALF + t * 128
                        P.dma("sp", y3[r0:r0 + 128, cbk * 256:(cbk + 1) * 256], ot[:], reads=[ot], writes=[("y3", r0 // 128)])


TWO_PI = 6.283185307179586


def sin_reduced(P, nc, out, ang, shift, shape, tmpa, tmpk, tmpf):
    P.op("dve", lambda: nc.vector.tensor_scalar(out=tmpa, in0=ang, scalar1=float(shift), scalar2=None, op0=ALU.add),
         reads=["sr_ang"], writes=["sr_a"])
    P.op("dve", lambda: nc.vector.tensor_scalar(out=tmpk, in0=tmpa, scalar1=1.0 / TWO_PI, scalar2=None, op0=ALU.mult),
         reads=["sr_a"], writes=["sr_k"])
    P.op("dve", lambda: nc.vector.tensor_copy(out=tmpf, in_=tmpk), reads=["sr_k"], writes=["sr_f"])
    P.op("dve", lambda: nc.vector.scalar_tensor_tensor(out=tmpa, in0=tmpf, scalar=-TWO_PI, in1=tmpa, op0=ALU.mult, op1=ALU.add),
         reads=["sr_f", "sr_a"], writes=["sr_a"])
    P.op("dve", lambda: nc.vector.tensor_scalar(out=tmpf, in0=tmpa, scalar1=3.141592653589793, scalar2=None, op0=ALU.is_gt),
         reads=["sr_a"], writes=["sr_f"])
    P.op("dve", lambda: nc.vector.scalar_tensor_tensor(out=tmpa, in0=tmpf, scalar=-TWO_PI, in1=tmpa, op0=ALU.mult, op1=ALU.add),
         reads=["sr_f", "sr_a"], writes=["sr_a"])
    P.op("dve", lambda: nc.vector.tensor_scalar(out=tmpf, in0=tmpa, scalar1=-3.141592653589793, scalar2=None, op0=ALU.is_lt),
         reads=["sr_a"], writes=["sr_f"])
    P.op("dve", lambda: nc.vector.scalar_tensor_tensor(out=tmpa, in0=tmpf, scalar=TWO_PI, in1=tmpa, op0=ALU.mult, op1=ALU.add),
         reads=["sr_f", "sr_a"], writes=["sr_a"])
    P.op("act", lambda: nc.scalar.activation(out=out, in_=tmpa, func=AF.Sin), reads=["sr_a"], writes=["sr_out"])


def s5_stage(P, nc, C, z, ysT, par):
    CH = 256
    NCH = SEQ // CH
    for half in range(2):
        with P.scope():
            NS = 16
            s0 = half * NS
            lre = P.sb([128, NS]); lim = P.sb([128, NS]); dt = P.sb([128, NS]); mag = P.sb([128, NS]); th = P.sb([128, NS])
            zr = P.sb([128, NS]); zi = P.sb([128, NS]); ar = P.sb([128, NS]); ai = P.sb([128, NS]); den = P.sb([128, NS]); tq = P.sb([128, NS])
            dcol = P.sb([128, 4])
            Bre = P.sb([128, NS, 128]); Bim = P.sb([128, NS, 128]); Cre = P.sb([128, NS, 128]); Cim = P.sb([128, NS, 128])
            tau = P.sb([128, CH])
            P.dma("sp", lre[:], par["s5_lre"][:, s0:s0 + NS], writes=[lre])
            P.dma("sp", lim[:], par["s5_lim"][:, s0:s0 + NS], writes=[lim])
            P.dma("sp", dt[:], par["s5_lstep"][:, s0:s0 + NS], writes=[dt])
            P.dma("sp", dcol[:], par["s5_d"][:, half * 4:(half + 1) * 4], writes=[dcol])
            P.dma("sp", tau[:], par["tau"], writes=[tau])
            for nm, tl in (("s5_Bre", Bre), ("s5_Bim", Bim), ("s5_Cre", Cre), ("s5_Cim", Cim)):
                P.dma("sp", tl[:], par[nm][s0:s0 + NS].rearrange("s p m -> p s m"), writes=[tl])
            P.op("pool", lambda: nc.gpsimd.tensor_scalar(out=Cim[:], in0=Cim[:], scalar1=-1.0, scalar2=None, op0=ALU.mult), reads=[Cim], writes=[Cim])
            V = lambda fn, r, w: P.op("dve", fn, reads=r, writes=w)
            P.op("act", lambda: nc.scalar.activation(out=dt[:], in_=dt[:], func=AF.Exp), reads=[dt], writes=[dt])
            V(lambda: nc.vector.tensor_scalar(out=lre[:], in0=lre[:], scalar1=-1e-4, scalar2=None, op0=ALU.min), [lre], [lre])
            V(lambda: nc.vector.tensor_tensor(out=mag[:], in0=lre[:], in1=dt[:], op=ALU.mult), [lre, dt], [mag])
            P.op("act", lambda: nc.scalar.activation(out=mag[:], in_=mag[:], func=AF.Exp), reads=[mag], writes=[mag])
            V(lambda: nc.vector.tensor_tensor(out=th[:], in0=lim[:], in1=dt[:], op=ALU.mult), [lim, dt], [th])
            ta = P.sb([128, CH]); tk = P.sb([128, CH], I32); tf = P.sb([128, CH])
            sn = P.sb([128, NS]); cs = P.sb([128, NS])
            P._record(("dve", P.cnt["dve"], "dve"), [], ["sr_ang"])
            sin_reduced(P, nc, sn[:], th[:], 0.0, None, ta[:, 0:NS], tk[:, 0:NS], tf[:, 0:NS])
            sin_reduced(P, nc, cs[:], th[:], 1.5707963267948966, None, ta[:, 0:NS], tk[:, 0:NS], tf[:, 0:NS])
            V(lambda: nc.vector.tensor_tensor(out=ar[:], in0=mag[:], in1=cs[:], op=ALU.mult), [mag, "sr_out"], [ar])
            V(lambda: nc.vector.tensor_tensor(out=ai[:], in0=mag[:], in1=sn[:], op=ALU.mult), [mag, "sr_out"], [ai])
            V(lambda: nc.vector.tensor_scalar(out=ar[:], in0=ar[:], scalar1=-1.0, scalar2=None, op0=ALU.add), [ar], [ar])
            V(lambda: nc.vector.tensor_tensor(out=den[:], in0=lre[:], in1=lre[:], op=ALU.mult), [lre], [den])
            V(lambda: nc.vector.tensor_tensor(out=tq[:], in0=lim[:], in1=lim[:], op=ALU.mult), [lim], [tq])
            V(lambda: nc.vector.tensor_tensor(out=den[:], in0=den[:], in1=tq[:], op=ALU.add), [den, tq], [den])
            V(lambda: nc.vector.reciprocal(out=den[:], in_=den[:]), [den], [den])
            V(lambda: nc.vector.tensor_tensor(out=zr[:], in0=ar[:], in1=lre[:], op=ALU.mult), [ar, lre], [zr])
            V(lambda: nc.vector.tensor_tensor(out=tq[:], in0=ai[:], in1=lim[:], op=ALU.mult), [ai, lim], [tq])
            V(lambda: nc.vector.tensor_tensor(out=zr[:], in0=zr[:], in1=tq[:], op=ALU.add), [zr, tq], [zr])
            V(lambda: nc.vector.tensor_tensor(out=zr[:], in0=zr[:], in1=den[:], op=ALU.mult), [zr, den], [zr])
            V(lambda: nc.vector.tensor_tensor(out=zi[:], in0=ai[:], in1=lre[:], op=ALU.mult), [ai, lre], [zi])
            V(lambda: nc.vector.tensor_tensor(out=tq[:], in0=ar[:], in1=lim[:], op=ALU.mult), [ar, lim], [tq])
            V(lambda: nc.vector.tensor_tensor(out=zi[:], in0=zi[:], in1=tq[:], op=ALU.subtract), [zi, tq], [zi])
            V(lambda: nc.vector.tensor_tensor(out=zi[:], in0=zi[:], in1=den[:], op=ALU.mult), [zi, den], [zi])
            TC = P.sb([128, NS, CH]); TS = P.sb([128, NS, CH]); TzR = P.sb([128, NS, CH]); TzI = P.sb([128, NS, CH])
            ang = P.sb([128, CH])
            for s in range(NS):
                V(lambda: nc.vector.tensor_scalar(out=ang[:], in0=tau[:], scalar1=th[:, s:s + 1], scalar2=None, op0=ALU.mult), [tau, th, "sr_a"], ["sr_ang"])
                sin_reduced(P, nc, TS[:, s, :], ang[:], 0.0, None, ta[:], tk[:], tf[:])
                sin_reduced(P, nc, TC[:, s, :], ang[:], 1.5707963267948966, None, ta[:], tk[:], tf[:])
                V(lambda: nc.vector.tensor_scalar(out=ta[:], in0=TS[:, s, :], scalar1=zi[:, s:s + 1], scalar2=None, op0=ALU.mult), ["sr_out", zi], ["sr_a"])
                V(lambda: nc.vector.scalar_tensor_tensor(out=TzR[:, s, :], in0=TC[:, s, :], scalar=zr[:, s:s + 1], in1=ta[:], op0=ALU.mult, op1=ALU.add),
                  ["sr_a", zr, "sr_out"], [TzR])
                V(lambda: nc.vector.tensor_scalar(out=ta[:], in0=TS[:, s, :], scalar1=zr[:, s:s + 1], scalar2=None, op0=ALU.mult), ["sr_out", zr], ["sr_a"])
                V(lambda: nc.vector.scalar_tensor_tensor(out=TzI[:, s, :], in0=TC[:, s, :], scalar=zi[:, s:s + 1], in1=ta[:], op0=ALU.mult, op1=ALU.subtract),
                  ["sr_a", zi, "sr_out"], [TzI])
            tabs = ["sr_out", TzR, TzI]
            carry = P.sb([128, NS, 2])
            P.op("dve", lambda: nc.vector.memset(carry[:], 0.0), writes=[(carry, s_) for s_ in range(NS)])
            psB = rot_ps(P, 4, [128, 512]); psY = rot_ps(P, 2, [128, 512]); psT = rot_ps(P, 2, [128, 512])
            tmps = [rot_sb(P, 8, [128, CH]) for _ in range(4)]
            xsets = [[(P.sb([128, CH]), P.sb([128, CH])) for _ in range(4)] for _ in range(2)]
            ucr = rot_sb(P, 2, [128, 4, CH]); zin = rot_sb(P, 2, [128, 512])
            yo = rot_sb(P, 2, [128, CH]); y1 = rot_sb(P, 2, [128, CH]); y2 = rot_sb(P, 2, [128, CH])
            Pl = lambda fn, r, w: P.op("pool", fn, reads=r, writes=w)

            def st_gen(s, kb, cs_, uc, tmp, x_r, x_i):
                br = psB.n(); bi = psB.n()
                P.pe_group([lambda: nc.tensor.matmul(br[:, 0:CH], lhsT=Bre[:, s, :], rhs=uc[:, kb, :], start=True, stop=True)], reads=[Bre, uc], writes=[br])
                P.pe_group([lambda: nc.tensor.matmul(bi[:, 0:CH], lhsT=Bim[:, s, :], rhs=uc[:, kb, :], start=True, stop=True)], reads=[Bim, uc], writes=[bi])
                m1 = tmp.n(); m2 = tmp.n(); m3 = tmp.n(); m4 = tmp.n()
                V(lambda: nc.vector.tensor_tensor(out=m1[:], in0=br[:, 0:CH], in1=TzR[:, s, :], op=ALU.mult), [br] + tabs, [m1])
                V(lambda: nc.vector.tensor_tensor(out=m2[:], in0=bi[:, 0:CH], in1=TzI[:, s, :], op=ALU.mult), [bi] + tabs, [m2])
                V(lambda: nc.vector.tensor_tensor(out=m3[:], in0=bi[:, 0:CH], in1=TzR[:, s, :], op=ALU.mult), [bi] + tabs, [m3])
                V(lambda: nc.vector.tensor_tensor(out=m4[:], in0=br[:, 0:CH], in1=TzI[:, s, :], op=ALU.mult), [br] + tabs, [m4])
                yield
                bre_ = tmp.n(); bim_ = tmp.n()
                Pl(lambda: nc.gpsimd.tensor_tensor(out=bre_[:], in0=m1[:], in1=m2[:], op=ALU.subtract), [m1, m2], [bre_])
                Pl(lambda: nc.gpsimd.tensor_tensor(out=bim_[:], in0=m3[:], in1=m4[:], op=ALU.add), [m3, m4], [bim_])
                yield
                w_r = tmp.n(); w_i = tmp.n()
                magb = mag[:, s:s + 1].to_broadcast([128, CH])
                V(lambda: nc.vector.tensor_tensor_scan(out=w_r[:], data0=magb, data1=bre_[:], initial=carry[:, s, 0:1], op0=ALU.mult, op1=ALU.add),
                  [mag, bre_, (carry, s)], [w_r])
                yield
                V(lambda: nc.vector.tensor_tensor_scan(out=w_i[:], data0=magb, data1=bim_[:], initial=carry[:, s, 1:2], op0=ALU.mult, op1=ALU.add),
                  [mag, bim_, (carry, s)], [w_i])
                yield
                n1 = tmp.n(); n2 = tmp.n()
                Pl(lambda: nc.gpsimd.tensor_tensor(out=n1[:], in0=w_r[:], in1=TC[:, s, :], op=ALU.mult), [w_r] + tabs, [n1])
                Pl(lambda: nc.gpsimd.tensor_tensor(out=n2[:], in0=w_i[:], in1=TS[:, s, :], op=ALU.mult), [w_i] + tabs, [n2])
                Pl(lambda: nc.gpsimd.tensor_tensor(out=x_r[:], in0=n1[:], in1=n2[:], op=ALU.subtract), [n1, n2], [x_r])
                yield
                n3 = tmp.n(); n4 = tmp.n()
                V(lambda: nc.vector.tensor_tensor(out=n3[:], in0=w_r[:], in1=TS[:, s, :], op=ALU.mult), [w_r] + tabs, [n3])
                Pl(lambda: nc.gpsimd.tensor_tensor(out=n4[:], in0=w_i[:], in1=TC[:, s, :], op=ALU.mult), [w_i] + tabs, [n4])
                yield
                V(lambda: nc.vector.tensor_tensor(out=x_i[:], in0=n3[:], in1=n4[:], op=ALU.add), [n3, n4], [x_i])
                yield
                P.op("act", lambda: nc.scalar.copy(out=carry[:, s, 0:1], in_=x_r[:, CH - 1:CH]), reads=[x_r], writes=[(carry, s)])
                P.op("act", lambda: nc.scalar.copy(out=carry[:, s, 1:2], in_=x_i[:, CH - 1:CH]), reads=[x_i], writes=[(carry, s)])
                yield

            it = 0
            for ci in range(NCH):
                cs_ = slice(ci * CH, (ci + 1) * CH)
                uc = ucr.n()
                for tt in range(CH // 128):
                    t = ci * (CH // 128) + tt
                    zt = zin.n()
                    P.dma("sp", zt[:], z[t * 128:(t + 1) * 128, O_SU + half * 512:O_SU + (half + 1) * 512], reads=[("z", t)], writes=[zt])
                    pt = psT.n()
                    P.pe_group([(lambda kk=kk: nc.tensor.transpose(out=pt[:, kk * 128:(kk + 1) * 128], in_=zt[:, kk * 128:(kk + 1) * 128],
                                                                   identity=C["ident"][:])) for kk in range(4)], reads=[zt, C["ident"]], writes=[pt])
                    P.op("act", lambda: nc.scalar.copy(out=uc[:, :, tt * 128:(tt + 1) * 128], in_=pt[:].rearrange("p (a b) -> p a b", a=4)),
                         reads=[pt], writes=[uc])
                for kb in range(4):
                    xs = xsets[it % 2]; it += 1
                    gens = [st_gen(kb * 4 + sl, kb, cs_, uc, tmps[sl], xs[sl][0], xs[sl][1]) for sl in range(4)]
                    while gens:
                        for g_ in list(gens):
                            try:
                                next(g_)
                            except StopIteration:
                                gens.remove(g_)
                    yp = psY.n()
                    fns = []
                    for j in range(4):
                        s_ = kb * 4 + j
                        fns.append(lambda s_=s_, j=j: nc.tensor.matmul(yp[:, 0:CH], lhsT=Cre[:, s_, :], rhs=xs[j][0][:], start=(j == 0), stop=False))
                        fns.append(lambda s_=s_, j=j: nc.tensor.matmul(yp[:, 0:CH], lhsT=Cim[:, s_, :], rhs=xs[j][1][:], start=False, stop=(j == 3)))
                    P.pe_group(fns, reads=[Cre, Cim] + [xs[j][0] for j in range(4)] + [xs[j][1] for j in range(4)], writes=[yp])
                    yv = yo.n(); t1 = y1.n(); t2 = y2.n()
                    V(lambda: nc.vector.scalar_tensor_tensor(out=yv[:], in0=uc[:, kb, :], scalar=dcol[:, kb:kb + 1], in1=yp[:, 0:CH], op0=ALU.mult, op1=ALU.add),
                      [uc, dcol, yp], [yv])
                    Pl(lambda: nc.gpsimd.tensor_tensor(out=t1[:], in0=yv[:], in1=yv[:], op=ALU.mult), [yv], [t1])
                    Pl(lambda: nc.gpsimd.tensor_scalar(out=t1[:], in0=t1[:], scalar1=0.044715, scalar2=1.0, op0=ALU.mult, op1=ALU.add), [t1], [t1])
                    Pl(lambda: nc.gpsimd.tensor_tensor(out=t1[:], in0=t1[:], in1=yv[:], op=ALU.mult), [t1, yv], [t1])
                    P.op("act", lambda: nc.scalar.activation(out=t1[:], in_=t1[:], func=AF.Sigmoid, scale=1.5957691216057308), reads=[t1], writes=[t1])
                    Pl(lambda: nc.gpsimd.tensor_tensor(out=t2[:], in0=t1[:], in1=yv[:], op=ALU.mult), [t1, yv], [t2])
                    r0 = (half * 4 + kb) * 128
                    P.dma("act", ysT[r0:r0 + 128, cs_], t2[:], reads=[t2], writes=[("ysT", half * 4 + kb)])


def load_T_to_F(P, nc, C, z, col0, ncols, dst_fn, psT, zin, eng_flip=[0]):
    for t in range(SEQ // 128):
        zt = zin.n()
        P.dma("sp", zt[:, 0:ncols], z[t * 128:(t + 1) * 128, col0:col0 + ncols], reads=[("z", t)], writes=[zt])
        pt = psT.n()
        P.pe_group([lambda: nc.tensor.transpose(out=pt[0:ncols, 0:128], in_=zt[:, 0:ncols], identity=C["ident"][:])],
                   reads=[zt, C["ident"]], writes=[pt])
        d, wk = dst_fn(t)
        eng_flip[0] ^= 1
        if eng_flip[0]:
            P.op("act", lambda: nc.scalar.copy(out=d, in_=pt[0:ncols, 0:128]), reads=[pt], writes=[wk])
        else:
            P.op("dve", lambda: nc.vector.tensor_copy(out=d, in_=pt[0:ncols, 0:128]), reads=[pt], writes=[wk])


def gla_stage(P, nc, C, z, ob, par):
    H, DK, DV = 4, 128, 256
    NP = SEQ // 128
    with P.scope():
        maskT = P.sb([128, 128]); rmask = P.sb([128, SEQ]); gw = P.sb([128, DV]); wup = P.sb([16, 512]); bcol = P.sb([128, 4])
        lowT = P.sb([16, SEQ])
        P.dma("sp", maskT[:], par["maskT01"], writes=[maskT])
        P.dma("sp", rmask[:], par["rmask"], writes=[rmask])
        P.dma("sp", gw[:], par["gla_norm_w"].partition_broadcast(128), writes=[gw])
        P.dma("sp", wup[:], par["gla_w_up"], writes=[wup])
        P.dma("sp", bcol[:], par["gla_b_col"], writes=[bcol])
        P.op("dve", lambda: nc.vector.tensor_scalar(out=bcol[:], in0=bcol[:], scalar1=-1.0, scalar2=None, op0=ALU.mult), reads=[bcol], writes=[bcol])
        zin = rot_sb(P, 3, [128, 128]); psT = rot_ps(P, 2, [128, 512])
        load_T_to_F(P, nc, C, z, O_GLOW, 16, lambda t: (lowT[:, t * 128:(t + 1) * 128], lowT), psT, zin)
        la = P.sb([128, SEQ]); eb = P.sb([128, SEQ])
        QT = [P.sb([128, SEQ]) for _ in range(2)]; KT = [P.sb([128, SEQ]) for _ in range(2)]; AL = [P.sb([128, SEQ // 64]) for _ in range(2)]
        PS = rot_ps(P, 6, [128, 512])
        ex = rot_sb(P, 2, [128, 512]); sq = P.sb([128, DV])
        pools = [dict(aq=rot_sb(P, 2, [128, 128]), kt=rot_sb(P, 2, [128, 128]), vt=rot_sb(P, 2, [128, DV]), rt=rot_sb(P, 2, [128, DV]),
                      osb=rot_sb(P, 2, [128, DV]), S=rot_sb(P, 3, [128, DV]), ts=rot_sb(P, 2, [128, DV]), ss=rot_sb(P, 2, [128, 1]), rs=rot_sb(P, 2, [128, 1]))
                 for _ in range(2)]

        def prep(h, qT, kT, al):
            load_T_to_F(P, nc, C, z, O_GQ + h * DK, DK, lambda t: (qT[:, t * 128:(t + 1) * 128], qT), psT, zin)
            load_T_to_F(P, nc, C, z, O_GK + h * DK, DK, lambda t: (kT[:, t * 128:(t + 1) * 128], kT), psT, zin)
            for c in range(SEQ // 512):
                cs_ = slice(c * 512, (c + 1) * 512)
                pl = PS.n(); e1 = ex.n()
                P.pe_group([lambda: nc.tensor.matmul(pl[:], lhsT=wup[:, h * DK:(h + 1) * DK], rhs=lowT[:, cs_], start=True, stop=True)],
                           reads=[wup, lowT], writes=[pl])
                P.op("act", lambda: nc.scalar.activation(out=e1[:], in_=pl[:], func=AF.Exp, bias=bcol[:, h:h + 1], scale=-1.0), reads=[pl, bcol], writes=[e1])
                P.op("act", lambda: nc.scalar.activation(out=la[:, cs_], in_=e1[:], func=AF.Ln, bias=1.0), reads=[e1], writes=[la])
            P.op("dve", lambda: nc.vector.tensor_tensor_scan(out=la[:], data0=rmask[:], data1=la[:], initial=0.0, op0=ALU.mult, op1=ALU.add),
                 reads=[rmask, la], writes=[la])
            P.op("act", lambda: nc.scalar.activation(out=eb[:], in_=la[:], func=AF.Exp, scale=-1.0 / 16.0), reads=[la], writes=[eb])
            P.op("dve", lambda: nc.vector.scalar_tensor_tensor(out=qT[:], in0=qT[:], scalar=float(DK) ** -0.5, in1=eb[:], op0=ALU.mult, op1=ALU.mult),
                 reads=[qT, eb], writes=[qT])
            P.op("pool", lambda: nc.gpsimd.tensor_copy(out=al[:], in_=eb[:].rearrange("p (c i) -> p c i", i=64)[:, :, 63]), reads=[eb], writes=[al])
            P.op("act", lambda: nc.scalar.activation(out=la[:], in_=la[:], func=AF.Exp, scale=1.0 / 16.0), reads=[la], writes=[la])
            P.op("pool", lambda: nc.gpsimd.tensor_tensor(out=kT[:], in0=kT[:], in1=la[:], op=ALU.mult), reads=[kT, la], writes=[kT])

        def pairs(h, qT, kT, al, pl_):
            S = pl_["S"].n()
            P.op("dve", lambda: nc.vector.memset(S[:], 0.0), writes=[S])
            yield
            for pr in range(NP):
                ps_ = slice(pr * 128, (pr + 1) * 128)
                pa = PS.n(); a_sb = pl_["aq"].n()
                P.pe_group([lambda: nc.tensor.matmul(pa[:, 0:128], lhsT=kT[:, ps_], rhs=qT[:, ps_], start=True, stop=True)], reads=[kT, qT], writes=[pa])
                P.op("dve", lambda: nc.vector.tensor_tensor(out=a_sb[:], in0=pa[:, 0:128], in1=maskT[:], op=ALU.mult), reads=[pa, maskT], writes=[a_sb])
                yield
                pk = PS.n(); k_sb = pl_["kt"].n()
                P.pe_group([lambda: nc.tensor.transpose(out=pk[:, 0:128], in_=kT[:, ps_], identity=C["ident"][:])], reads=[kT, C["ident"]], writes=[pk])
                P.op("act", lambda: nc.scalar.copy(out=k_sb[:], in_=pk[:, 0:128]), reads=[pk], writes=[k_sb])
                v = pl_["vt"].n(); r = pl_["rt"].n(); o_sb = pl_["osb"].n()
                P.dma("sp", v[:], z[ps_, O_GV + h * DV:O_GV + (h + 1) * DV], reads=[("z", pr)], writes=[v])
                P.dma("sp", r[:], z[ps_, O_GR + h * DV:O_GR + (h + 1) * DV], reads=[("z", pr)], writes=[r])
                yield
                for cc in range(2):
                    rr = slice(cc * 64, (cc + 1) * 64)
                    c_ = pr * 2 + cc
                    pkv = PS.n()
                    P.pe_group([lambda: nc.tensor.matmul(pkv[:, 0:DV], lhsT=k_sb[rr, :], rhs=v[rr, :], start=True, stop=True)], reads=[k_sb, v], writes=[pkv])
                    ts_ = pl_["ts"].n(); S2 = pl_["S"].n()
                    P.op("dve", lambda: nc.vector.tensor_tensor(out=ts_[:], in0=pkv[:, 0:DV], in1=S[:], op=ALU.add), reads=[pkv, S], writes=[ts_])
                    yield
                    P.op("dve", lambda: nc.vector.tensor_scalar(out=S2[:], in0=ts_[:], scalar1=al[:, c_:c_ + 1], scalar2=None, op0=ALU.mult),
                         reads=[ts_, al], writes=[S2])
                    yield
                    po = PS.n()
                    P.pe_group([lambda: nc.tensor.matmul(po[:, 0:DV], lhsT=qT[:, ps_], rhs=S[:], start=True, stop=False),
                                lambda: nc.tensor.matmul(po[:, 0:DV], lhsT=a_sb[rr, :], rhs=v[rr, :], start=False, stop=True)],
                               reads=[qT, S, a_sb, v], writes=[po])
                    P.op("act", lambda: nc.scalar.copy(out=o_sb[rr, :], in_=po[rr, 0:DV]), reads=[po], writes=[o_sb])
                    yield
                    S = S2
                ss = pl_["ss"].n(); rs = pl_["rs"].n()
                rms_scale(P, nc, o_sb[:], DV, sq, ss, rs, [o_sb])
                yield
                P.op("dve", lambda: nc.vector.scalar_tensor_tensor(out=o_sb[:], in0=o_sb[:], scalar=rs[:, 0:1], in1=gw[:], op0=ALU.mult, op1=ALU.mult),
                     reads=[o_sb, rs, gw], writes=[o_sb])
                P.op("act", lambda: nc.scalar.activation(out=r[:], in_=r[:], func=AF.Silu), reads=[r], writes=[r])
                yield
                P.op("pool", lambda: nc.gpsimd.tensor_tensor(out=o_sb[:], in0=o_sb[:], in1=r[:], op=ALU.mult), reads=[o_sb, r], writes=[o_sb])
                P.dma("pool", ob[ps_, h * DV:(h + 1) * DV], o_sb[:], reads=[o_sb], writes=[("ob", pr)])
                yield

        for hp in range(H // 2):
            for k in range(2):
                prep(2 * hp + k, QT[k], KT[k], AL[k])
            gens = [pairs(2 * hp + k, QT[k], KT[k], AL[k], pools[k]) for k in range(2)]
            while gens:
                for g_ in list(gens):
                    try:
                        next(g_)
                    except StopIteration:
                        gens.remove(g_)


def dn_stage(P, nc, C, z, oa, par, stop=0, nheads=8, npairs=None):
    H, DK = 8, 128
    NP = SEQ // 128
    V = lambda fn, r, w: P.op("dve", fn, reads=r, writes=w)
    A = lambda fn, r, w: P.op("act", fn, reads=r, writes=w)
    G = lambda fn, r, w: P.op("pool", fn, reads=r, writes=w)
    with P.scope():
        maskbig = P.sb([128, 128]); strict = P.sb([128, 128]); nw = P.sb([128, DK]); ones = P.sb([128, 128])
        dpar = P.sb([8, 2]); cwt = P.sb([128, 3 * H, 4])
        P.dma("sp", maskbig[:], par["maskbig"], writes=[maskbig])
        P.dma("sp", strict[:], par["strict01"], writes=[strict])
        P.dma("sp", nw[:], par["dn_norm_w"].partition_broadcast(128), writes=[nw])
        P.dma("sp", dpar[:], par["dn_par"], writes=[dpar])
        P.dma("sp", cwt[:], par["dn_conv_w"].rearrange("p (m j) -> p m j", j=4), writes=[cwt])
        V(lambda: nc.vector.memset(ones[:], 1.0), [], [ones])
        Rg = P.sb([8, SEQ])
        COLb = P.sb([128, NP, 8]); COLg = P.sb([128, NP, 8]); COLl = P.sb([128, NP, 8])
        nbeta = P.sb([128, NP, H]); egc = P.sb([128, NP, H]); bE = P.sb([128, NP, H]); dkc = P.sb([128, NP, H])
        with P.scope():
            Rb = P.sb([8, SEQ]); Rl = P.sb([8, SEQ]); rmask = P.sb([8, SEQ])
            P.dma("sp", rmask[:], par["rmask"][0:8, :], writes=[rmask])
            zt_r = rot_sb(P, 2, [128, 16]); psT = rot_ps(P, 2, [128, 512])
            for t in range(NP):
                zt = zt_r.n()
                P.dma("sp", zt[:], z[t * 128:(t + 1) * 128, O_DB:O_DB + 16], reads=[("z", t)], writes=[zt])
                pt = psT.n()
                P.pe_group([lambda: nc.tensor.transpose(out=pt[0:8, 0:128], in_=zt[:, 0:8], identity=C["ident"][:]),
                            lambda: nc.tensor.transpose(out=pt[0:8, 128:256], in_=zt[:, 8:16], identity=C["ident"][:])],
                           reads=[zt, C["ident"]], writes=[pt])
                V(lambda: nc.vector.tensor_copy(out=Rb[:, t * 128:(t + 1) * 128], in_=pt[0:8, 0:128]), [pt], [Rb])
                V(lambda: nc.vector.tensor_copy(out=Rg[:, t * 128:(t + 1) * 128], in_=pt[0:8, 128:256]), [pt], [Rg])
            A(lambda: nc.scalar.activation(out=Rb[:], in_=Rb[:], func=AF.Sigmoid), [Rb], [Rb])
            A(lambda: nc.scalar.activation(out=Rg[:], in_=Rg[:], func=AF.Exp, bias=dpar[:, 0:1], scale=1.0), [Rg, dpar], [Rg])
            A(lambda: nc.scalar.activation(out=Rg[:], in_=Rg[:], func=AF.Ln, bias=1.0), [Rg], [Rg])
            A(lambda: nc.scalar.activation(out=dpar[:, 1:2], in_=dpar[:, 1:2], func=AF.Exp), [dpar], [dpar])
            V(lambda: nc.vector.tensor_scalar(out=dpar[:, 1:2], in0=dpar[:, 1:2], scalar1=-1.0, scalar2=None, op0=ALU.mult), [dpar], [dpar])
            V(lambda: nc.vector.tensor_scalar(out=Rg[:], in0=Rg[:], scalar1=dpar[:, 1:2], scalar2=None, op0=ALU.mult), [Rg, dpar], [Rg])
            V(lambda: nc.vector.tensor_tensor_scan(out=Rg[:], data0=rmask[0:8, :], data1=Rg[:], initial=0.0, op0=ALU.mult, op1=ALU.add),
              [Rg, rmask], [Rg])
            V(lambda: nc.vector.tensor_copy(out=Rl[:].rearrange("p (c i) -> p c i", i=64),
                                            in_=Rg[:].rearrange("p (c i) -> p c i", i=64)[:, :, 63:64].to_broadcast([8, SEQ // 64, 64])),
              [Rg], [Rl])
            for t in range(NP):
                for (src, dst) in ((Rb, COLb), (Rg, COLg), (Rl, COLl)):
                    pt = psT.n()
                    P.pe_group([lambda: nc.tensor.transpose(out=pt[:, 0:8], in_=src[:, t * 128:(t + 1) * 128], identity=C["ident"][0:8, 0:8])],
                               reads=[src, C["ident"]], writes=[pt])
                    A(lambda: nc.scalar.copy(out=dst[:, t, :], in_=pt[:, 0:8]), [pt], [dst])
            V(lambda: nc.vector.tensor_scalar(out=nbeta[:], in0=COLb[:], scalar1=-1.0, scalar2=None, op0=ALU.mult), [COLb], [nbeta])
            A(lambda: nc.scalar.activation(out=egc[:], in_=COLg[:], func=AF.Exp), [COLg], [egc])
            V(lambda: nc.vector.tensor_tensor(out=bE[:], in0=egc[:], in1=COLb[:], op=ALU.mult), [egc, COLb], [bE])
            V(lambda: nc.vector.tensor_tensor(out=dkc[:], in0=COLl[:], in1=COLg[:], op=ALU.subtract), [COLl, COLg], [dkc])
            A(lambda: nc.scalar.activation(out=dkc[:], in_=dkc[:], func=AF.Exp), [dkc], [dkc])
        if stop == 1:
            return
        xin = P.sb([128, 3 + SEQ]); qT = P.sb([128, SEQ]); kT = P.sb([128, SEQ]); vT = P.sb([128, SEQ])
        gcb = P.sb([128, SEQ]); egb = P.sb([128, SEQ]); sel = P.sb([8, 128])
        V(lambda: nc.vector.memset(xin[:, 0:3], 0.0), [], [xin])
        zin = rot_sb(P, 3, [128, 128])
        PS = rot_ps(P, 8, [128, 512]); psA = PS; psB = PS
        GP = 4
        temps = [rot_sb(P, 12, [128, 128]) for _ in range(GP)]
        persist = [[{n: P.sb([128, 128]) for n in ("AqT", "QdT", "WT", "U", "Kdc", "AT0", "AT1", "B0", "B1")} for _ in range(GP)] for _ in range(2)]
        sm1 = rot_sb(P, 8, [128, 1]); vnr = rot_sb(P, 2, [128, 128])
        Sr = rot_sb(P, 4, [128, 128]); osr = rot_sb(P, 2, [128, 128]); ztr = rot_sb(P, 2, [128, 128]); sq = P.sb([128, 128])
        for h in range(nheads):
            P.dma("sp", sel[:], par["dn_sel"][h], writes=[sel])
            for wi_, dstT in enumerate((qT, kT, vT)):
                m = wi_ * H + h
                load_T_to_F(P, nc, C, z, (O_DQ, O_DK, O_DV)[wi_] + h * DK, DK, lambda t: (xin[:, 3 + t * 128:3 + (t + 1) * 128], xin), psA, zin)
                V(lambda: nc.vector.tensor_scalar(out=dstT[:], in0=xin[:, 0:SEQ], scalar1=cwt[:, m, 0:1], scalar2=None, op0=ALU.mult), [xin, cwt], [dstT])
                for j in range(1, 4):
                    V(lambda: nc.vector.scalar_tensor_tensor(out=dstT[:], in0=xin[:, j:j + SEQ], scalar=cwt[:, m, j:j + 1], in1=dstT[:],
                                                              op0=ALU.mult, op1=ALU.add), [xin, cwt, dstT], [dstT])
                A(lambda: nc.scalar.activation(out=dstT[:], in_=dstT[:], func=AF.Silu), [dstT], [dstT])
                if wi_ < 2:
                    for c in range(SEQ // 512):
                        cs_ = slice(c * 512, (c + 1) * 512)
                        G(lambda: nc.gpsimd.tensor_tensor(out=xin[:, cs_], in0=dstT[:, cs_], in1=dstT[:, cs_], op=ALU.mult), [dstT, xin], [xin])
                        pn = psB.n()
                        P.pe_group([lambda: nc.tensor.matmul(pn[:], lhsT=ones[:], rhs=xin[:, cs_], start=True, stop=True)], reads=[ones, xin], writes=[pn])
                        V(lambda: nc.vector.tensor_scalar(out=xin[:, cs_], in0=pn[:], scalar1=EPS, scalar2=None, op0=ALU.add), [pn, xin], [xin])
                        A(lambda: nc.scalar.activation(out=xin[:, cs_], in_=xin[:, cs_], func=AF.Sqrt), [xin], [xin])
                        V(lambda: nc.vector.reciprocal(out=xin[:, cs_], in_=xin[:, cs_]), [xin], [xin])
                        scl = float(DK) ** -0.5 if wi_ == 0 else 1.0
                        V(lambda: nc.vector.scalar_tensor_tensor(out=dstT[:, cs_], in0=dstT[:, cs_], scalar=scl, in1=xin[:, cs_], op0=ALU.mult, op1=ALU.mult),
                          [dstT, xin], [dstT])
                    V(lambda: nc.vector.memset(xin[:, 0:3], 0.0), [xin], [xin])
            for c in range(SEQ // 512):
                cs_ = slice(c * 512, (c + 1) * 512)
                pn = psB.n()
                P.pe_group([lambda: nc.tensor.matmul(pn[:], lhsT=sel[:], rhs=Rg[:, cs_], start=True, stop=True)], reads=[sel, Rg], writes=[pn])
                V(lambda: nc.vector.tensor_copy(out=gcb[:, cs_], in_=pn[:]), [pn], [gcb])
            A(lambda: nc.scalar.activation(out=egb[:], in_=gcb[:], func=AF.Exp), [gcb], [egb])
            Sh = [Sr.n()]
            V(lambda: nc.vector.memset(Sh[0][:], 0.0), [], [Sh[0]])
            if stop == 2:
                continue
            NPR = NP if npairs is None else npairs

            def phaseA(pr, tp, pp):
                ps_ = slice(pr * 128, (pr + 1) * 128)
                dd = tp.n(); E = tp.n()
                V(lambda: nc.vector.scalar_tensor_tensor(out=dd[:], in0=gcb[:, ps_], scalar=COLg[:, pr, h:h + 1], in1=maskbig[:],
                                                          op0=ALU.subtract, op1=ALU.max), [gcb, COLg, maskbig], [dd])
                yield
                A(lambda: nc.scalar.activation(out=E[:], in_=dd[:], func=AF.Exp, scale=-1.0), [dd], [E])
                yield
                Ln = tp.n(); X = tp.n(); Aqk = tp.n()
                pkk = PS.n()
                P.pe_group([lambda: nc.tensor.matmul(pkk[:, 0:128], lhsT=kT[:, ps_], rhs=kT[:, ps_], start=True, stop=True)], reads=[kT], writes=[pkk])
                V(lambda: nc.vector.scalar_tensor_tensor(out=Ln[:], in0=pkk[:, 0:128], scalar=nbeta[:, pr, h:h + 1], in1=E[:], op0=ALU.mult, op1=ALU.mult),
                  [pkk, nbeta, E], [Ln])
                yield
                G(lambda: nc.gpsimd.tensor_tensor(out=Ln[:], in0=Ln[:], in1=strict[:], op=ALU.mult), [Ln, strict], [Ln])
                yield
                px = PS.n()
                P.pe_group([lambda: nc.tensor.transpose(out=px[:, 0:128], in_=Ln[:], identity=C["ident"][:])], reads=[Ln, C["ident"]], writes=[px])
                A(lambda: nc.scalar.copy(out=X[:], in_=px[:, 0:128]), [px], [X])
                yield
                pqk = PS.n()
                P.pe_group([lambda: nc.tensor.matmul(pqk[:, 0:128], lhsT=qT[:, ps_], rhs=kT[:, ps_], start=True, stop=True)], reads=[qT, kT], writes=[pqk])
                V(lambda: nc.vector.tensor_tensor(out=Aqk[:], in0=pqk[:, 0:128], in1=E[:], op=ALU.mult), [pqk, E], [Aqk])
                yield
                pa = PS.n()
                P.pe_group([lambda: nc.tensor.transpose(out=pa[:, 0:128], in_=Aqk[:], identity=C["ident"][:])], reads=[Aqk, C["ident"]], writes=[pa])
                A(lambda: nc.scalar.copy(out=pp["AqT"][:], in_=pa[:, 0:128]), [pa], [pp["AqT"]])
                yield
                G(lambda: nc.gpsimd.tensor_tensor(out=pp["QdT"][:], in0=qT[:, ps_], in1=egb[:, ps_], op=ALU.mult), [qT, egb], [pp["QdT"]])
                yield
                Pk, PkT = X, Ln
                Rm = tp.n()
                V(lambda: nc.vector.tensor_tensor(out=Rm[:], in0=X[:], in1=C["ident"][:], op=ALU.add), [X, C["ident"]], [Rm])
                yield
                for lev in range(5):
                    last = (lev == 4)
                    pT2 = PS.n()
                    P.pe_group([lambda: nc.tensor.matmul(pT2[:, 0:128], lhsT=Pk[:], rhs=PkT[:], start=True, stop=True)], reads=[Pk, PkT], writes=[pT2])
                    nPT = tp.n()
                    A(lambda: nc.scalar.copy(out=nPT[:], in_=pT2[:, 0:128]), [pT2], [nPT])
                    yield
                    if not last:
                        p2 = PS.n()
                        P.pe_group([lambda: nc.tensor.matmul(p2[:, 0:128], lhsT=PkT[:], rhs=Pk[:], start=True, stop=True)], reads=[Pk, PkT], writes=[p2])
                        nP = tp.n()
                        V(lambda: nc.vector.tensor_copy(out=nP[:], in_=p2[:, 0:128]), [p2], [nP])
                        yield
                    pr_ = PS.n()
                    P.pe_group([lambda: nc.tensor.matmul(pr_[:, 0:128], lhsT=nPT[:], rhs=Rm[:], start=True, stop=True)], reads=[nPT, Rm], writes=[pr_])
                    nR = tp.n()
                    V(lambda: nc.vector.tensor_tensor(out=nR[:], in0=pr_[:, 0:128], in1=Rm[:], op=ALU.add), [pr_, Rm], [nR])
                    yield
                    Rm = nR
                    PkT = nPT
                    if not last:
                        Pk = nP
                KbE = tp.n(); bV = tp.n()
                pk1 = PS.n()
                P.pe_group([lambda: nc.tensor.transpose(out=pk1[:, 0:128], in_=kT[:, ps_], identity=C["ident"][:])], reads=[kT, C["ident"]], writes=[pk1])
                V(lambda: nc.vector.tensor_scalar(out=KbE[:], in0=pk1[:, 0:128], scalar1=bE[:, pr, h:h + 1], scalar2=None, op0=ALU.mult), [pk1, bE], [KbE])
                V(lambda: nc.vector.tensor_scalar(out=pp["Kdc"][:], in0=pk1[:, 0:128], scalar1=dkc[:, pr, h:h + 1], scalar2=None, op0=ALU.mult),
                  [pk1, dkc], [pp["Kdc"]])
                yield
                pk2 = PS.n()
                P.pe_group([lambda: nc.tensor.transpose(out=pk2[:, 0:128], in_=vT[:, ps_], identity=C["ident"][:])], reads=[vT, C["ident"]], writes=[pk2])
                V(lambda: nc.vector.tensor_scalar(out=bV[:], in0=pk2[:, 0:128], scalar1=COLb[:, pr, h:h + 1], scalar2=None, op0=ALU.mult), [pk2, COLb], [bV])
                yield
                pw1 = PS.n()
                P.pe_group([lambda: nc.tensor.matmul(pw1[:, 0:128], lhsT=KbE[:], rhs=Rm[:], start=True, stop=True)], reads=[KbE, Rm], writes=[pw1])
                A(lambda: nc.scalar.copy(out=pp["WT"][:], in_=pw1[:, 0:128]), [pw1], [pp["WT"]])
                yield
                pw2 = PS.n()
                P.pe_group([lambda: nc.tensor.matmul(pw2[:, 0:128], lhsT=Rm[:], rhs=bV[:], start=True, stop=True)], reads=[Rm, bV], writes=[pw2])
                V(lambda: nc.vector.tensor_copy(out=pp["U"][:], in_=pw2[:, 0:128]), [pw2], [pp["U"]])
                yield
                Wt = tp.n()
                pw3 = PS.n()
                P.pe_group([lambda: nc.tensor.matmul(pw3[:, 0:128], lhsT=Rm[:], rhs=KbE[:], start=True, stop=True)], reads=[Rm, KbE], writes=[pw3])
                A(lambda: nc.scalar.copy(out=Wt[:], in_=pw3[:, 0:128]), [pw3], [Wt])
                yield
                for cc in range(2):
                    rr = slice(cc * 64, (cc + 1) * 64)
                    tl = pr * 128 + cc * 64 + 63
                    pm = PS.n()
                    P.pe_group([lambda: nc.tensor.matmul(pm[:, 0:128], lhsT=Wt[rr, :], rhs=pp["Kdc"][rr, :], start=True, stop=True)],
                               reads=[Wt, pp["Kdc"]], writes=[pm])
                    V(lambda: nc.vector.scalar_tensor_tensor(out=pp["AT%d" % cc][:], in0=C["ident"][:], scalar=egb[:, tl:tl + 1], in1=pm[:, 0:128],
                                                              op0=ALU.mult, op1=ALU.subtract), [C["ident"], egb, pm], [pp["AT%d" % cc]])
                    yield
                    pb = PS.n()
                    P.pe_group([lambda: nc.tensor.matmul(pb[:, 0:128], lhsT=pp["Kdc"][rr, :], rhs=pp["U"][rr, :], start=True, stop=True)],
                               reads=[pp["Kdc"], pp["U"]], writes=[pb])
                    A(lambda: nc.scalar.copy(out=pp["B%d" % cc][:], in_=pb[:, 0:128]), [pb], [pp["B%d" % cc]])
                    yield

            def recur(prs, pps):
                for pr, pp in zip(prs, pps):
                    ps_ = slice(pr * 128, (pr + 1) * 128)
                    WT, U, AqT, QdT = pp["WT"], pp["U"], pp["AqT"], pp["QdT"]
                    o_sb = osr.n(); vn = vnr.n()
                    for cc in range(2):
                        rr = slice(cc * 64, (cc + 1) * 64)
                        S = Sh[0]
                        pas = PS.n()
                        P.pe_group([lambda: nc.tensor.matmul(pas[:, 0:128], lhsT=pp["AT%d" % cc][:], rhs=S[:], start=True, stop=True)],
                                   reads=[pp["AT%d" % cc], S], writes=[pas])
                        S2 = Sr.n()
                        V(lambda: nc.vector.tensor_tensor(out=S2[:], in0=pas[:, 0:128], in1=pp["B%d" % cc][:], op=ALU.add), [pas, pp["B%d" % cc]], [S2])
                        yield
                        pws = PS.n()
                        P.pe_group([lambda: nc.tensor.matmul(pws[:, 0:128], lhsT=WT[:], rhs=S[:], start=True, stop=True)], reads=[WT, S], writes=[pws])
                        V(lambda: nc.vector.tensor_tensor(out=vn[rr, :], in0=U[rr, :], in1=pws[rr, 0:128], op=ALU.subtract), [U, pws], [vn])
                        yield
                        po = PS.n()
                        P.pe_group([lambda: nc.tensor.matmul(po[:, 0:128], lhsT=QdT[:], rhs=S[:], start=True, stop=False),
                                    lambda: nc.tensor.matmul(po[:, 0:128], lhsT=AqT[rr, :], rhs=vn[rr, :], start=False, stop=True)],
                                   reads=[QdT, S, AqT, vn], writes=[po])
                        A(lambda: nc.scalar.copy(out=o_sb[rr, :], in_=po[rr, 0:128]), [po], [o_sb])
                        yield
                        Sh[0] = S2
                    zt = ztr.n(); ss = sm1.n(); rs = sm1.n()
                    P.dma("sp", zt[:], z[ps_, O_DZ + h * DK:O_DZ + (h + 1) * DK], reads=[("z", pr)], writes=[zt])
                    rms_scale(P, nc, o_sb[:], DK, sq, ss, rs, [o_sb])
                    yield
                    V(lambda: nc.vector.scalar_tensor_tensor(out=o_sb[:], in0=o_sb[:], scalar=rs[:, 0:1], in1=nw[:], op0=ALU.mult, op1=ALU.mult),
                      [o_sb, rs, nw], [o_sb])
                    A(lambda: nc.scalar.activation(out=zt[:], in_=zt[:], func=AF.Silu), [zt], [zt])
                    yield
                    G(lambda: nc.gpsimd.tensor_tensor(out=o_sb[:], in0=o_sb[:], in1=zt[:], op=ALU.mult), [o_sb, zt], [o_sb])
                    P.dma("pool", oa[ps_, h * DK:(h + 1) * DK], o_sb[:], reads=[o_sb], writes=[("oa", pr)])
                    yield

            groups = [list(range(g0, min(g0 + GP, NPR))) for g0 in range(0, NPR, GP)]
            prev = None
            for gi, grp in enumerate(groups + [None]):
                gens = []
                if grp is not None:
                    pps = [persist[gi % 2][k] for k in range(len(grp))]
                    gens += [phaseA(pr, temps[k], pps[k]) for k, pr in enumerate(grp)]
                if prev is not None:
                    gens.append(prev)
                while gens:
                    for g_ in list(gens):
                        try:
                            next(g_)
                        except StopIteration:
                            gens.remove(g_)
                prev = recur(grp, pps) if grp is not None else None


def make_consts():
    c = {}
    c["ident"] = np.eye(128, dtype=np.float32)
    i = np.arange(128)[:, None]; j = np.arange(128)[None, :]
    same = (i // 64) == (j // 64)
    c["maskbig"] = np.where(same & (i >= j), 0.0, 1e30).astype(np.float32)
    c["strict01"] = (same & (i > j)).astype(np.float32)
    c["maskT01"] = (same & (i <= j)).astype(np.float32)
    r = np.ones((128, SEQ), np.float32); r[:, ::64] = 0.0
    c["rmask"] = r
    c["tau"] = np.tile(np.arange(1, 257, dtype=np.float32)[None, :], (128, 1))
    sel = np.zeros((8, 8, 128), np.float32)
    for h in range(8):
        sel[h, h, :] = 1.0
    c["dn_sel"] = sel
    return c


def prep_layer_params(inp, l):
    f = lambda a: np.ascontiguousarray(a, dtype=np.float32)
    p = {}
    p["dn_norm_w"] = f(inp["dn_norm_w"][l][None, :])
    dpar = np.zeros((8, 2), np.float32)
    dpar[:, 0] = inp["dn_dt_bias"][l]
    dpar[:, 1] = inp["dn_a_log"][l]
    p["dn_par"] = dpar
    p["dn_conv_w"] = f(inp["dn_conv_w"][l].reshape(3, 8, 128, 4).transpose(2, 0, 1, 3).reshape(128, 96))
    p["gla_norm_w"] = f(inp["gla_norm_w"][l][None, :])
    p["gla_w_up"] = f(inp["gla_w_up"][l])
    p["gla_b_col"] = f(inp["gla_b_up"][l].reshape(4, 128).T)
    lay = lambda a: f(a.reshape(32, 2, 64).transpose(1, 2, 0).reshape(128, 32))
    p["s5_lre"] = lay(inp["s5_lam_re"][l]); p["s5_lim"] = lay(inp["s5_lam_im"][l])
    p["s5_lstep"] = lay(np.repeat(inp["s5_log_step"][l][:, None], 64, axis=1))
    p["s5_d"] = f(inp["s5_d"][l].reshape(8, 128).T)
    Bre = np.zeros((32, 128, 128), np.float32); Bim = np.zeros_like(Bre); Cre = np.zeros_like(Bre); Cim = np.zeros_like(Bre)
    for g in range(64):
        st = g // 2; p0 = (g % 2) * 64; k0 = (g % 8) * 16
        Bre[st, k0:k0 + 16, p0:p0 + 64] = inp["s5_b_re"][l][g].T
        Bim[st, k0:k0 + 16, p0:p0 + 64] = inp["s5_b_im"][l][g].T
        Cre[st, p0:p0 + 64, k0:k0 + 16] = inp["s5_c_re"][l][g].T
        Cim[st, p0:p0 + 64, k0:k0 + 16] = inp["s5_c_im"][l][g].T
    p["s5_Bre"] = Bre; p["s5_Bim"] = Bim; p["s5_Cre"] = Cre; p["s5_Cim"] = Cim
    return p


PAR_SHAPES = {
    "ident": [128, 128], "maskbig": [128, 128], "strict01": [128, 128], "maskT01": [128, 128], "rmask": [128, SEQ], "tau": [128, 256],
    "dn_sel": [8, 8, 128], "dn_norm_w": [1, 128], "dn_par": [8, 2], "dn_conv_w": [128, 96], "gla_norm_w": [1, 256], "gla_w_up": [16, 512],
    "gla_b_col": [128, 4], "s5_lre": [128, 32], "s5_lim": [128, 32], "s5_lstep": [128, 32], "s5_d": [128, 8],
    "s5_Bre": [32, 128, 128], "s5_Bim": [32, 128, 128], "s5_Cre": [32, 128, 128], "s5_Cim": [32, 128, 128],
}
CONST_NAMES = ["ident", "maskbig", "strict01", "maskT01", "rmask", "tau", "dn_sel"]


def linear_F(P, nc, srcT, skey, c0, NTOK, K, w, wc0, N, dst, dkey, dr0):
    KT = K // 128
    with P.scope():
        actT = P.sb([128, KT, NTOK], BF16)
        for kt in range(KT):
            P.dma("pool", actT[:, kt, :], srcT[kt * 128:(kt + 1) * 128, c0:c0 + NTOK], reads=[(skey, kt)],
                  writes=[(actT, t) for t in range(NTOK // 128)])
        gemm_T(P, nc, actT, NTOK, K, w, wc0, N, dst, dkey, dr0, 0)


WNAMES = {
    "w_in": [D_MODEL, IN_WIDTH], "w_dn_out": [1024, D_MODEL], "w_gla_out": [1024, D_MODEL], "w_s5_glu": [1024, 2 * D_MODEL],
    "w_mix_out": [D_MODEL, D_MODEL], "w_xa_q": [D_MODEL, 512], "w_xa_kv": [D_MODEL, 1024], "w_xa_out": [512, D_MODEL],
    "w_ffn_up": [D_MODEL, 2 * FFN_H], "w_ffn_down": [FFN_H, D_MODEL], "ffn_conv_w": [128, 132],
    "norm_mix_pre": [1, D_MODEL], "norm_mix_post": [1, D_MODEL], "norm_xa_pre": [1, D_MODEL], "norm_xa_post": [1, D_MODEL],
    "norm_mem": [1, D_MODEL], "norm_ffn_pre": [1, D_MODEL], "norm_ffn_post": [1, D_MODEL],
}
LAYER_PAR = [n for n in PAR_SHAPES if n not in CONST_NAMES]


def build_full(depth=DEPTH):
    nc = bass.Bass("TRN2", target_bir_lowering=False)
    P = Prog(nc)
    D = D_MODEL
    dr = lambda n, s, k="ExternalInput": P.dram(n, s, F32, k)
    x_in = dr("x", [SEQ, D]); mem = dr("mem", [N_MEM, D])
    cst = {n: dr(n, PAR_SHAPES[n]) for n in CONST_NAMES}
    W = [{n: dr("%s_l%d" % (n, l), s) for n, s in WNAMES.items()} for l in range(depth)]
    LP = [{n: dr("%s_l%d" % (n, l), PAR_SHAPES[n]) for n in LAYER_PAR} for l in range(depth)]
    out = dr("out", [SEQ, D], "ExternalOutput")
    I = "Internal"
    z = dr("z", [SEQ, IN_WIDTH], I); oa = dr("oa", [SEQ, 1024], I); ob = dr("ob", [SEQ, 1024], I); ysT = dr("ysT", [1024, SEQ], I)
    ya = dr("ya", [SEQ, D], I); yb = dr("yb", [SEQ, D], I); yg = dr("yg", [SEQ, 2 * D], I); merged = dr("merged", [SEQ, D], I)
    y1 = dr("y1", [SEQ, D], I); x1 = dr("x1", [SEQ, D], I); q = dr("q", [SEQ, 512], I); kv = dr("kv", [N_MEM, 1024], I)
    o = dr("o", [SEQ, 512], I); y2 = dr("y2", [SEQ, D], I); x2h = dr("x2h", [128 + SEQ, D], I); y3 = dr("y3", [SEQ, D], I)
    xmid = dr("xmid", [SEQ, D], I)
    C = load_consts(P, cst["ident"])
    with P.scope():
        zt = P.sb([128, D])
        P.op("dve", lambda: nc.vector.memset(zt[:], 0.0), writes=[zt])
        P.dma("sp", x2h[0:128, :], zt[:], reads=[zt], writes=[("x2h", 0)])
    NT = 2048
    xcur, xkey = x_in, "x"
    for l in range(depth):
        w = W[l]; par = dict(cst); par.update(LP[l])
        xnext, nkey = (out, "out") if l == depth - 1 else (xmid, "xmid")
        for r0 in range(0, SEQ, NT):
            linear_T(P, nc, C, xcur, xkey, r0, NT, D, w["w_in"], 0, IN_WIDTH, z, "z", dr0=r0, norm_g=w["norm_mix_pre"])
        dn_stage(P, nc, C, z, oa, par)
        gla_stage(P, nc, C, z, ob, par)
        s5_stage(P, nc, C, z, ysT, par)
        linear_T(P, nc, C, mem, "mem", 0, N_MEM, D, w["w_xa_kv"], 0, 1024, kv, "kv", norm_g=w["norm_mem"])
        for r0 in range(0, SEQ, NT):
            linear_T(P, nc, C, oa, "oa", r0, NT, 1024, w["w_dn_out"], 0, D, ya, "ya", dr0=r0)
            linear_T(P, nc, C, ob, "ob", r0, NT, 1024, w["w_gla_out"], 0, D, yb, "yb", dr0=r0)
            linear_F(P, nc, ysT, "ysT", r0, NT, 1024, w["w_s5_glu"], 0, 2 * D, yg, "yg", r0)
            merge_stage(P, nc, ya, yb, yg, z, "z", O_GATES, merged, NT, r0=r0)
            linear_T(P, nc, C, merged, "merged", r0, NT, D, w["w_mix_out"], 0, D, y1, "y1", dr0=r0)
            residual_stage(P, nc, xcur, xkey, y1, "y1", w["norm_mix_post"], x1, "x1", NT, xr0=r0, yr0=r0, dr0=r0)
            linear_T(P, nc, C, x1, "x1", r0, NT, D, w["w_xa_q"], 0, 512, q, "q", dr0=r0, norm_g=w["norm_xa_pre"])
            attn_stage(P, nc, C, q, kv, o, NT, r0=r0)
            linear_T(P, nc, C, o, "o", r0, NT, 512, w["w_xa_out"], 0, D, y2, "y2", dr0=r0)
            residual_stage(P, nc, x1, "x1", y2, "y2", w["norm_xa_post"], x2h, "x2h", NT, xr0=r0, yr0=r0, dr0=128 + r0)
        with P.scope():
            ffn_stage(P, nc, C, x2h, w["w_ffn_up"], w["w_ffn_down"], w["ffn_conv_w"], w["norm_ffn_pre"], y3)
        for r0 in range(0, SEQ, NT):
            residual_stage(P, nc, x2h, "x2h", y3, "y3", w["norm_ffn_post"], xnext, nkey, NT, xr0=128 + r0, yr0=r0, dr0=r0)
        xcur, xkey = xnext, nkey
    P.finish([("out", t) for t in range(SEQ // 128)])
    P.close()
    return nc, P


def host_inputs(inp, b, depth=DEPTH):
    f = lambda a: np.ascontiguousarray(a, dtype=np.float32)
    m = dict(make_consts())
    m["x"] = f(inp["x"][b]); m["mem"] = f(inp["mem"][b])
    for l in range(depth):
        for n in WNAMES:
            if n == "ffn_conv_w":
                a = inp[n][l].reshape(44, 128, 3).transpose(1, 0, 2).reshape(128, 132)
            elif n.startswith("norm_"):
                a = inp[n][l][None, :]
            else:
                a = inp[n][l]
            m["%s_l%d" % (n, l)] = f(a)
        for n, a in prep_layer_params(inp, l).items():
            m["%s_l%d" % (n, l)] = a
    return m


_FULL = {}


def kernel(**inputs):
    inp = {k: np.asarray(v) for k, v in inputs.items()}
    if "nc" not in _FULL:
        _FULL["nc"] = build_full()[0]
    nc = _FULL["nc"]
    in_maps = [host_inputs(inp, b) for b in range(BATCH)]
    res = run_bass_kernel_spmd(nc, in_maps, core_ids=list(range(BATCH)))
    out = np.stack([res.results[b]["out"] for b in range(BATCH)], axis=0)
    return out.astype(np.float32)
```
